# Optimizing a Trainium2 kernel written in Bass

```python
import math
import jax
import jax.numpy as jnp
from jax import lax
import numpy as np

D_MODEL = 1024
BATCH = 4
SEQ = 4096
DEPTH = 1

GRID_W = 64
CTX_LEN = 256
N_DIR = 2
DN_HEADS = 4
DN_HEAD_DIM = 128
DN_WIDTH = DN_HEADS * DN_HEAD_DIM
DN_CHUNK = 64
SHORT_CONV = 3
S5_WIDTH = 512
S5_GROUP = 16
S5_GROUPS = S5_WIDTH // S5_GROUP
S5_STATE = 64
D_FF = 2816
FFN_CONV = 3
N_BRANCH = 2
N_MOD = 6
RMS_EPS = 1e-6
L2_EPS = 1e-6
IN_SIZES = (DN_WIDTH, DN_WIDTH, DN_WIDTH, DN_WIDTH, N_DIR * DN_HEADS, N_DIR * DN_HEADS, S5_WIDTH, D_MODEL, D_MODEL)
IN_COLS = 4 * DN_WIDTH + 2 * N_DIR * DN_HEADS + S5_WIDTH + N_BRANCH * D_MODEL

kernel_name = "hybrid_gdn_s5_convffn_block"


def rmsnorm(x, w):
    xf = x.astype(jnp.float32)
    y = xf * lax.rsqrt(jnp.mean(xf * xf, axis=-1, keepdims=True) + RMS_EPS)
    return (y * w.astype(jnp.float32)).astype(x.dtype)


def l2norm(x):
    xf = x.astype(jnp.float32)
    return xf * lax.rsqrt(jnp.sum(xf * xf, axis=-1, keepdims=True) + L2_EPS)


def modulate(h, shift, scale):
    return h * (1 + scale) + shift


def flip_if(a, rev, axis):
    return jnp.flip(a, axis=axis) if rev else a


def dw_conv1d(x, w):
    k = w.shape[0]
    pad = k // 2
    t = x.shape[1]
    xp = jnp.pad(x, ((0, 0), (pad, pad), (0, 0)))
    out = xp[:, 0:t] * w[0]
    for j in range(1, k):
        out = out + xp[:, j:j + t] * w[j]
    return out


def dw_conv2d(x, w, rows):
    b, t, ch = x.shape
    img = x.reshape(b, rows, t // rows, ch)
    y = lax.conv_general_dilated(img, w[:, :, None, :].astype(x.dtype), (1, 1), 'SAME',
                                 dimension_numbers=('NHWC', 'HWIO', 'NHWC'), feature_group_count=ch)
    return y.reshape(b, t, ch)


def gated_delta_chunked(q, k, v, g, beta, s0):
    bsz, h, t, _ = q.shape
    dv = v.shape[-1]
    n = t // DN_CHUNK

    def chunks(a):
        return jnp.moveaxis(a.reshape(bsz, h, n, DN_CHUNK, *a.shape[3:]), 2, 0)

    q, k, v, beta = chunks(q), chunks(k), chunks(v), chunks(beta)
    g = jnp.cumsum(chunks(g), axis=-1)
    kb = k * beta[..., None]
    vb = v * beta[..., None]
    idx = jnp.arange(DN_CHUNK)
    incl = idx[:, None] >= idx[None, :]
    strict = idx[:, None] > idx[None, :]
    diff = g[..., :, None] - g[..., None, :]
    decay = jnp.where(incl, jnp.exp(jnp.where(incl, diff, 0.0)), 0.0)
    lmat = jnp.where(strict, jnp.einsum('nbhcd,nbhsd->nbhcs', kb, k) * decay, 0.0)
    rhs = jnp.concatenate([vb, kb * jnp.exp(g)[..., None]], axis=-1)
    sol = lax.linalg.triangular_solve(lmat, rhs, left_side=True, lower=True, unit_diagonal=True)
    u, w = sol[..., :dv], sol[..., dv:]
    a_intra = jnp.einsum('nbhcd,nbhsd->nbhcs', q, k) * decay

    def step(state, xs):
        q_i, k_i, u_i, w_i, g_i, a_i = xs
        v_new = u_i - jnp.einsum('bhck,bhkv->bhcv', w_i, state)
        o = (jnp.einsum('bhck,bhkv->bhcv', q_i * jnp.exp(g_i)[..., None], state)
             + jnp.einsum('bhcs,bhsv->bhcv', a_i, v_new))
        g_last = g_i[..., -1:]
        state = (state * jnp.exp(g_last)[..., None]
                 + jnp.einsum('bhck,bhcv->bhkv', k_i * jnp.exp(g_last - g_i)[..., None], v_new))
        return state, o

    s_fin, o = lax.scan(step, s0, (q, k, u, w, g, a_intra))
    o = jnp.moveaxis(o, 0, 2).reshape(bsz, h, t, dv)
    return s_fin, o


def delta_prep(q, k, v, beta_logit, alpha, conv_w, a_log, dt_bias):
    b, t, _ = q.shape
    qkv = jax.nn.silu(dw_conv1d(jnp.concatenate([q, k, v], axis=-1), conv_w))

    def heads(a):
        return a.reshape(b, t, DN_HEADS, DN_HEAD_DIM).transpose(0, 2, 1, 3)

    qh, kh, vh = [heads(a) for a in jnp.split(qkv, 3, axis=-1)]
    qh = l2norm(qh) * (DN_HEAD_DIM ** -0.5)
    kh = l2norm(kh)
    vh = vh.astype(jnp.float32)

    def per_dir(a):
        return a.astype(jnp.float32).reshape(b, t, N_DIR, DN_HEADS).transpose(0, 2, 3, 1)

    beta = jax.nn.sigmoid(per_dir(beta_logit))
    g = (-jnp.exp(a_log.astype(jnp.float32))[None, :, :, None]
         * jax.nn.softplus(per_dir(alpha) + dt_bias.astype(jnp.float32)[None, :, :, None]))
    return qh, kh, vh, g, beta


def delta_branch(p_ctx, p_lat, conv_w, a_log, dt_bias, norm_w, w_out, need_ctx):
    qc, kc, vc, zc, bc, ac = p_ctx
    ql, kl, vl, zl, bl, al = p_lat
    ctx_in = delta_prep(qc, kc, vc, bc, ac, conv_w, a_log, dt_bias)
    lat_in = delta_prep(ql, kl, vl, bl, al, conv_w, a_log, dt_bias)
    b = ql.shape[0]
    s0 = jnp.zeros((b, DN_HEADS, DN_HEAD_DIM, DN_HEAD_DIM), jnp.float32)
    o_ctx = jnp.zeros_like(ctx_in[2])
    o_lat = jnp.zeros_like(lat_in[2])
    for r in range(N_DIR):
        rev = r == 1
        qh, kh, vh, g, beta = ctx_in
        s_ctx, o = gated_delta_chunked(flip_if(qh, rev, 2), flip_if(kh, rev, 2), flip_if(vh, rev, 2),
                                       flip_if(g[:, r], rev, 2), flip_if(beta[:, r], rev, 2), s0)
        if need_ctx:
            o_ctx = o_ctx + flip_if(o, rev, 2)
        qh, kh, vh, g, beta = lat_in
        _, o = gated_delta_chunked(flip_if(qh, rev, 2), flip_if(kh, rev, 2), flip_if(vh, rev, 2),
                                   flip_if(g[:, r], rev, 2), flip_if(beta[:, r], rev, 2), s_ctx)
        o_lat = o_lat + flip_if(o, rev, 2)

    def post(o, z):
        t = o.shape[2]
        on = rmsnorm(o.transpose(0, 2, 1, 3), norm_w)
        zf = z.astype(jnp.float32).reshape(b, t, DN_HEADS, DN_HEAD_DIM)
        y = (on * jax.nn.silu(zf)).reshape(b, t, DN_WIDTH).astype(z.dtype)
        return y @ w_out

    out_lat = post(o_lat, zl)
    out_ctx = post(o_ctx, zc) if need_ctx else None
    return out_ctx, out_lat


def s5_discretize(a_re, a_im, log_step, b_re, b_im):
    lam = lax.complex(a_re.astype(jnp.float32), a_im.astype(jnp.float32))
    step = jnp.exp(log_step.astype(jnp.float32))[:, None]
    lam_bar = jnp.exp(lam * step)
    bmat = lax.complex(b_re.astype(jnp.float32), b_im.astype(jnp.float32))
    b_bar = ((lam_bar - 1.0) / lam)[..., None] * bmat
    return lam_bar, b_bar


def s5_combine(left, right):
    a_l, b_l = left
    a_r, b_r = right
    return a_r * a_l, a_r * b_l + b_r


def s5_scan(u, lam_bar, b_bar, x0):
    bu = lax.complex(jnp.einsum('btgp,gnp->btgn', u, jnp.real(b_bar)),
                     jnp.einsum('btgp,gnp->btgn', u, jnp.imag(b_bar)))
    bu = bu.at[:, 0].add(lam_bar * x0)
    lam_el = jnp.broadcast_to(lam_bar, bu.shape)
    _, xs = lax.associative_scan(s5_combine, (lam_el, bu), axis=1)
    return xs


def s5_readout(c_re, c_im, xs):
    return (jnp.einsum('gpn,btgn->btgp', c_re, jnp.real(xs))
            - jnp.einsum('gpn,btgn->btgp', c_im, jnp.imag(xs)))


def s5_branch(u_ctx, u_lat, a_re, a_im, log_step, b_re, b_im, c_re, c_im, d_skip,
              w_glu, b_glu, w_out, need_ctx):
    def groups(u):
        return u.astype(jnp.float32).reshape(u.shape[0], u.shape[1], S5_GROUPS, S5_GROUP)

    gc, gl = groups(u_ctx), groups(u_lat)
    dsk = d_skip.astype(jnp.float32).reshape(S5_GROUPS, S5_GROUP)
    y_ctx = dsk * gc
    y_lat = dsk * gl
    x0 = jnp.zeros((gl.shape[0], S5_GROUPS, S5_STATE), jnp.complex64)
    for r in range(N_DIR):
        rev = r == 1
        lam_bar, b_bar = s5_discretize(a_re[r], a_im[r], log_step[r], b_re[r], b_im[r])
        cr = c_re[r].astype(jnp.float32)
        ci = c_im[r].astype(jnp.float32)
        xs_ctx = s5_scan(flip_if(gc, rev, 1), lam_bar, b_bar, x0)
        xs_lat = s5_scan(flip_if(gl, rev, 1), lam_bar, b_bar, xs_ctx[:, -1])
        y_lat = y_lat + flip_if(s5_readout(cr, ci, xs_lat), rev, 1)
        if need_ctx:
            y_ctx = y_ctx + flip_if(s5_readout(cr, ci, xs_ctx), rev, 1)

    def glu_out(y, dtype):
        y = jax.nn.gelu(y.reshape(y.shape[0], y.shape[1], S5_WIDTH)).astype(dtype)
        z = y @ w_glu + b_glu
        y = z[..., :S5_WIDTH] * jax.nn.sigmoid(z[..., S5_WIDTH:])
        return y @ w_out

    out_lat = glu_out(y_lat, u_lat.dtype)
    out_ctx = glu_out(y_ctx, u_ctx.dtype) if need_ctx else None
    return out_ctx, out_lat


def conv_ffn(h, w_up, conv_w, w_down, rows):
    u = dw_conv2d(h @ w_up, conv_w, rows)
    gate, val = jnp.split(u, 2, axis=-1)
    return (jax.nn.silu(gate) * val) @ w_down


def setup_inputs(seed: int = 0) -> dict:
    key = jax.random.key(seed)
    ks = iter(list(jax.random.split(key, 48)))
    f32 = jnp.float32
    L = DEPTH

    def nrm(shape, scale):
        return jax.random.normal(next(ks), shape, f32) * scale

    def gain(shape):
        return 1.0 + nrm(shape, 0.02)

    x = nrm((BATCH, SEQ, D_MODEL), 1.0)
    c = nrm((BATCH, D_MODEL), 1.0)
    ctx = nrm((BATCH, CTX_LEN, D_MODEL), 1.0)
    c_ctx = nrm((D_MODEL,), 1.0)
    w_ada = nrm((L, D_MODEL, N_MOD * D_MODEL), 0.5 * D_MODEL ** -0.5)
    b_ada = nrm((L, N_MOD * D_MODEL), 0.02)
    norm1_w = gain((L, D_MODEL))
    w_in = nrm((L, D_MODEL, IN_COLS), D_MODEL ** -0.5)
    dn_conv_w = nrm((L, SHORT_CONV, 3 * DN_WIDTH), SHORT_CONV ** -0.5)
    dn_a_log = jnp.log(jax.random.uniform(next(ks), (L, N_DIR, DN_HEADS), f32, 1.0, 16.0))
    dt = jnp.exp(jax.random.uniform(next(ks), (L, N_DIR, DN_HEADS), f32, math.log(1e-3), math.log(1e-1)))
    dn_dt_bias = dt + jnp.log(-jnp.expm1(-dt))
    dn_norm_w = gain((L, DN_HEAD_DIM))
    w_a_out = nrm((L, DN_WIDTH, D_MODEL), DN_WIDTH ** -0.5)
    s5_a_re = -0.5 + nrm((L, N_DIR, S5_GROUPS, S5_STATE), 0.01)
    s5_a_im = jnp.pi * jnp.arange(S5_STATE, dtype=f32) + nrm((L, N_DIR, S5_GROUPS, S5_STATE), 0.01)
    s5_log_step = jax.random.uniform(next(ks), (L, N_DIR, S5_GROUPS), f32, math.log(1e-3), math.log(1e-1))
    s5_b_re = nrm((L, N_DIR, S5_GROUPS, S5_STATE, S5_GROUP), (2 * S5_GROUP) ** -0.5)
    s5_b_im = nrm((L, N_DIR, S5_GROUPS, S5_STATE, S5_GROUP), (2 * S5_GROUP) ** -0.5)
    s5_c_re = nrm((L, N_DIR, S5_GROUPS, S5_GROUP, S5_STATE), S5_STATE ** -0.5)
    s5_c_im = nrm((L, N_DIR, S5_GROUPS, S5_GROUP, S5_STATE), S5_STATE ** -0.5)
    s5_d = nrm((L, S5_WIDTH), 0.5)
    w_glu = nrm((L, S5_WIDTH, 2 * S5_WIDTH), S5_WIDTH ** -0.5)
    b_glu = nrm((L, 2 * S5_WIDTH), 0.02)
    w_b_out = nrm((L, S5_WIDTH, D_MODEL), S5_WIDTH ** -0.5)
    w_o = nrm((L, D_MODEL, D_MODEL), D_MODEL ** -0.5)
    norm2_w = gain((L, D_MODEL))
    w_up = nrm((L, D_MODEL, 2 * D_FF), D_MODEL ** -0.5)
    ffn_conv_w = nrm((L, FFN_CONV, FFN_CONV, 2 * D_FF), 1.0 / FFN_CONV)
    w_down = nrm((L, D_FF, D_MODEL), D_FF ** -0.5)
    norm_f_w = gain((D_MODEL,))
    return {"x": x, "c": c, "ctx": ctx, "c_ctx": c_ctx, "w_ada": w_ada, "b_ada": b_ada,
            "norm1_w": norm1_w, "w_in": w_in, "dn_conv_w": dn_conv_w, "dn_a_log": dn_a_log,
            "dn_dt_bias": dn_dt_bias, "dn_norm_w": dn_norm_w, "w_a_out": w_a_out,
            "s5_a_re": s5_a_re, "s5_a_im": s5_a_im, "s5_log_step": s5_log_step,
            "s5_b_re": s5_b_re, "s5_b_im": s5_b_im, "s5_c_re": s5_c_re, "s5_c_im": s5_c_im,
            "s5_d": s5_d, "w_glu": w_glu, "b_glu": b_glu, "w_b_out": w_b_out, "w_o": w_o,
            "norm2_w": norm2_w, "w_up": w_up, "ffn_conv_w": ffn_conv_w, "w_down": w_down,
            "norm_f_w": norm_f_w}


def reference(x, c, ctx, c_ctx, w_ada, b_ada, norm1_w, w_in, dn_conv_w, dn_a_log, dn_dt_bias,
              dn_norm_w, w_a_out, s5_a_re, s5_a_im, s5_log_step, s5_b_re, s5_b_im, s5_c_re,
              s5_c_im, s5_d, w_glu, b_glu, w_b_out, w_o, norm2_w, w_up, ffn_conv_w, w_down,
              norm_f_w):
    rows = x.shape[1] // GRID_W
    split_idx = np.cumsum(IN_SIZES)[:-1].tolist()
    xl, xc = x, ctx
    sc = jax.nn.silu(c)
    scc = jax.nn.silu(c_ctx)
    for l in range(DEPTH):
        need_ctx = l < DEPTH - 1
        mod_l = jnp.split((sc @ w_ada[l] + b_ada[l])[:, None, :], N_MOD, axis=-1)
        mod_c = jnp.split(scc @ w_ada[l] + b_ada[l], N_MOD, axis=-1)

        hl = modulate(rmsnorm(xl, norm1_w[l]), mod_l[0], mod_l[1])
        hc = modulate(rmsnorm(xc, norm1_w[l]), mod_c[0], mod_c[1])
        pl = jnp.split(hl @ w_in[l], split_idx, axis=-1)
        pc = jnp.split(hc @ w_in[l], split_idx, axis=-1)
        ya_c, ya_l = delta_branch(pc[:6], pl[:6], dn_conv_w[l], dn_a_log[l], dn_dt_bias[l],
                                  dn_norm_w[l], w_a_out[l], need_ctx)
        yb_c, yb_l = s5_branch(pc[6], pl[6], s5_a_re[l], s5_a_im[l], s5_log_step[l], s5_b_re[l],
                               s5_b_im[l], s5_c_re[l], s5_c_im[l], s5_d[l], w_glu[l], b_glu[l],
                               w_b_out[l], need_ctx)
        mix_l = (jax.nn.sigmoid(pl[7]) * ya_l + jax.nn.sigmoid(pl[8]) * yb_l) @ w_o[l]
        xl = xl + mod_l[2] * mix_l
        if need_ctx:
            mix_c = (jax.nn.sigmoid(pc[7]) * ya_c + jax.nn.sigmoid(pc[8]) * yb_c) @ w_o[l]
            xc = xc + mod_c[2] * mix_c

        hl = modulate(rmsnorm(xl, norm2_w[l]), mod_l[3], mod_l[4])
        xl = xl + mod_l[5] * conv_ffn(hl, w_up[l], ffn_conv_w[l], w_down[l], rows)
        if need_ctx:
            hc = modulate(rmsnorm(xc, norm2_w[l]), mod_c[3], mod_c[4])
            xc = xc + mod_c[5] * conv_ffn(hc, w_up[l], ffn_conv_w[l], w_down[l], 1)
    return rmsnorm(xl, norm_f_w)
```

```python
import numpy as np
from contextlib import ExitStack
import concourse.bass as bass
import concourse.mybir as mybir
from concourse.bass_utils import run_bass_kernel_spmd

F32 = mybir.dt.float32
BF16 = mybir.dt.bfloat16
AF = mybir.ActivationFunctionType
ALU = mybir.AluOpType

D = 1024
KC = 8
T = 4096
CTX = 256
OWN = 2176
OUTN = 2048
TT = T + CTX
NTL = 32
NT = 34
OWNT = 17
INC = 4624
DFF = 2816
BIG = 30000.0


class Sched:
    def __init__(self, nc, es, same_engine_sync=True, n_dma_sems=32):
        self.nc = nc
        self.eng = {"pe": nc.tensor, "act": nc.scalar, "dve": nc.vector, "pool": nc.gpsimd, "sp": nc.sync}
        self.sem = {k: es.enter_context(nc.semaphore("sem_" + k)) for k in self.eng}
        self.cnt = {k: 0 for k in self.eng}
        self.seen = {k: {} for k in self.eng}
        self.dma_sems = [es.enter_context(nc.semaphore("dsem%d" % i)) for i in range(n_dma_sems)]
        self.dma_cnt = [0] * n_dma_sems
        self.dma_rr = 0
        self.W = {}
        self.R = {}
        self.same = same_engine_sync
        self.ninst = 0

    def _wait(self, e, tok):
        sem, val, owner = tok
        if owner == e and (not self.same or e == "pe"):
            return
        sid = id(sem)
        if self.seen[e].get(sid, 0) >= val:
            return
        self.eng[e].wait_ge(sem, val)
        self.seen[e][sid] = val

    def _deps(self, e, reads, writes):
        toks = []
        for k in reads:
            toks += list(self.W.get(k, {}).values())
        for k in writes:
            toks += [t for t in self.W.get(k, {}).values() if t[2] != e or k in reads]
            toks += [t for t in self.R.get(k, {}).values() if t[2] != e]
        for t in toks:
            self._wait(e, t)

    def _record(self, tok, reads, writes):
        sid = id(tok[0])
        for k in reads:
            self.R.setdefault(k, {})[sid] = tok
        for k in writes:
            self.W.setdefault(k, {})[sid] = tok
            self.R[k] = {}

    def op(self, e, fn, reads=(), writes=()):
        self._deps(e, reads, writes)
        inst = fn()
        self.cnt[e] += 1
        inst.then_inc(self.sem[e], 1)
        tok = (self.sem[e], self.cnt[e], e)
        self._record(tok, reads, writes)
        self.ninst += 1
        return tok

    def dma(self, q, out, in_, reads=(), writes=(), **kw):
        self._deps(q, reads, writes)
        i = self.dma_rr
        self.dma_rr = (self.dma_rr + 1) % len(self.dma_sems)
        sem = self.dma_sems[i]
        if self.dma_cnt[i] > 0:
            self._wait(q, (sem, self.dma_cnt[i], None))
        self.dma_cnt[i] += 16
        self.eng[q].dma_start(out=out, in_=in_, **kw).then_inc(sem, 16)
        tok = (sem, self.dma_cnt[i], None)
        self._record(tok, reads, writes)
        self.ninst += 1
        return tok

    def barrier(self):
        for e in self.eng:
            for o in self.eng:
                if o != e and self.cnt[o] > 0:
                    self._wait(e, (self.sem[o], self.cnt[o], o))
            for i, s in enumerate(self.dma_sems):
                if self.dma_cnt[i] > 0:
                    self._wait(e, (s, self.dma_cnt[i], None))

    def finish(self, keys):
        for k in keys:
            for t in self.W.get(k, {}).values():
                self._wait("sp", t)


class K:
    def __init__(self, nc, S):
        self.nc = nc
        self.S = S
        self.rr = 0

    def mm(self, out, lhsT, rhs, start, stop, r, w):
        nc = self.nc
        return self.S.op("pe", lambda: nc.tensor.matmul(out, lhsT=lhsT, rhs=rhs, start=start, stop=stop), reads=r, writes=w)

    def tr(self, out, in_, ident, r, w):
        nc = self.nc
        return self.S.op("pe", lambda: nc.tensor.transpose(out, in_, ident), reads=r, writes=w)

    def act(self, out, in_, func, r, w, scale=None, bias=None):
        nc = self.nc
        kw = {}
        if scale is not None:
            kw["scale"] = scale
        if bias is not None:
            kw["bias"] = bias
        return self.S.op("act", lambda: nc.scalar.activation(out=out, in_=in_, func=func, **kw), reads=r, writes=w)

    def stt(self, out, in0, scalar, in1, op0, op1, r, w):
        nc = self.nc
        return self.S.op("dve", lambda: nc.vector.scalar_tensor_tensor(out=out, in0=in0, scalar=scalar, in1=in1, op0=op0, op1=op1), reads=r, writes=w)

    def tt(self, e, out, in0, in1, op, r, w):
        eng = self.S.eng[e]
        return self.S.op(e, lambda: eng.tensor_tensor(out=out, in0=in0, in1=in1, op=op), reads=r, writes=w)

    def ts(self, e, out, in0, s1, op0, r, w, s2=None, op1=None):
        eng = self.S.eng[e]
        if op1 is None:
            return self.S.op(e, lambda: eng.tensor_scalar(out=out, in0=in0, scalar1=s1, scalar2=None, op0=op0), reads=r, writes=w)
        return self.S.op(e, lambda: eng.tensor_scalar(out=out, in0=in0, scalar1=s1, scalar2=s2, op0=op0, op1=op1), reads=r, writes=w)

    def cp(self, e, out, in_, r, w):
        if e == "act":
            return self.act(out, in_, AF.Copy, r, w)
        eng = self.S.eng[e]
        return self.S.op(e, lambda: eng.tensor_copy(out=out, in_=in_), reads=r, writes=w)

    def memset(self, e, ap, val, w):
        eng = self.S.eng[e]
        return self.S.op(e, lambda: eng.memset(ap, val), writes=w)

    def recip(self, out, in_, r, w):
        nc = self.nc
        return self.S.op("dve", lambda: nc.vector.reciprocal(out=out, in_=in_), reads=r, writes=w)

    def dma(self, q, out, in_, r, w, **kw):
        return self.S.dma(q, out, in_, reads=r, writes=w, **kw)

    def load_cast(self, dst, src, ncols, r, w):
        c = 0
        while c < ncols:
            n = min(1024, ncols - c)
            self.dma("pool", dst[..., c:c + n], src[..., c:c + n], r, w)
            c += n


def build_program(upto=99, dbg=False, dn_limit=None, skip2=False, skip3=False, stop_after=None):
    nc = bass.Bass("TRN2", target_bir_lowering=False)
    skind = "ExternalOutput" if dbg else "Internal"

    def din(name, shape, dt=F32):
        return nc.dram_tensor(name, list(shape), dt, kind="ExternalInput").ap()

    def dscr(name, shape, dt=F32):
        return nc.dram_tensor(name, list(shape), dt, kind=skind).ap()

    xT = din("xT", [D, T])
    ctxT = din("ctxT", [D, CTX])
    cvec = din("cvec", [128, KC, 2])
    w_ada = din("w_ada", [D, 6 * D])
    b_ada = din("b_ada", [128, 48])
    nw = din("nw", [128, 3, KC])
    w_in = din("w_in", [D, INC])
    convw = din("convw", [128, 12, 3])
    gpar = din("gpar", [128, 2, NT * 8])
    dnw = din("dnw", [128, 128])
    w_a_out = din("w_a_out", [512, D])
    s5p = din("s5p", [128, 3, 2 * 16 * 32])
    s5b = din("s5b", [128, 2, 2 * 16 * 32])
    s5c = din("s5c", [128, 2 * 2 * 16 * 128])
    s5d = din("s5d", [128, 4])
    w_glu = din("w_glu", [512, D])
    b_glu = din("b_glu", [128, 8])
    w_b_out = din("w_b_out", [512, D])
    w_o = din("w_o", [D, D])
    w_up = din("w_up", [D, 2 * DFF])
    fcw = din("fcw", [128, 44, 9])
    w_down = din("w_down", [DFF, D])
    consts = din("consts", [128, 11, 128])
    yT = nc.dram_tensor("yT", [D, OUTN], F32, kind="ExternalOutput").ap()

    qkvT_s = dscr("qkvT_s", [1536, TT])
    z_s = dscr("z_s", [OWN, 512], BF16)
    ba_s = dscr("ba_s", [TT, 16])
    uT_s = dscr("uT_s", [512, TT], BF16)
    gT_s = dscr("gT_s", [2048, OWN], BF16)
    o_s = dscr("o_s", [2, OWN, 512])
    xl1_s = dscr("xl1_s", [D, OWN])
    ygT_s = dscr("ygT_s", [512, OWN], BF16)
    act_s = dscr("act_s", [DFF, OUTN], BF16)
    y_dbg = dscr("y_dbg", [512, OWN]) if dbg else None
    dbg_s = dscr("dbg_s", [128, 4096]) if dbg else None

    with ExitStack() as es:
        S = Sched(nc, es)
        k = K(nc, S)

        def sb(name, shape, dt=F32, stack=es):
            return stack.enter_context(nc.sbuf_tensor(name, list(shape), dt))

        psall = es.enter_context(nc.psum_tensor("psall", [128, 8, 512], F32))
        ps = [psall[:, i, :] for i in range(8)]
        pk = ["ps%d" % i for i in range(8)]

        def dbgdump(name, ap, keys, shape, dt=F32):
            if not dbg:
                return
            t_ = nc.dram_tensor("dd_" + name, list(shape), dt, kind="ExternalOutput").ap()
            k.dma("sp", t_, ap, keys, ["dd_" + name])
            dumpkeys.append("dd_" + name)
        dumpkeys = []

        cst = sb("cst", [128, 11, 128])
        k.dma("sp", cst[:], consts[:, :, :], [], ["cst"])
        ident_f = cst[:, 0, :]
        ones_f = cst[:, 7, :]
        cstb = sb("cstb", [128, 2, 128], BF16)
        k.cp("dve", cstb[:, 0, :], cst[:, 0, :], ["cst"], ["cstb"])
        k.cp("dve", cstb[:, 1, :], cst[:, 7, :], ["cst"], ["cstb"])
        ident_b = cstb[:, 0, :]
        ones_b = cstb[:, 1, :]
        mods = sb("mods", [128, 6, KC, 2])
        nwt = sb("nwt", [128, 3, KC])
        k.dma("sp", nwt[:], nw[:, :, :], [], ["nwt"])
        A1 = sb("A1", [128, KC, 2])
        A2 = sb("A2", [128, KC])

        p01 = es.enter_context(ExitStack())
        win = sb("win", [128, KC, INC], BF16, stack=p01)
        for kc in range(KC):
            k.load_cast(win[:, kc, :], w_in[kc * 128:(kc + 1) * 128, :], INC, [], ["win"])
        with ExitStack() as p0:
            cv = sb("cv", [128, KC, 2], stack=p0)
            scv = sb("scv", [128, KC, 2], stack=p0)
            bad = sb("bad", [128, 48], stack=p0)
            k.dma("sp", cv[:], cvec[:, :, :], [], ["cv"])
            k.dma("sp", bad[:], b_ada[:, :], [], ["bad"])
            k.act(scv[:], cv[:], AF.Silu, ["cv"], ["scv"])
            wad = [sb("wad%d" % i, [128, KC, D], stack=p0) for i in range(2)]
            for j in range(6):
                wb = wad[j % 2]
                wk = "wad%d" % (j % 2)
                for kc in range(KC):
                    k.dma("sp", wb[:, kc, :], w_ada[kc * 128:(kc + 1) * 128, j * D:(j + 1) * D], [], [wk])
                for oc in range(KC):
                    for kc in range(KC):
                        k.mm(ps[0][:, (j * 8 + oc) * 2:(j * 8 + oc) * 2 + 2], wb[:, kc, oc * 128:(oc + 1) * 128], scv[:, kc, :],
                             kc == 0, kc == KC - 1, [wk, "scv"], ["ps0"])
                for w_ in range(2):
                    k.tt("dve", mods[:, j, :, w_], ps[0][:, j * 16 + w_:j * 16 + 16:2], bad[:, j * 8:(j + 1) * 8], ALU.add,
                         ["ps0", "bad"], ["mods"])
            for w_ in range(2):
                k.stt(A1[:, :, w_], mods[:, 1, :, w_], 1.0, nwt[:, 0, :], ALU.add, ALU.mult, ["mods", "nwt"], ["A1"])
            k.stt(A2[:], mods[:, 4, :, 0], 1.0, nwt[:, 1, :], ALU.add, ALU.mult, ["mods", "nwt"], ["A2"])
        S.barrier()
        if dbg:
            k.dma("sp", dbg_s[:, 0:96], mods[:].rearrange("p a b c -> p (a b c)"), ["mods"], ["dbg_s"])

        if upto >= 1:
            with ExitStack() as p1:
                xb = [sb("xb%d" % i, [128, KC, 512], stack=p1) for i in range(2)]
                sqb = sb("sqb", [128, KC, 512], BF16, stack=p1)
                hb = [sb("hb%d" % i, [128, KC, 512], BF16, stack=p1) for i in range(2)]
                tmp = [sb("tmp%d" % i, [128, 512], stack=p1) for i in range(2)]
                rstd = [sb("rstd%d" % i, [128, 512], stack=p1) for i in range(2)]
                stF = [sb("stF%d" % i, [128, 4, 512], stack=p1) for i in range(2)]
                stB = [sb("stB%d" % i, [128, 4, 512], BF16, stack=p1) for i in range(2)]
                stZ = [sb("stZ%d" % i, [128, 512], BF16, stack=p1) for i in range(2)]
                stA = [sb("stA%d" % i, [128, 16], stack=p1) for i in range(2)]
                blocks = [(0, 512, 0), (512, 512, 0), (1024, 512, 0), (1536, 512, 0), (2048, 128, 0),
                          (2176, 512, 1), (2688, 512, 1), (3200, 512, 1), (3712, 384, 1), (4096, 256, 2)]
                cnt = {"F": 0, "B": 0, "Z": 0, "A": 0, "ps": 0}

                def load_x(bi):
                    t0, N, kind = blocks[bi]
                    xk = "xb%d" % (bi % 2)
                    src = ctxT if kind == 2 else xT
                    c0 = t0 - T if kind == 2 else t0
                    k.dma("sp", xb[bi % 2][:, :, 0:N], src.rearrange("(kc p) t -> p kc t", p=128)[:, :, c0:c0 + N], [], [xk])

                def norm_block(bi):
                    t0, N, kind = blocks[bi]
                    X = xb[bi % 2]
                    xk = "xb%d" % (bi % 2)
                    H = hb[bi % 2]
                    hk = "hb%d" % (bi % 2)
                    wsel = 1 if kind == 2 else 0
                    k.act(sqb[:, :, 0:N], X[:, :, 0:N], AF.Square, [xk], ["sqb"])
                    for kc in range(KC):
                        k.mm(ps[7][:, 0:N], ones_b, sqb[:, kc, 0:N], kc == 0, kc == KC - 1, ["cstb", "sqb"], ["ps7"])
                    rs = rstd[bi % 2]
                    rk = "rstd%d" % (bi % 2)
                    k.act(rs[:, 0:N], ps[7][:, 0:N], AF.Sqrt, ["ps7"], [rk], scale=1.0 / D, bias=1e-6)
                    k.recip(rs[:, 0:N], rs[:, 0:N], [rk], [rk])
                    for kc in range(KC):
                        tb = tmp[kc % 2]
                        tk = "tmp%d" % (kc % 2)
                        k.stt(tb[:, 0:N], X[:, kc, 0:N], A1[:, kc, wsel:wsel + 1], rs[:, 0:N], ALU.mult, ALU.mult, [xk, "A1", rk], [tk])
                        k.act(H[:, kc, 0:N], tb[:, 0:N], AF.Identity, [tk, "mods"], [hk], bias=mods[:, 0, kc, wsel:wsel + 1])

                load_x(0)
                load_x(1)
                norm_block(0)
                for bi, (t0, N, kind) in enumerate(blocks):
                    H = hb[bi % 2]
                    hk = "hb%d" % (bi % 2)
                    if bi + 1 < len(blocks):
                        norm_block(bi + 1)
                    if bi + 2 < len(blocks):
                        load_x(bi + 2)

                    def proj(col0, evac):
                        pi = cnt["ps"] % 4
                        cnt["ps"] += 1
                        for kc in range(KC):
                            k.mm(ps[pi][:, 0:N], win[:, kc, col0:col0 + 128], H[:, kc, 0:N], kc == 0, kc == KC - 1, ["win", hk], [pk[pi]])
                        evac(ps[pi][:, 0:N], pk[pi])

                    groups = [0, 1, 2] if kind == 0 else [1, 2]
                    for g in groups:
                        si = cnt["F"] % 2
                        cnt["F"] += 1
                        st = stF[si]
                        sk = "stF%d" % si
                        for c in range(4):
                            eng = "act" if c % 2 == 0 else "dve"
                            proj((g * 4 + c) * 128, lambda p_, pk_, c=c, eng=eng: k.cp(eng, st[:, c, 0:N], p_, [pk_], [sk + ":%d" % c]))
                        k.dma("sp", qkvT_s[g * 512:(g + 1) * 512, t0:t0 + N].rearrange("(c p) t -> p c t", p=128), st[:, :, 0:N],
                              [sk + ":%d" % c for c in range(4)], ["qkvT_s"])
                    si = cnt["B"] % 2
                    cnt["B"] += 1
                    st = stB[si]
                    sk = "stB%d" % si
                    for c in range(4):
                        eng = "act" if c % 2 == 0 else "dve"
                        proj(2064 + c * 128, lambda p_, pk_, c=c, eng=eng: k.cp(eng, st[:, c, 0:N], p_, [pk_], [sk + ":%d" % c]))
                    k.dma("sp", uT_s[:, t0:t0 + N].rearrange("(c p) t -> p c t", p=128), st[:, :, 0:N], [sk + ":%d" % c for c in range(4)], ["uT_s"])
                    if kind == 0:
                        for gg in range(4):
                            si = cnt["B"] % 2
                            cnt["B"] += 1
                            st = stB[si]
                            sk = "stB%d" % si
                            for c in range(4):
                                proj(2576 + (gg * 4 + c) * 128, lambda p_, pk_, c=c: k.act(st[:, c, 0:N], p_, AF.Sigmoid, [pk_], [sk + ":%d" % c]))
                            k.dma("sp", gT_s[gg * 512:(gg + 1) * 512, t0:t0 + N].rearrange("(c p) t -> p c t", p=128), st[:, :, 0:N],
                                  [sk + ":%d" % c for c in range(4)], ["gT_s"])
                    for ti in range(N // 128):
                        tsl = slice(ti * 128, (ti + 1) * 128)
                        if kind == 0:
                            for kc in range(KC):
                                k.mm(ps[4][:, :], H[:, kc, tsl], win[:, kc, 1536:2048], kc == 0, kc == KC - 1, [hk, "win"], ["ps4"])
                            si = cnt["Z"] % 2
                            cnt["Z"] += 1
                            k.act(stZ[si][:], ps[4][:, :], AF.Silu, ["ps4"], ["stZ%d" % si])
                            k.dma("sp", z_s[t0 + ti * 128:t0 + (ti + 1) * 128, :], stZ[si][:], ["stZ%d" % si], ["z_s"])
                        for kc in range(KC):
                            k.mm(ps[5][:, 0:16], H[:, kc, tsl], win[:, kc, 2048:2064], kc == 0, kc == KC - 1, [hk, "win"], ["ps5"])
                        si = cnt["A"] % 2
                        cnt["A"] += 1
                        k.cp("dve", stA[si][:], ps[5][:, 0:16], ["ps5"], ["stA%d" % si])
                        k.dma("sp", ba_s[t0 + ti * 128:t0 + (ti + 1) * 128, :], stA[si][:], ["stA%d" % si], ["ba_s"])
            S.barrier()
        p01.close()

        if upto >= 2 and not skip2:
            phase2(nc, S, k, sb, ps, pk, psall, cst, ident_b, ones_b, dict(qkvT_s=qkvT_s, ba_s=ba_s, o_s=o_s, convw=convw, gpar=gpar, dbg_s=dbg_s, dump=dbgdump, dn_limit=dn_limit))
            S.barrier()
        if upto >= 3 and not skip3:
            phase3(nc, S, k, sb, ps, pk, psall, cst, ident_b, dict(uT_s=uT_s, ygT_s=ygT_s, s5p=s5p, s5b=s5b, s5c=s5c, s5d=s5d, dump=dbgdump, y_dbg=y_dbg))
            S.barrier()
        if upto >= 4:
            d4 = dict(xT=xT, o_s=o_s, z_s=z_s, gT_s=gT_s, ygT_s=ygT_s, xl1_s=xl1_s, yT=yT, act_s=act_s, dump=dbgdump, w_a_out=w_a_out, w_glu=w_glu,
                      w_b_out=w_b_out, w_o=w_o, dnw=dnw, b_glu=b_glu, fcw=fcw, w_up=w_up, w_down=w_down, stop_after=stop_after)
            phase4(nc, S, k, sb, ps, pk, psall, cst, ident_b, ones_b, mods, A2, nwt, d4)
            S.barrier()
        S.finish(["qkvT_s", "z_s", "ba_s", "uT_s", "gT_s", "o_s", "xl1_s", "dbg_s", "yT", "ygT_s", "y_dbg", "act_s"] + dumpkeys)
    return nc


def phase2(nc, S, k, sb, ps, pk, psall, cst, ident_b, ones_b, dr):
    qkvT_s, ba_s, o_s, convw, gpar = dr["qkvT_s"], dr["ba_s"], dr["o_s"], dr["convw"], dr["gpar"]
    ident_f = cst[:, 0, :]
    ones_f = cst[:, 7, :]
    with ExitStack() as p2:
        khT = sb("khT", [128, 4, TT], BF16, stack=p2)
        qhT = sb("qhT", [128, 4, OWN], BF16, stack=p2)
        ktok = sb("ktok", [128, NT, 512], BF16, stack=p2)
        vtok = sb("vtok", [128, NT, 512], BF16, stack=p2)
        cw = sb("cw", [128, 12, 3], stack=p2)
        k.dma("sp", cw[:], convw[:, :, :], [], ["cw"])
        with ExitStack() as pa:
            sx = [sb("sx%d" % i, [128, 4, 514], stack=pa) for i in range(2)]
            acc = [sb("acc%d" % i, [128, 512], stack=pa) for i in range(4)]
            sil = [sb("sil%d" % i, [128, 512], stack=pa) for i in range(4)]
            sq = [sb("sq%d" % i, [128, 512], BF16, stack=pa) for i in range(4)]
            nrm = [sb("nrm%d" % i, [128, 512], stack=pa) for i in range(4)]
            vT = sb("vT", [128, 4, 512], BF16, stack=pa)
            blocks = [(i * 512, 512, 0, T) for i in range(8)] + [(T, 256, T, TT)]
            n = 0
            ci = 0
            for (t0, N, slo, shi) in blocks:
                nq = max(0, min(N, OWN - t0)) if t0 < T else 0
                for g in range(3):
                    if g == 0 and nq == 0:
                        continue
                    Ng = nq if g == 0 else N
                    st = sx[n % 2]
                    sk = "sx%d" % (n % 2)
                    n += 1
                    k.memset("pool", st[:, :, 0:1], 0.0, [sk])
                    k.memset("pool", st[:, :, Ng + 1:Ng + 2], 0.0, [sk])
                    lo = max(t0 - 1, slo)
                    hi = min(t0 + Ng + 1, OWN if g == 0 else shi)
                    k.dma("sp", st[:, :, lo - (t0 - 1):hi - (t0 - 1)],
                          qkvT_s[g * 512:(g + 1) * 512, lo:hi].rearrange("(c p) t -> p c t", p=128), ["qkvT_s"], [sk])
                    for stage in range(9):
                        for c in range(4):
                            a, ak = acc[c], "acc%d" % c
                            sl, slk = sil[c], "sil%d" % c
                            sqb, sqk = sq[c], "sq%d" % c
                            nr, nk = nrm[c], "nrm%d" % c
                            pi = 2 + c
                            gc = g * 4 + c
                            if stage == 0:
                                k.act(a[:, 0:Ng], st[:, c, 1:Ng + 1], AF.Copy, [sk, "cw"], [ak], scale=cw[:, gc, 1:2])
                            elif stage == 1:
                                k.stt(a[:, 0:Ng], st[:, c, 0:Ng], cw[:, gc, 0:1], a[:, 0:Ng], ALU.mult, ALU.add, [sk, "cw", ak], [ak])
                            elif stage == 2:
                                k.stt(a[:, 0:Ng], st[:, c, 2:Ng + 2], cw[:, gc, 2:3], a[:, 0:Ng], ALU.mult, ALU.add, [sk, "cw", ak], [ak])
                            elif stage == 3:
                                if g == 2:
                                    k.act(vT[:, c, 0:Ng], a[:, 0:Ng], AF.Silu, [ak], ["vT%d" % c])
                                else:
                                    k.act(sl[:, 0:Ng], a[:, 0:Ng], AF.Silu, [ak], [slk])
                            elif g == 2:
                                continue
                            elif stage == 4:
                                k.act(sqb[:, 0:Ng], sl[:, 0:Ng], AF.Square, [slk], [sqk])
                            elif stage == 5:
                                k.mm(ps[pi][:, 0:Ng], ones_b, sqb[:, 0:Ng], True, True, ["cstb", sqk], [pk[pi]])
                            elif stage == 6:
                                k.act(nr[:, 0:Ng], ps[pi][:, 0:Ng], AF.Sqrt, [pk[pi]], [nk], bias=1e-6)
                            elif stage == 7:
                                k.recip(nr[:, 0:Ng], nr[:, 0:Ng], [nk], [nk])
                            elif stage == 8:
                                if g == 1:
                                    k.tt("dve", khT[:, c, t0:t0 + Ng], sl[:, 0:Ng], nr[:, 0:Ng], ALU.mult, [slk, nk], ["khT%d" % c])
                                else:
                                    k.stt(qhT[:, c, t0:t0 + Ng], sl[:, 0:Ng], 128.0 ** -0.5, nr[:, 0:Ng], ALU.mult, ALU.mult, [slk, nk], ["qhT"])
                    if g >= 1:
                        for ti in range(Ng // 128):
                            tile_i = (t0 + ti * 128) // 128
                            pT = ps[6].bitcast(BF16)
                            for c in range(4):
                                src = khT[:, c, t0 + ti * 128:t0 + (ti + 1) * 128] if g == 1 else vT[:, c, ti * 128:(ti + 1) * 128]
                                k.tr(pT[:, c * 128:(c + 1) * 128], src, ident_b, [("khT%d" if g == 1 else "vT%d") % c, "cstb"], ["ps6"])
                            dst = ktok if g == 1 else vtok
                            k.cp("act", dst[:, tile_i, :], pT[:, 0:512], ["ps6"], ["ktok" if g == 1 else "vtok"])
        S.barrier()
        dump = dr["dump"]
        dump("khT", khT[:], ["khT"], [128, 4, TT], BF16)
        dump("qhT", qhT[:], ["qhT"], [128, 4, OWN], BF16)
        dump("ktok", ktok[:], ["ktok"], [128, NT, 512], BF16)
        dump("vtok", vtok[:], ["vtok"], [128, NT, 512], BF16)
        with ExitStack() as pb:
            ba = sb("ba", [128, NT, 16], stack=pb)
            gp = sb("gp", [128, 2, NT * 8], stack=pb)
            k.dma("sp", ba[:], ba_s.rearrange("(n p) c -> p n c", p=128), ["ba_s"], ["ba"])
            k.dma("sp", gp[:], gpar[:, :, :], [], ["gp"])
            nbeta = sb("nbeta", [128, NT, 8], stack=pb)
            graw = sb("graw", [128, NT, 8], stack=pb)
            gcol = sb("gcol", [128, NT, 8], stack=pb)
            eg = sb("eg", [128, NT, 8], stack=pb)
            t1 = sb("t1", [128, NT, 8], stack=pb)
            t2 = sb("t2", [128, NT, 8], stack=pb)
            k.act(t1[:], ba[:, :, 0:8], AF.Sigmoid, ["ba"], ["t1"])
            k.ts("pool", nbeta[:], t1[:], -1.0, ALU.mult, ["t1"], ["nbeta"])
            gpv = gp[:].rearrange("p a (n c) -> p a n c", c=8)
            k.tt("dve", t2[:], ba[:, :, 8:16], gpv[:, 1, :, :], ALU.add, ["ba", "gp"], ["t2"])
            k.act(t2[:], t2[:], AF.Exp, ["t2"], ["t2"])
            k.act(t2[:], t2[:], AF.Ln, ["t2"], ["t2"], bias=1.0)
            k.act(t1[:], gpv[:, 0, :, :], AF.Exp, ["gp", "nbeta"], ["t1"])
            k.stt(graw[:], t2[:], -1.0, t1[:], ALU.mult, ALU.mult, ["t1", "t2"], ["graw"])
            for ld in range(2):
                k.mm(ps[0][:, 0:NT * 4].rearrange("p (n c) -> p n c", c=4), cst[:, 1 + ld, :], graw[:, :, ld * 4:(ld + 1) * 4], True, True, ["cst", "graw"], ["ps0"])
                k.cp("dve", gcol[:, :, ld * 4:(ld + 1) * 4], ps[0][:, 0:NT * 4].rearrange("p (n c) -> p n c", c=4), ["ps0"], ["gcol"])
            k.act(eg[:], gcol[:], AF.Exp, ["gcol"], ["eg"])
            if dr.get("dbg_s") is not None:
                k.dma("sp", dr["dbg_s"][:, 128:128 + NT * 8], graw[:].rearrange("p n c -> p (n c)"), ["graw"], ["dbg_s"])
                k.dma("sp", dr["dbg_s"][:, 512:512 + NT * 8], gcol[:].rearrange("p n c -> p (n c)"), ["gcol"], ["dbg_s"])
                k.dma("sp", dr["dbg_s"][:, 1024:1024 + NT * 8], nbeta[:].rearrange("p n c -> p (n c)"), ["nbeta"], ["dbg_s"])

            R = 3
            TTb = [sb("TTb%d" % i, [128, 4, 128], BF16, stack=pb) for i in range(R)]
            ATb = [sb("ATb%d" % i, [128, 4, 128], BF16, stack=pb) for i in range(R)]
            kdb = [sb("kdb%d" % i, [128, 4, 128], BF16, stack=pb) for i in range(R)]
            egl = [sb("egl%d" % i, [128, 4], stack=pb) for i in range(R)]
            Dg = sb("Dg", [128, 4, 128], stack=pb)
            dT = sb("dT", [128, 4, 128], stack=pb)
            d2 = sb("d2", [128, 4, 128], stack=pb)
            decT = sb("decT", [128, 4, 128], stack=pb)
            decS = sb("decS", [128, 4, 128], stack=pb)
            XT32 = sb("XT32", [128, 4, 128], stack=pb)
            XAs = [sb("XA%d" % i, [128, 4, 256], BF16, stack=pb) for i in range(2)]
            XBs = [sb("XB%d" % i, [128, 4, 256], BF16, stack=pb) for i in range(2)]
            Mb = [sb("Mb%d" % i, [128, 4, 128], BF16, stack=pb) for i in range(1)]
            NT0b = sb("NT0b", [128, 4, 128], BF16, stack=pb)
            No1 = sb("No1", [128, 4, 128], BF16, stack=pb)
            No1T = sb("No1T", [128, 4, 128], BF16, stack=pb)
            No2 = sb("No2", [128, 4, 128], BF16, stack=pb)
            Pb = sb("Pb", [128, 4, 128], BF16, stack=pb)
            Qb = sb("Qb", [128, 4, 128], BF16, stack=pb)
            msk = sb("msk", [128, 3, 4, 128], BF16, stack=pb)
            for mi in range(3):
                for h in range(4):
                    k.cp("pool", msk[:, mi, h, :], cst[:, 8 + mi, :], ["cst"], ["msk"])
            idr = sb("idr", [128, 4, 128], stack=pb)
            dd = sb("dd", [128, 4], stack=pb)
            edl = sb("edl", [128, 4], stack=pb)
            for h in range(4):
                k.cp("pool", idr[:, h, :], ident_f, ["cst"], ["idr"])
            S32 = [sb("S32_%d" % i, [128, 4, 128], stack=pb) for i in range(2)]
            Sbf = [sb("Sbf_%d" % i, [128, 4, 128], BF16, stack=pb) for i in range(2)]
            for i in range(2):
                k.memset("pool", S32[i][:], 0.0, ["S32_%d_%d" % (i, h) for h in range(4)])
                k.memset("pool", Sbf[i][:], 0.0, ["Sbf_%d" % i])
            rbf = sb("rbf", [128, 4, 128], BF16, stack=pb)
            vnb = sb("vnb", [128, 4, 128], BF16, stack=pb)
            oq = sb("oq", [128, 4, 128], stack=pb)
            ost = [sb("ost%d" % i, [128, 512], stack=pb) for i in range(2)]
            psA = psall[:, 2:4, :].rearrange("p b (h x) -> p (b h) x", x=256)
            psB = psall[:, 4:6, :].rearrange("p b (h x) -> p (b h) x", x=256)

            order = {1: [33, 32] + list(range(31, -1, -1)), 0: [32, 33] + list(range(0, 17))}
            state = {"slot": 0, "ost": 0}

            def tcols(tl):
                t0 = tl * 128
                return slice(t0, t0 + 128)

            def pre_front(ld, tl, slot):
                own = tl < OWNT
                sfx = "%d" % slot
                last = 127 if ld == 0 else 0
                for h in range(4):
                    k.act(Dg[:, h, :], cst[:, 1 + ld, :], AF.Copy, ["cst", "graw"], ["Dg"], scale=graw[:, tl, ld * 4 + h:ld * 4 + h + 1])
                k.mm(ps[0][:, :], ones_f, Dg[:].rearrange("p h x -> p (h x)"), True, True, ["cst", "Dg"], ["ps0"])
                g0 = ps[0].rearrange("p (h x) -> p h x", x=128)
                for h in range(4):
                    gc = gcol[:, tl, ld * 4 + h:ld * 4 + h + 1]
                    k.stt(dT[:, h, :], g0[:, h, :], gc, cst[:, 3 + ld, :], ALU.subtract, ALU.add, ["ps0", "gcol", "cst"], ["dT"])
                    k.stt(d2[:, h, :], g0[:, h, :], gc, cst[:, 5 + ld, :], ALU.subtract, ALU.add, ["ps0", "gcol", "cst"], ["d2"])
                yield
                k.act(decT[:], dT[:], AF.Exp, ["dT"], ["decT"])
                k.act(decS[:], d2[:], AF.Exp, ["d2"], ["decS"], scale=-1.0)
                gl4 = ps[0][:, last::128]
                k.tt("dve", dd[:], gl4, gcol[:, tl, ld * 4:(ld + 1) * 4], ALU.subtract, ["ps0", "gcol"], ["dd"])
                k.act(edl[:], dd[:], AF.Exp, ["dd"], ["edl"])
                k.act(egl[slot][:], gl4, AF.Exp, ["ps0"], ["egl" + sfx])
                for h in range(4):
                    k.act(kdb[slot][:, h, :], ktok[:, tl, h * 128:(h + 1) * 128], AF.Copy, ["ktok", "edl"], ["kdb" + sfx], scale=edl[:, h:h + 1])
                yield
                k1 = ps[1].rearrange("p (h x) -> p h x", x=128)
                for h in range(4):
                    k.mm(k1[:, h, :], khT[:, h, tcols(tl)], khT[:, h, tcols(tl)], True, True, ["khT"], ["ps1"])
                M = Mb[0]
                for h in range(4):
                    k.stt(M[:, h, :], k1[:, h, :], nbeta[:, tl, ld * 4 + h:ld * 4 + h + 1], decS[:, h, :], ALU.mult, ALU.mult,
                          ["ps1", "nbeta", "decS"], ["Mb0"])
                if own:
                    for h in range(4):
                        k.mm(k1[:, h, :], khT[:, h, tcols(tl)], qhT[:, h, tcols(tl)], True, True, ["khT", "qhT"], ["ps1"])
                    k.tt("dve", ATb[slot][:], k1, decT[:], ALU.mult, ["ps1", "decT"], ["ATb" + sfx])
                yield
                pT = ps[0].bitcast(BF16)
                for h in range(4):
                    k.tr(pT[:, h * 128:(h + 1) * 128], M[:, h, :], ident_b, ["Mb0", "cstb"], ["ps0"])
                k.cp("act", NT0b[:], pT[:, 0:512].rearrange("p (h x) -> p h x", x=128), ["ps0"], ["NT0b"])
                yield

            def pre_back(ld, tl, slot, hh):
                sfx = "%d" % slot
                H = slice(2 * hh, 2 * hh + 2)
                hs = (2 * hh, 2 * hh + 1)
                M = Mb[0]
                XA, XB = XAs, XBs
                pA = psall[:, 2 + hh, :].rearrange("p (h x) -> p h x", x=256)
                pB = psall[:, 4 + hh, :].rearrange("p (h x) -> p h x", x=256)
                pAk, pBk = ["ps%d" % (2 + hh)], ["ps%d" % (4 + hh)]
                ka = lambda c: "XA%d_%d" % (c, hh)
                kb = lambda c: "XB%d_%d" % (c, hh)
                n1, n1t, n2, pbk, qbk = "No1_%d" % hh, "No1T_%d" % hh, "No2_%d" % hh, "Pb_%d" % hh, "Qb_%d" % hh
                k.tt("dve", XA[0][:, H, 128:256], NT0b[:, H], msk[:, 0, H], ALU.mult, ["NT0b", "msk"], [ka(0)])
                k.tt("dve", XB[0][:, H, 128:256], M[:, H], msk[:, 0, H], ALU.mult, ["Mb0", "msk"], [kb(0)])
                k.tt("dve", No1[:, H], M[:, H], msk[:, 1, H], ALU.mult, ["Mb0", "msk"], [n1])
                k.tt("dve", No1T[:, H], NT0b[:, H], msk[:, 1, H], ALU.mult, ["NT0b", "msk"], [n1t])
                k.tt("dve", No2[:, H], M[:, H], msk[:, 2, H], ALU.mult, ["Mb0", "msk"], [n2])
                yield
                k.tt("dve", XA[0][:, H, 0:128], XA[0][:, H, 128:256], idr[:, H], ALU.add, [ka(0), "idr"], [ka(0)])
                k.tt("dve", XA[1][:, H, 0:128], XA[0][:, H, 128:256], idr[:, H], ALU.add, [ka(0), "idr"], [ka(1)])
                k.tt("dve", XB[0][:, H, 0:128], XB[0][:, H, 128:256], idr[:, H], ALU.add, [kb(0), "idr"], [kb(0)])
                k.tt("dve", XB[1][:, H, 0:128], XB[0][:, H, 128:256], idr[:, H], ALU.add, [kb(0), "idr"], [kb(1)])
                yield
                cur = 0
                for lev in range(5):
                    a, b = XA[cur], XB[cur]
                    ak, bk = ka(cur), kb(cur)
                    nxt = 1 - cur
                    an, bn = XA[nxt], XB[nxt]
                    ank, bnk = ka(nxt), kb(nxt)
                    for hi_, h in enumerate(hs):
                        if lev == 0:
                            cs = slice(128, 256)
                        elif lev == 4:
                            cs = slice(0, 128)
                        else:
                            cs = slice(0, 256)
                        k.mm(pA[:, hi_, cs], b[:, h, 128:256], a[:, h, cs], True, True, [ak, bk], pAk)
                        k.mm(pB[:, hi_, cs], a[:, h, 128:256], b[:, h, cs], True, True, [ak, bk], pBk)
                    yield
                    if lev > 0:
                        k.tt("dve", an[:, H, 0:128], pA[:, :, 0:128], a[:, H, 0:128], ALU.add, pAk + [ak], [ank])
                        k.tt("dve", bn[:, H, 0:128], pB[:, :, 0:128], b[:, H, 0:128], ALU.add, pBk + [bk], [bnk])
                    if lev < 4:
                        k.cp("act", an[:, H, 128:256], pA[:, :, 128:256], pAk, [ank])
                        k.cp("act", bn[:, H, 128:256], pB[:, :, 128:256], pBk, [bnk])
                    cur = nxt
                    yield
                a, b = XA[cur], XB[cur]
                ak, bk = ka(cur), kb(cur)
                nxt = 1 - cur
                an, bn = XA[nxt], XB[nxt]
                ank, bnk = ka(nxt), kb(nxt)
                for hi_, h in enumerate(hs):
                    k.mm(pA[:, hi_, 0:128], No1[:, h, :], a[:, h, 0:128], True, True, [n1, ak], pAk)
                    k.mm(pB[:, hi_, 0:128], No1T[:, h, :], b[:, h, 0:128], True, True, [n1t, bk], pBk)
                yield
                k.cp("act", Pb[:, H], pA[:, :, 0:128], pAk, [pbk])
                k.cp("dve", Qb[:, H], pB[:, :, 0:128], pBk, [qbk])
                yield
                for hi_, h in enumerate(hs):
                    k.mm(pA[:, hi_, 128:256], b[:, h, 0:128], Pb[:, h, :], True, True, [bk, pbk], pAk)
                    k.mm(pB[:, hi_, 128:256], a[:, h, 0:128], Qb[:, h, :], True, True, [ak, qbk], pBk)
                yield
                k.tt("dve", an[:, H, 0:128], pA[:, :, 128:256], a[:, H, 0:128], ALU.add, pAk + [ak], [ank])
                k.tt("dve", bn[:, H, 0:128], pB[:, :, 128:256], b[:, H, 0:128], ALU.add, pBk + [bk], [bnk])
                cur = nxt
                yield
                a, b = XA[cur], XB[cur]
                ak, bk = ka(cur), kb(cur)
                for hi_, h in enumerate(hs):
                    k.mm(pA[:, hi_, 0:128], No2[:, h, :], a[:, h, 0:128], True, True, [n2, ak], pAk)
                yield
                k.cp("act", Pb[:, H], pA[:, :, 0:128], pAk, [pbk])
                yield
                for hi_, h in enumerate(hs):
                    k.mm(pA[:, hi_, 128:256], b[:, h, 0:128], Pb[:, h, :], True, True, [bk, pbk], pAk)
                yield
                xk = "XT32_%d" % hh
                k.tt("dve", XT32[:, H], pA[:, :, 128:256], a[:, H, 0:128], ALU.add, pAk + [ak], [xk])
                for h in hs:
                    k.act(TTb[slot][:, h, :], XT32[:, h, :], AF.Copy, [xk, "nbeta"], ["TTb" + sfx], scale=nbeta[:, tl, ld * 4 + h:ld * 4 + h + 1])
                yield

            def serial(ld, tl, slot):
                own = tl < OWNT
                sfx = "%d" % slot
                sk32, skb = "S32_%d" % ld, "Sbf_%d" % ld
                p5 = ps[6].rearrange("p (h x) -> p h x", x=128)
                p7 = ps[7].rearrange("p (h x) -> p h x", x=128)
                for h in range(4):
                    k.mm(p5[:, h, :], khT[:, h, tcols(tl)], Sbf[ld][:, h, :], True, True, ["khT", skb], ["ps6"])
                yield
                for h in range(4):
                    k.stt(rbf[:, h, :], p5[:, h, :], eg[:, tl, ld * 4 + h:ld * 4 + h + 1], vtok[:, tl, h * 128:(h + 1) * 128], ALU.mult, ALU.subtract,
                          ["ps6", "eg", "vtok"], ["rbf"])
                yield
                for h in range(4):
                    k.mm(p7[:, h, :], TTb[slot][:, h, :], rbf[:, h, :], True, True, ["TTb" + sfx, "rbf"], ["ps7"])
                if own:
                    for h in range(4):
                        k.mm(p5[:, h, :], qhT[:, h, tcols(tl)], Sbf[ld][:, h, :], True, True, ["qhT", skb], ["ps6"])
                yield
                k.cp("act", vnb[:], p7, ["ps7"], ["vnb"])
                if own:
                    for h in range(4):
                        k.act(oq[:, h, :], p5[:, h, :], AF.Copy, ["ps6", "eg"], ["oq"], scale=eg[:, tl, ld * 4 + h:ld * 4 + h + 1])
                yield
                for h in range(4):
                    k.mm(p7[:, h, :], kdb[slot][:, h, :], vnb[:, h, :], True, True, ["kdb" + sfx, "vnb"], ["ps7"])
                if own:
                    for h in range(4):
                        k.mm(p5[:, h, :], ATb[slot][:, h, :], vnb[:, h, :], True, True, ["ATb" + sfx, "vnb"], ["ps6"])
                yield
                for h in range(4):
                    k.stt(S32[ld][:, h, :], S32[ld][:, h, :], egl[slot][:, h:h + 1], p7[:, h, :], ALU.mult, ALU.add,
                          ["%s_%d" % (sk32, h), "egl" + sfx, "ps7"], ["%s_%d" % (sk32, h)])
                yield
                k.cp("act", Sbf[ld][:], S32[ld][:], ["%s_%d" % (sk32, h) for h in range(4)], [skb])
                if own:
                    oi = state["ost"] % 2
                    state["ost"] += 1
                    k.tt("dve", ost[oi][:].rearrange("p (h x) -> p h x", x=128), p5, oq[:], ALU.add, ["ps6", "oq"], ["ost%d" % oi])
                    k.dma("sp", o_s[ld, tl * 128:(tl + 1) * 128, :], ost[oi][:], ["ost%d" % oi], ["o_s"])
                yield

            steps = []
            for i in range(34):
                steps.append((1, order[1][i]))
                if i < len(order[0]):
                    steps.append((0, order[0][i]))
            lim = dr.get("dn_limit")
            if lim:
                steps = steps[:lim]
            def runall(gens):
                gens = [g_ for g_ in gens if g_ is not None]
                while gens:
                    nxt_ = []
                    for g_ in gens:
                        if next(g_, "done") != "done":
                            nxt_.append(g_)
                    gens = nxt_

            nst = len(steps)
            runall([pre_front(steps[0][0], steps[0][1], 0)])
            for i in range(nst):
                ld, tl = steps[i]
                b0, b1 = pre_back(ld, tl, i % R, 0), pre_back(ld, tl, i % R, 1)
                fr = pre_front(steps[i + 1][0], steps[i + 1][1], (i + 1) % R) if i + 1 < nst else None
                se = serial(steps[i - 1][0], steps[i - 1][1], (i - 1) % R) if i >= 1 else None
                rnd = 0
                while b0 is not None or b1 is not None or fr is not None or se is not None:
                    if se is not None:
                        if next(se, "done") == "done":
                            se = None
                    if b0 is not None and next(b0, "done") == "done":
                        b0 = None
                    if b1 is not None and next(b1, "done") == "done":
                        b1 = None
                    backs_done = b0 is None and b1 is None
                    if fr is not None and (rnd % 2 == 1 or backs_done):
                        if next(fr, "done") == "done":
                            fr = None
                    rnd += 1
            runall([serial(steps[nst - 1][0], steps[nst - 1][1], (nst - 1) % R)])


def phase3(nc, S, k, sb, ps, pk, psall, cst, ident_b, dr):
    uT_s, ygT_s, s5p, s5b, s5c, s5d = dr["uT_s"], dr["ygT_s"], dr["s5p"], dr["s5b"], dr["s5c"], dr["s5d"]
    dump = dr["dump"]
    W = 1024

    def subkeys(base):
        return [base + ":%d" % i for i in range(16)] + [base + ":f%d" % i for i in range(8)]
    with ExitStack() as p3:
        uTcs = [sb("uTc%d" % i, [128, TT], BF16, stack=p3) for i in range(2)]
        CCb = sb("CCb", [128, 64, 128], BF16, stack=p3)
        for i in range(8):
            k.load_cast(CCb[:, i * 8:(i + 1) * 8, :].rearrange("p a b -> p (a b)"), s5c[:, i * 1024:(i + 1) * 1024], 1024, [], ["CCb"])
        for ld in range(2):
            v = CCb[:, ld * 32 + 16:ld * 32 + 32, :]
            k.ts("pool", v, v, -1.0, ALU.mult, ["CCb"], ["CCb"])
        dsk = sb("dsk", [128, 4], stack=p3)
        k.dma("sp", dsk[:], s5d[:, :], [], ["dsk"])
        LZ = sb("LZ", [128, 16, 128], BF16, stack=p3)
        LZb = sb("LZb", [128, 16, 128], BF16, stack=p3)
        P1 = sb("P1", [128, 3, 8, 32], stack=p3)
        PW8 = sb("PW8", [128, 3, 8, 32], stack=p3)
        PW64 = sb("PW64", [128, 3, 4, 32], stack=p3)
        PWK = sb("PWK", [128, 3, 7, 32], stack=p3)
        WKS = [[sb("WKS%d%d" % (d_, i_), [128, 2, 196], stack=p3) for i_ in range(2)] for d_ in range(2)]
        for d_ in range(2):
            for i_ in range(2):
                k.memset("pool", WKS[d_][i_][:], 0.0, ["WKS%d%d" % (d_, i_)])
        with ExitStack() as pp:
            prm = sb("prm", [128, 3, W], stack=pp)
            bb = sb("bb", [128, 2, W], stack=pp)
            k.dma("sp", prm[:], s5p[:, :, :], [], ["prm"])
            k.dma("sp", bb[:], s5b[:, :, :], [], ["bb"])
            names = ["dt", "mag", "th", "c", "s", "t1", "t2", "t3", "lr", "li", "fr", "fi"]
            A = {n_: sb("w_" + n_, [128, W], stack=pp) for n_ in names}
            key = lambda n_: "w_" + n_
            are, aim = prm[:, 0, :], prm[:, 1, :]
            k.act(A["dt"][:], prm[:, 2, :], AF.Exp, ["prm"], [key("dt")])
            k.tt("dve", A["mag"][:], are, A["dt"][:], ALU.mult, ["prm", key("dt")], [key("mag")])
            k.act(A["mag"][:], A["mag"][:], AF.Exp, [key("mag")], [key("mag")])
            k.tt("dve", A["th"][:], aim, A["dt"][:], ALU.mult, ["prm", key("dt")], [key("th")])
            k.act(A["s"][:], A["th"][:], AF.Sin, [key("th")], [key("s")], scale=0.125)
            hp = sb("hp", [128, 1], stack=pp)
            k.memset("pool", hp[:], float(np.pi / 2), ["hp"])
            k.act(A["c"][:], A["th"][:], AF.Sin, [key("th"), "hp"], [key("c")], scale=-0.125, bias=hp[:, 0:1])
            for it in range(3):
                k.tt("dve", A["t1"][:], A["c"][:], A["c"][:], ALU.mult, [key("c")], [key("t1")])
                k.tt("dve", A["t2"][:], A["s"][:], A["s"][:], ALU.mult, [key("s")], [key("t2")])
                k.tt("dve", A["t3"][:], A["c"][:], A["s"][:], ALU.mult, [key("c"), key("s")], [key("t3")])
                k.tt("dve", A["c"][:], A["t1"][:], A["t2"][:], ALU.subtract, [key("t1"), key("t2")], [key("c")])
                k.ts("dve", A["s"][:], A["t3"][:], 2.0, ALU.mult, [key("t3")], [key("s")])
            k.tt("dve", A["lr"][:], A["mag"][:], A["c"][:], ALU.mult, [key("mag"), key("c")], [key("lr")])
            k.tt("dve", A["li"][:], A["mag"][:], A["s"][:], ALU.mult, [key("mag"), key("s")], [key("li")])
            k.tt("dve", A["t1"][:], are, are, ALU.mult, ["prm"], [key("t1")])
            k.tt("dve", A["t2"][:], aim, aim, ALU.mult, ["prm"], [key("t2")])
            k.tt("dve", A["t1"][:], A["t1"][:], A["t2"][:], ALU.add, [key("t1"), key("t2")], [key("t1")])
            k.recip(A["t1"][:], A["t1"][:], [key("t1")], [key("t1")])
            k.ts("dve", A["t2"][:], A["lr"][:], -1.0, ALU.add, [key("lr")], [key("t2")])
            k.tt("dve", A["fr"][:], A["t2"][:], are, ALU.mult, [key("t2"), "prm"], [key("fr")])
            k.tt("dve", A["t3"][:], A["li"][:], aim, ALU.mult, [key("li"), "prm"], [key("t3")])
            k.tt("dve", A["fr"][:], A["fr"][:], A["t3"][:], ALU.add, [key("fr"), key("t3")], [key("fr")])
            k.tt("dve", A["fr"][:], A["fr"][:], A["t1"][:], ALU.mult, [key("fr"), key("t1")], [key("fr")])
            k.tt("dve", A["fi"][:], A["li"][:], are, ALU.mult, [key("li"), "prm"], [key("fi")])
            k.tt("dve", A["t3"][:], A["t2"][:], aim, ALU.mult, [key("t2"), "prm"], [key("t3")])
            k.tt("dve", A["fi"][:], A["fi"][:], A["t3"][:], ALU.subtract, [key("fi"), key("t3")], [key("fi")])
            k.tt("dve", A["fi"][:], A["fi"][:], A["t1"][:], ALU.mult, [key("fi"), key("t1")], [key("fi")])
            BB = sb("BB", [128, 2, W], BF16, stack=pp)
            k.tt("dve", A["t1"][:], A["fr"][:], bb[:, 0, :], ALU.mult, [key("fr"), "bb"], [key("t1")])
            k.tt("dve", A["t2"][:], A["fi"][:], bb[:, 1, :], ALU.mult, [key("fi"), "bb"], [key("t2")])
            k.tt("dve", BB[:, 0, :], A["t1"][:], A["t2"][:], ALU.subtract, [key("t1"), key("t2")], ["BB"])
            k.tt("dve", A["t1"][:], A["fr"][:], bb[:, 1, :], ALU.mult, [key("fr"), "bb"], [key("t1")])
            k.tt("dve", A["t2"][:], A["fi"][:], bb[:, 0, :], ALU.mult, [key("fi"), "bb"], [key("t2")])
            k.tt("dve", BB[:, 1, :], A["t1"][:], A["t2"][:], ALU.add, [key("t1"), key("t2")], ["BB"])
            pT = ps[7].bitcast(BF16)
            for ld in range(2):
                for ri in range(2):
                    for cu in range(4):
                        idx = (ld * 2 + ri) * 4 + cu
                        c0 = ld * 512 + cu * 128
                        k.tr(pT[:, (idx % 4) * 128:(idx % 4 + 1) * 128], BB[:, ri, c0:c0 + 128], ident_b, ["BB", "cstb"], ["ps7"])
                        if idx % 4 == 3:
                            k.cp("act", LZ[:, idx - 3:idx + 1, :], pT[:, 0:512].rearrange("p (a x) -> p a x", x=128), ["ps7"], ["LZ"])
            lrv = A["lr"][:].rearrange("p (a r) -> p a r", r=32)[:, :, 0]
            liv = A["li"][:].rearrange("p (a r) -> p a r", r=32)[:, :, 0]
            k.cp("dve", P1[:, 0, 0, :], lrv, [key("lr")], ["P1"])
            k.cp("dve", P1[:, 1, 0, :], liv, [key("li")], ["P1"])
            q1 = sb("q1", [128, 4, 32], stack=pp)

            def cmul(dst, m, a_r, a_i, b_r, b_i, dk):
                k.tt("dve", q1[:, 0, :], a_r, b_r, ALU.mult, [dk], ["q1"])
                k.tt("dve", q1[:, 1, :], a_i, b_i, ALU.mult, [dk], ["q1"])
                k.tt("dve", q1[:, 2, :], a_r, b_i, ALU.mult, [dk], ["q1"])
                k.tt("dve", q1[:, 3, :], a_i, b_r, ALU.mult, [dk], ["q1"])
                k.tt("dve", dst[:, 0, m, :], q1[:, 0, :], q1[:, 1, :], ALU.subtract, ["q1"], [dk])
                k.tt("dve", dst[:, 1, m, :], q1[:, 2, :], q1[:, 3, :], ALU.add, ["q1"], [dk])
            for m in range(1, 8):
                cmul(P1, m, P1[:, 0, m - 1, :], P1[:, 1, m - 1, :], P1[:, 0, 0, :], P1[:, 1, 0, :], "P1")
            k.cp("dve", PW8[:, 0, 0, :], P1[:, 0, 7, :], ["P1"], ["P16"])
            k.cp("dve", PW8[:, 1, 0, :], P1[:, 1, 7, :], ["P1"], ["P16"])
            for m in range(1, 8):
                cmul(PW8, m, PW8[:, 0, m - 1, :], PW8[:, 1, m - 1, :], PW8[:, 0, 0, :], PW8[:, 1, 0, :], "P16")
            k.cp("dve", PW64[:, 0, 0, :], PW8[:, 0, 7, :], ["P16"], ["P16"])
            k.cp("dve", PW64[:, 1, 0, :], PW8[:, 1, 7, :], ["P16"], ["P16"])
            for m in range(1, 4):
                cmul(PW64, m, PW64[:, 0, m - 1, :], PW64[:, 1, m - 1, :], PW64[:, 0, 0, :], PW64[:, 1, 0, :], "P16")
            for kk_, src_m in ((0, 0), (1, 1), (2, 3)):
                k.cp("dve", PWK[:, 0, kk_, :], PW64[:, 0, src_m, :], ["P16"], ["P16"])
                k.cp("dve", PWK[:, 1, kk_, :], PW64[:, 1, src_m, :], ["P16"], ["P16"])
            for kk_ in range(3, 7):
                cmul(PWK, kk_, PWK[:, 0, kk_ - 1, :], PWK[:, 1, kk_ - 1, :], PWK[:, 0, kk_ - 1, :], PWK[:, 1, kk_ - 1, :], "P16")
            k.ts("dve", PWK[:, 2, :, :], PWK[:, 1, :, :], -1.0, ALU.mult, ["P16"], ["P16"])
            k.ts("dve", PW8[:, 2, :, :], PW8[:, 1, :, :], -1.0, ALU.mult, ["P16"], ["P16"])
            k.ts("dve", PW64[:, 2, :, :], PW64[:, 1, :, :], -1.0, ALU.mult, ["P16"], ["P16"])
            k.ts("dve", P1[:, 2, :, :], P1[:, 1, :, :], -1.0, ALU.mult, ["P1"], ["P1"])
            k.cp("pool", LZb[:], LZ[:], ["LZ"], ["LZb"])
            k.memset("pool", LZb[64:96, :, :], 0.0, ["LZb"])
            dump("LZ", LZ[:], ["LZ"], [128, 16, 128], BF16)
        S.barrier()
        L1, L0 = TT, 2560
        Z1cs = [sb("Z1c%d" % a, [128, 2, L1], stack=p3) for a in range(2)]
        Z0cs = [sb("Z0c%d" % a, [128, 2, L0], stack=p3) for a in range(2)]
        Xb = [[sb("Xb%d%d" % (ld, ri), [128, OWN], BF16, stack=p3) for ri in range(2)] for ld in range(2)]
        yv = sb("yv", [128, 512], stack=p3)
        Gs = [sb("G%d" % i, [128, 4, OWN // 8], BF16, stack=p3) for i in range(2)]
        gcnt = {"n": 0}
        gw = [sb("gw%d" % i, [128, 512], stack=p3) for i in range(2)]
        ygs = [sb("ygs%d" % i, [128, 512], BF16, stack=p3) for i in range(2)]
        for a in range(2):
            k.memset("pool", Z0cs[a][:, :, 0:128], 0.0, subkeys("Z0%d" % a))
        oblocks = [(0, 512), (512, 512), (1024, 512), (1536, 512), (2048, 128)]
        zcnt = {"n": 0}

        def cmadd(dr_, di_, sr_, si_, pr, pi, npi, kr, ki, cd, cs):
            rd = [kr + ":%d" % cs, ki + ":%d" % cs, "P1", "P16"]
            wr_, wi_ = [kr + ":%d" % cd], [ki + ":%d" % cd]
            k.stt(dr_, sr_, pr, dr_, ALU.mult, ALU.add, rd + wr_, wr_)
            k.stt(di_, si_, pr, di_, ALU.mult, ALU.add, rd + wi_, wi_)
            k.stt(dr_, si_, npi, dr_, ALU.mult, ALU.add, rd + wr_, wr_)
            k.stt(di_, sr_, pi, di_, ALU.mult, ALU.add, rd + wi_, wi_)

        def scan(Zc, kz, L, rev, col):
            ns = L // 256
            Z5 = Zc[:].rearrange("p r (s b a i) -> p r s b a i", b=4, a=8, i=8)
            Z3 = Zc[:].rearrange("p r (q i) -> p r q i", i=8)
            Z2 = Zc[:].rearrange("p r (q a i) -> p r q a i", a=8, i=8)
            e8 = 0 if rev else 7
            e4 = 0 if rev else 3

            def sc(P, m):
                return P[:, 0, m, col:col + 1], P[:, 1, m, col:col + 1], P[:, 2, m, col:col + 1]

            def cm(dv, sv, pw, cd, cs_, wtag=None):
                pr, pi, npi = pw
                rd = [kz + ":%d" % c for c in (cs_, cs_ + 8)] + ["P1", "P16"]
                wr_ = [kz + ":%d" % c for c in (cd, cd + 8)] if wtag is None else [kz + ":f%d" % wtag]
                k.stt(dv, sv, pr, dv, ALU.mult, ALU.add, rd + wr_, wr_)
                yield
                k.stt(dv[:, 0], sv[:, 1], npi, dv[:, 0], ALU.mult, ALU.add, rd + wr_, wr_)
                k.stt(dv[:, 1], sv[:, 0], pi, dv[:, 1], ALU.mult, ALU.add, rd + wr_, wr_)
            for step in range(1, 8):
                i = 7 - step if rev else step
                pv = i + 1 if rev else i - 1
                yield from cm(Z3[:, :, :, i], Z3[:, :, :, pv], sc(P1, 0), i, pv)
                yield
            for step in range(1, 8):
                a = 7 - step if rev else step
                pv = a + 1 if rev else a - 1
                yield from cm(Z2[:, :, :, a, e8], Z2[:, :, :, pv, e8], sc(PW8, 0), e8, e8)
                yield
            nq = L // 64
            ldx = 1 if rev else 0
            V = Z2[:, :, :, e8, e8]
            ecls = [kz + ":%d" % c for c in (e8, e8 + 8)]
            cur = 0
            wk = lambda i_: "WKS%d%d" % (ldx, i_)
            k.cp("dve", WKS[ldx][0][:, :, 64:64 + nq], V, ecls, [wk(0)])
            yield
            kk_ = 0
            sft = 1
            while sft < nq:
                src, dst = WKS[ldx][cur], WKS[ldx][1 - cur]
                off = 64 + sft if rev else 64 - sft
                sview = src[:, :, off:off + nq]
                cview = src[:, :, 64:64 + nq]
                dview = dst[:, :, 64:64 + nq]
                pr, pi, npi = sc(PWK, kk_)
                k.stt(dview, sview, pr, cview, ALU.mult, ALU.add, [wk(cur), "P16"], [wk(1 - cur)])
                yield
                k.stt(dview[:, 0], sview[:, 1], npi, dview[:, 0], ALU.mult, ALU.add, [wk(cur), wk(1 - cur), "P16"], [wk(1 - cur)])
                k.stt(dview[:, 1], sview[:, 0], pi, dview[:, 1], ALU.mult, ALU.add, [wk(cur), wk(1 - cur), "P16"], [wk(1 - cur)])
                cur = 1 - cur
                sft *= 2
                kk_ += 1
                yield
            k.cp("dve", V, WKS[ldx][cur][:, :, 64:64 + nq], [wk(cur)], ecls)
            yield
            nq = L // 64
            for a in range(8):
                if a == e8:
                    continue
                m = (8 - a) if rev else (a + 1)
                if rev:
                    dsl, ssl = slice(0, OWN // 64 + 1), slice(1, OWN // 64 + 2)
                else:
                    dsl, ssl = slice(5, nq), slice(4, nq - 1)
                yield from cm(Z2[:, :, dsl, a, e8], Z2[:, :, ssl, e8, e8], sc(PW8, m - 1), e8, e8, wtag=a)
                yield

        def zfill(j, ld, a):
            cu, off = j // 4, (j % 4) * 32
            uTc, uk = uTcs[cu % 2], "uTc%d" % (cu % 2)
            if ld == 1:
                segs = [(0, c0, min(512, L1 - c0)) for c0 in range(0, L1, 512)]
                buf = Z1cs[a]
            else:
                segs = [(128 - T, T, 256)] + [(384, c0, n_) for (c0, n_) in oblocks]
                buf = Z0cs[a]
            for ri in range(2):
                for (dd_, c0, n_) in segs:
                    dcol = c0 + dd_ if ld == 0 else c0
                    pi_ = 5 + zcnt["n"] % 2
                    zcnt["n"] += 1
                    if off == 96:
                        k.mm(ps[pi_][:, 0:n_], LZb[64:128, (ld * 2 + ri) * 4 + cu, :], uTc[64:128, c0:c0 + n_], True, True, ["LZb", uk], [pk[pi_]])
                    else:
                        k.mm(ps[pi_][:, 0:n_], LZ[off:off + 32, (ld * 2 + ri) * 4 + cu, :], uTc[off:off + 32, c0:c0 + n_], True, True, ["LZ", uk], [pk[pi_]])
                    k.cp("act", buf[:, ri, dcol:dcol + n_], ps[pi_][:, 0:n_], [pk[pi_]], subkeys("Z%d%d" % (ld, a)))

        def fill_pair(j):
            cu = j // 4
            if j % 4 == 0:
                k.dma("sp", uTcs[cu % 2][:], uT_s[cu * 128:(cu + 1) * 128, :], ["uT_s"], ["uTc%d" % (cu % 2)])
            for ld in (1, 0):
                zfill(j, ld, j % 2)

        fill_pair(0)
        for j in range(16):
            cu = j // 4
            a = j % 2
            uTc, uk = uTcs[cu % 2], "uTc%d" % (cu % 2)
            if j + 1 < 16:
                fill_pair(j + 1)
            gens = [scan(Z1cs[a], "Z1%d" % a, L1, True, 16 + j), scan(Z0cs[a], "Z0%d" % a, L0, False, j)]
            while gens:
                gens = [g_ for g_ in gens if next(g_, "done") != "done"]
            for ri in range(2):
                k.cp("act", Xb[1][ri][:], Z1cs[a][:, ri, 0:OWN], subkeys("Z1%d" % a), ["Xb1%d" % ri])
            for ri in range(2):
                k.cp("act", Xb[0][ri][:], Z0cs[a][:, ri, 384:L0], subkeys("Z0%d" % a), ["Xb0%d" % ri])
            NG = OWN // 8
            for ld in range(2):
                Zc_ = Z1cs[a] if ld == 1 else Z0cs[a]
                zk = subkeys("Z%d%d" % (ld, a))
                col = (16 + j) if ld == 1 else j
                Zg = Zc_[:].rearrange("p r (q i) -> p r q i", i=8)
                for pos in range(8):
                    if ld == 1:
                        if pos == 0:
                            continue
                        m = 8 - pos
                        Er, Ei = Zg[:, 0, 1:NG + 1, 0], Zg[:, 1, 1:NG + 1, 0]
                    else:
                        if pos == 7:
                            continue
                        m = pos + 1
                        Er, Ei = Zg[:, 0, 47:47 + NG, 7], Zg[:, 1, 47:47 + NG, 7]
                    pr, pi, npi = P1[:, 0, m - 1, col:col + 1], P1[:, 1, m - 1, col:col + 1], P1[:, 2, m - 1, col:col + 1]
                    gi = gcnt["n"] % 2
                    gcnt["n"] += 1
                    G = Gs[gi]
                    gk = "G%d" % gi
                    k.act(G[:, 0, :], Er, AF.Copy, zk + ["P1"], [gk], scale=pr)
                    k.act(G[:, 1, :], Ei, AF.Copy, zk + ["P1"], [gk], scale=npi)
                    k.act(G[:, 2, :], Er, AF.Copy, zk + ["P1"], [gk], scale=pi)
                    k.act(G[:, 3, :], Ei, AF.Copy, zk + ["P1"], [gk], scale=pr)
                    first = (j % 4 == 0) and ld == 0 and pos == 0
                    for bi, (c0, n_) in enumerate(oblocks):
                        g0, gn = c0 // 8, n_ // 8
                        if first:
                            k.mm(ps[bi][:, 0:n_], CCb[:, (0 * 2 + 0) * 16 + j, :], Xb[0][0][:, c0:c0 + n_], True, False, ["CCb", "Xb00"], [pk[bi]])
                        O = ps[bi][:, 0:n_].rearrange("p (g i) -> p g i", i=8)[:, :, pos]
                        for q_, (cm_, gsel) in enumerate(((0, 0), (0, 1), (1, 2), (1, 3))):
                            k.mm(O, CCb[:, (ld * 2 + cm_) * 16 + j, :], G[:, gsel, g0:g0 + gn], False, False, ["CCb", gk], [pk[bi]])
            for bi, (c0, n_) in enumerate(oblocks):
                q = 0
                for ld in range(2):
                    for ri in range(2):
                        if not (ld == 0 and ri == 0 and j % 4 == 0):
                            k.mm(ps[bi][:, 0:n_], CCb[:, (ld * 2 + ri) * 16 + j, :], Xb[ld][ri][:, c0:c0 + n_],
                                 False, (j % 4 == 3) and q == 3, ["CCb", "Xb%d%d" % (ld, ri)], [pk[bi]])
                        elif False:
                            pass
                        q += 1
                if j % 4 != 0:
                    pass
            if j % 4 == 3:
                for bi, (c0, n_) in enumerate(oblocks):
                    g1, g2 = gw[0], gw[1]
                    yg = ygs[bi % 2]
                    ygk = "ygs%d" % (bi % 2)
                    k.stt(yv[:, 0:n_], uTc[:, c0:c0 + n_], dsk[:, cu:cu + 1], ps[bi][:, 0:n_], ALU.mult, ALU.add, [uk, "dsk", pk[bi]], ["yv"])
                    if dr.get("y_dbg") is not None:
                        k.dma("sp", dr["y_dbg"][cu * 128:(cu + 1) * 128, c0:c0 + n_], yv[:, 0:n_], ["yv"], ["y_dbg"])
                    ge = "dve" if cu == 3 else "pool"
                    k.tt(ge, g1[:, 0:n_], yv[:, 0:n_], yv[:, 0:n_], ALU.mult, ["yv"], ["gw0"])
                    k.ts(ge, g1[:, 0:n_], g1[:, 0:n_], 0.044715, ALU.mult, ["gw0"], ["gw0"], s2=1.0, op1=ALU.add)
                    k.tt(ge, g1[:, 0:n_], g1[:, 0:n_], yv[:, 0:n_], ALU.mult, ["gw0", "yv"], ["gw0"])
                    k.act(g2[:, 0:n_], g1[:, 0:n_], AF.Sigmoid, ["gw0"], ["gw1"], scale=1.5957691216057308)
                    k.tt(ge, yg[:, 0:n_], g2[:, 0:n_], yv[:, 0:n_], ALU.mult, ["gw1", "yv"], [ygk])
                    k.dma("sp", ygT_s[cu * 128:(cu + 1) * 128, c0:c0 + n_], yg[:, 0:n_], [ygk], ["ygT_s"])


def phase4(nc, S, k, sb, ps, pk, psall, cst, ident_b, ones_b, mods, A2, nwt, dr):
    xT, o_s, z_s, gT_s, ygT_s, xl1_s, yT = dr["xT"], dr["o_s"], dr["z_s"], dr["gT_s"], dr["ygT_s"], dr["xl1_s"], dr["yT"]
    dump = dr["dump"]
    blocks = [(0, 512), (512, 512), (1024, 512), (1536, 512), (2048, 128)]
    with ExitStack() as p4:
        h2 = sb("h2", [128, KC, OWN], BF16, stack=p4)
        with ExitStack() as pa:
            wa = sb("wa", [128, 4, D], BF16, stack=pa)
            wg = sb("wg", [128, 4, D], BF16, stack=pa)
            wbo = sb("wbo", [128, 4, D], BF16, stack=pa)
            wo = sb("wo", [128, KC, D], BF16, stack=pa)
            for kc in range(4):
                k.load_cast(wa[:, kc, :], dr["w_a_out"][kc * 128:(kc + 1) * 128, :], D, [], ["wa"])
                k.load_cast(wg[:, kc, :], dr["w_glu"][kc * 128:(kc + 1) * 128, :], D, [], ["wg"])
                k.load_cast(wbo[:, kc, :], dr["w_b_out"][kc * 128:(kc + 1) * 128, :], D, [], ["wbo"])
            for kc in range(KC):
                k.load_cast(wo[:, kc, :], dr["w_o"][kc * 128:(kc + 1) * 128, :], D, [], ["wo"])
            dnw = sb("dnw_sb", [128, 128], stack=pa)
            bgl = sb("bgl_sb", [128, 8], stack=pa)
            k.dma("sp", dnw[:], dr["dnw"][:, :], [], ["dnw"])
            k.dma("sp", bgl[:], dr["b_glu"][:, :], [], ["bgl"])
            xb = sb("xb4", [128, KC, 512], stack=pa)
            xl1 = sb("xl1", [128, KC, 512], stack=pa)
            gts = sb("gts", [128, 16, 512], BF16, stack=pa)
            ygbs = [sb("ygb%d" % i, [128, 4, 512], BF16, stack=pa) for i in range(2)]
            glus = [sb("glu%d" % i, [128, 4, 512], BF16, stack=pa) for i in range(2)]
            yTbs = [sb("yTb%d" % i, [128, 4, 512], BF16, stack=pa) for i in range(2)]
            mixin = sb("mixin", [128, KC, 512], BF16, stack=pa)
            o0 = [sb("o0_%d" % i, [128, 512], stack=pa) for i in range(2)]
            o1 = [sb("o1_%d" % i, [128, 512], stack=pa) for i in range(2)]
            zs = [sb("zs%d" % i, [128, 512], BF16, stack=pa) for i in range(2)]
            ons = [sb("on%d" % i, [128, 512], stack=pa) for i in range(2)]
            junk = sb("junk", [128, 128], stack=pa)
            ss4s = [sb("ss4_%d" % i, [128, 4], stack=pa) for i in range(2)]
            ytoks = [sb("ytok%d" % i, [128, 512], BF16, stack=pa) for i in range(2)]
            sgb = sb("sgb", [128, 512], stack=pa)
            tA = sb("tA", [128, 512], stack=pa)
            tB = sb("tB", [128, 512], stack=pa)
            sqb = sb("sqb4", [128, KC, 512], BF16, stack=pa)
            rstd = sb("rstd4", [128, 512], stack=pa)
            tmp = [sb("tmp4_%d" % i, [128, 512], stack=pa) for i in range(2)]
            tc = {"n": 0}

            def stageX(bx):
                t0, N = blocks[bx]
                ygb, ygk = ygbs[bx % 2], "ygb%d" % (bx % 2)
                yTb, yTk = yTbs[bx % 2], "yTb%d" % (bx % 2)
                glu, gluk = glus[bx % 2], "glu%d" % (bx % 2)
                k.dma("sp", ygb[:, :, 0:N], ygT_s[:, t0:t0 + N].rearrange("(c p) t -> p c t", p=128), ["ygT_s"], [ygk])
                for ti in range(N // 128):
                    r0 = t0 + ti * 128
                    bi = tc["n"] % 2
                    tc["n"] += 1
                    k.dma("sp", o0[bi][:], o_s[0, r0:r0 + 128, :], ["o_s"], ["o0_%d" % bi])
                    k.dma("sp", o1[bi][:], o_s[1, r0:r0 + 128, :], ["o_s"], ["o1_%d" % bi])
                    k.dma("sp", zs[bi][:], z_s[r0:r0 + 128, :], ["z_s"], ["zs%d" % bi])
                    ok = "o0_%d" % bi
                    k.tt("dve", o0[bi][:], o0[bi][:], o1[bi][:], ALU.add, [ok, "o1_%d" % bi], [ok])
                    on, onk = ons[bi], "on%d" % bi
                    ss4, ssk = ss4s[bi], "ss4_%d" % bi
                    ytok, ytk = ytoks[bi], "ytok%d" % bi
                    for h in range(4):
                        k.S.op("act", lambda h=h, ss4=ss4, bi=bi: nc.scalar.activation(out=junk[:], in_=o0[bi][:, h * 128:(h + 1) * 128], func=AF.Square, accum_out=ss4[:, h:h + 1]),
                               reads=[ok], writes=["junk", ssk])
                    k.act(ss4[:], ss4[:], AF.Sqrt, [ssk], [ssk], scale=1.0 / 128, bias=1e-6)
                    k.recip(ss4[:], ss4[:], [ssk], [ssk])
                    yield
                    for h in range(4):
                        k.stt(on[:, h * 128:(h + 1) * 128], o0[bi][:, h * 128:(h + 1) * 128], ss4[:, h:h + 1], dnw[:], ALU.mult, ALU.mult, [ok, ssk, "dnw"], [onk])
                    k.tt("dve", ytok[:], on[:], zs[bi][:], ALU.mult, [onk, "zs%d" % bi], [ytk])
                    pT = ps[7].bitcast(BF16)
                    for h in range(4):
                        k.tr(pT[:, h * 128:(h + 1) * 128], ytok[:, h * 128:(h + 1) * 128], ident_b, [ytk, "cstb"], ["ps7"])
                    k.cp("act", yTb[:, :, ti * 128:(ti + 1) * 128], pT[:, 0:512].rearrange("p (h x) -> p h x", x=128), ["ps7"], [yTk])
                    yield
                for a in range(4):
                    for kc in range(4):
                        k.mm(ps[0][:, 0:N], wg[:, kc, a * 128:(a + 1) * 128], ygb[:, kc, 0:N], kc == 0, kc == 3, ["wg", ygk], ["ps0"])
                    for kc in range(4):
                        k.mm(ps[1][:, 0:N], wg[:, kc, (a + 4) * 128:(a + 5) * 128], ygb[:, kc, 0:N], kc == 0, kc == 3, ["wg", ygk], ["ps1"])
                    k.act(sgb[:, 0:N], ps[1][:, 0:N], AF.Sigmoid, ["ps1", "bgl"], ["sgb"], bias=bgl[:, a + 4:a + 5])
                    k.stt(glu[:, a, 0:N], ps[0][:, 0:N], bgl[:, a:a + 1], sgb[:, 0:N], ALU.add, ALU.mult, ["ps0", "bgl", "sgb"], [gluk])
                    yield

            def stageY(bx):
                t0, N = blocks[bx]
                yTb, yTk = yTbs[bx % 2], "yTb%d" % (bx % 2)
                glu, gluk = glus[bx % 2], "glu%d" % (bx % 2)
                k.dma("sp", xb[:, :, 0:N], xT.rearrange("(kc p) t -> p kc t", p=128)[:, :, t0:t0 + N], [], ["xb4"])
                k.dma("sp", gts[:, :, 0:N], gT_s[:, t0:t0 + N].rearrange("(c p) t -> p c t", p=128), ["gT_s"], ["gts"])
                for oc in range(KC):
                    pa_, pb_ = 2 + (oc % 2) * 2, 3 + (oc % 2) * 2
                    for kc in range(4):
                        k.mm(ps[pa_][:, 0:N], wa[:, kc, oc * 128:(oc + 1) * 128], yTb[:, kc, 0:N], kc == 0, kc == 3, ["wa", yTk], [pk[pa_]])
                    for kc in range(4):
                        k.mm(ps[pb_][:, 0:N], wbo[:, kc, oc * 128:(oc + 1) * 128], glu[:, kc, 0:N], kc == 0, kc == 3, ["wbo", gluk], [pk[pb_]])
                    k.tt("dve", tA[:, 0:N], ps[pa_][:, 0:N], gts[:, oc, 0:N], ALU.mult, [pk[pa_], "gts"], ["tA"])
                    k.tt("dve", tB[:, 0:N], ps[pb_][:, 0:N], gts[:, 8 + oc, 0:N], ALU.mult, [pk[pb_], "gts"], ["tB"])
                    k.tt("dve", mixin[:, oc, 0:N], tA[:, 0:N], tB[:, 0:N], ALU.add, ["tA", "tB"], ["mixin"])
                    yield
                for oc in range(KC):
                    pi = 6 if oc % 2 == 0 else 2
                    for kc in range(KC):
                        k.mm(ps[pi][:, 0:N], wo[:, kc, oc * 128:(oc + 1) * 128], mixin[:, kc, 0:N], kc == 0, kc == KC - 1, ["wo", "mixin"], [pk[pi]])
                    k.stt(xl1[:, oc, 0:N], ps[pi][:, 0:N], mods[:, 2, oc, 0:1], xb[:, oc, 0:N], ALU.mult, ALU.add, [pk[pi], "mods", "xb4"], ["xl1"])
                    yield
                k.dma("sp", xl1_s[:, t0:t0 + N].rearrange("(c p) t -> p c t", p=128), xl1[:, :, 0:N], ["xl1"], ["xl1_s"])
                k.act(sqb[:, :, 0:N], xl1[:, :, 0:N], AF.Square, ["xl1"], ["sqb4"])
                for kc in range(KC):
                    k.mm(ps[3][:, 0:N], ones_b, sqb[:, kc, 0:N], kc == 0, kc == KC - 1, ["cstb", "sqb4"], ["ps3"])
                k.act(rstd[:, 0:N], ps[3][:, 0:N], AF.Sqrt, ["ps3"], ["rstd4"], scale=1.0 / D, bias=1e-6)
                k.recip(rstd[:, 0:N], rstd[:, 0:N], ["rstd4"], ["rstd4"])
                yield
                for kc in range(KC):
                    tb = tmp[kc % 2]
                    tk = "tmp4_%d" % (kc % 2)
                    k.stt(tb[:, 0:N], xl1[:, kc, 0:N], A2[:, kc:kc + 1], rstd[:, 0:N], ALU.mult, ALU.mult, ["xl1", "A2", "rstd4"], [tk])
                    k.act(h2[:, kc, t0:t0 + N], tb[:, 0:N], AF.Identity, [tk, "mods"], ["h2"], bias=mods[:, 3, kc, 0:1])
                    if kc % 2 == 1:
                        yield

            def run_rr(gens):
                gens = [g_ for g_ in gens if g_ is not None]
                while gens:
                    gens = [g_ for g_ in gens if next(g_, "done") != "done"]

            run_rr([stageX(0)])
            for bx in range(len(blocks)):
                run_rr([stageY(bx), stageX(bx + 1) if bx + 1 < len(blocks) else None])
        S.barrier()
        if dr.get("stop_after") == "4A":
            return
        act_s = dr["act_s"]
        wd = sb("wd", [128, 22, D], BF16, stack=p4)
        with ExitStack() as pbk:
            actst = [sb("actst%d" % i, [128, 512], BF16, stack=pbk) for i in range(2)]
            fcw = sb("fcw_sb", [128, 44, 9], stack=pbk)
            k.dma("sp", fcw[:], dr["fcw"][:, :, :], [], ["fcw"])
            wu = [sb("wu%d" % i, [128, KC, 128], BF16, stack=pbk) for i in range(4)]
            dg = [sb("dg%d" % i, [128, 9, 128], BF16, stack=pbk) for i in range(4)]
            upb = [sb("upb%d" % i, [128, 35 * 64], BF16, stack=pbk) for i in range(4)]
            sg = [sb("sg%d" % i, [128, 512], stack=pbk) for i in range(2)]
            cacc = [[sb("cacc%d%d" % (g_, i_), [128, 512], stack=pbk) for i_ in range(2)] for g_ in range(2)]
            for i in range(4):
                k.memset("pool", upb[i][:, 0:64], 0.0, ["upb%d" % i])
            ublocks = [(0, 512), (512, 512), (1024, 512), (1536, 512), (2048, 64)]
            pcnt = 0
            ccnt = 0
            w_up = dr["w_up"]
            for i in range(22):
                k.load_cast(wd[:, i, :], dr["w_down"][i * 128:(i + 1) * 128, :], D, [], ["wd"])
                sel = []
                for gv in range(2):
                    bi = (i % 2) * 2 + gv
                    ch = i + 22 * gv
                    col0 = ch * 128
                    k.dma("pool", wu[bi][:], w_up[:, col0:col0 + 128].rearrange("(kc p) c -> p kc c", p=128), [], ["wu%d" % bi])
                    for tap in range(9):
                        k.act(dg[bi][:, tap, :], cst[:, 0, :], AF.Copy, ["cst", "fcw"], ["dg%d" % bi], scale=fcw[:, ch, tap:tap + 1])
                    for (t0, N) in ublocks:
                        pi = pcnt % 4
                        pcnt += 1
                        for kc in range(KC):
                            k.mm(ps[pi][:, 0:N], wu[bi][:, kc, :], h2[:, kc, t0:t0 + N], kc == 0, kc == KC - 1, ["wu%d" % bi, "h2"], [pk[pi]])
                        k.cp("act", upb[bi][:, 64 + t0:64 + t0 + N], ps[pi][:, 0:N], [pk[pi]], ["upb%d" % bi])
                    sel.append(bi)
                for b in range(4):
                    pg, pv = 4 + (ccnt % 2) * 2, 5 + (ccnt % 2) * 2
                    ccnt += 1
                    for gv, pp in ((0, pg), (1, pv)):
                        bi = sel[gv]
                        U = upb[bi][:].rearrange("p (r c) -> p r c", c=64)
                        O = ps[pp].rearrange("p (r c) -> p r c", c=64)
                        taps = [(1, 1)] + [(a_, b_) for a_ in range(3) for b_ in range(3) if (a_, b_) != (1, 1)]
                        for ti, (a_, b_) in enumerate(taps):
                            da, db = a_ - 1, b_ - 1
                            c_lo, c_hi = max(0, -db), 64 - max(0, db)
                            r_in = 8 * b + 1 + da
                            k.mm(O[:, :, c_lo:c_hi], dg[bi][:, a_ * 3 + b_, :], U[:, r_in:r_in + 8, c_lo + db:c_hi + db], ti == 0, ti == 8,
                                 ["dg%d" % bi, "upb%d" % bi], [pk[pp]])
                    si = ccnt % 2
                    k.act(sg[si][:], ps[pg][:, :], AF.Silu, [pk[pg]], ["sg%d" % si])
                    k.tt("dve", actst[si][:], ps[pv][:, :], sg[si][:], ALU.mult, [pk[pv], "sg%d" % si], ["actst%d" % si])
                    k.dma("sp", act_s[i * 128:(i + 1) * 128, b * 512:(b + 1) * 512], actst[si][:], ["actst%d" % si], ["act_s"])
        S.barrier()
        phase4c(nc, S, k, sb, ps, pk, ones_b, mods, nwt, dr, wd)


def phase4c(nc, S, k, sb, ps, pk, ones_b, mods, nwt, dr, wd):
    xl1_s, yT, act_s = dr["xl1_s"], dr["yT"], dr["act_s"]
    with ExitStack() as pc:
        acb = [sb("acb%d" % i, [128, 22, 512], BF16, stack=pc) for i in range(2)]
        xl = [sb("xl_%d" % i, [128, KC, 512], stack=pc) for i in range(2)]
        sq = sb("sqc", [128, KC, 512], BF16, stack=pc)
        rstd = sb("rstdc", [128, 512], stack=pc)
        ob = [sb("ob%d" % i, [128, KC, 512], stack=pc) for i in range(1)]
        for b in range(4):
            X = xl[b % 2]
            xk = "xl_%d" % (b % 2)
            O = ob[0]
            okk = "ob0"
            k.dma("sp", X[:], xl1_s[:, b * 512:(b + 1) * 512].rearrange("(c p) t -> p c t", p=128), ["xl1_s"], [xk])
            act = acb[b % 2]
            ack = "acb%d" % (b % 2)
            k.dma("sp", act[:], act_s[:, b * 512:(b + 1) * 512].rearrange("(c p) t -> p c t", p=128), ["act_s"], [ack])
            for oc in range(KC):
                pi = oc % 4
                for kc in range(22):
                    k.mm(ps[pi][:, :], wd[:, kc, oc * 128:(oc + 1) * 128], act[:, kc, :], kc == 0, kc == 21, ["wd", ack], [pk[pi]])
                k.stt(X[:, oc, :], ps[pi][:, :], mods[:, 5, oc, 0:1], X[:, oc, :], ALU.mult, ALU.add, [pk[pi], "mods", xk], [xk])
            k.act(sq[:], X[:], AF.Square, [xk], ["sqc"])
            for kc in range(KC):
                k.mm(ps[4][:, :], ones_b, sq[:, kc, :], kc == 0, kc == KC - 1, ["cstb", "sqc"], ["ps4"])
            k.act(rstd[:], ps[4][:, :], AF.Sqrt, ["ps4"], ["rstdc"], scale=1.0 / D, bias=1e-6)
            k.recip(rstd[:], rstd[:], ["rstdc"], ["rstdc"])
            for oc in range(KC):
                k.stt(O[:, oc, :], X[:, oc, :], nwt[:, 2, oc:oc + 1], rstd[:], ALU.mult, ALU.mult, [xk, "nwt", "rstdc"], [okk])
            k.dma("sp", yT[:, b * 512:(b + 1) * 512].rearrange("(c p) t -> p c t", p=128), O[:], [okk], ["yT"])


def _chunk(v):
    return np.ascontiguousarray(v.reshape(-1, 128).T)


def make_consts():
    c = np.zeros((128, 11, 128), np.float32)
    i = np.arange(128)
    c[:, 0, :] = np.eye(128)
    c[:, 1, :] = (i[:, None] <= i[None, :])
    c[:, 2, :] = (i[:, None] >= i[None, :])
    c[:, 3, :] = np.where(i[None, :] >= i[:, None], 0.0, -BIG)
    c[:, 4, :] = np.where(i[None, :] <= i[:, None], 0.0, -BIG)
    c[:, 5, :] = np.where(i[None, :] < i[:, None], 0.0, BIG)
    c[:, 6, :] = np.where(i[None, :] > i[:, None], 0.0, BIG)
    c[:, 7, :] = 1.0
    c[:, 8, :] = (i[:, None] // 32 == i[None, :] // 32)
    c[:, 9, :] = (i[:, None] // 64 == i[None, :] // 64) & (i[:, None] // 32 != i[None, :] // 32)
    c[:, 10, :] = (i[:, None] // 64 != i[None, :] // 64)
    return c


def prep_core(inp, core, cst):
    b, h = core // 2, core % 2
    rev = h == 1
    dm = [1, 0] if rev else [0, 1]
    f = np.float32
    x = inp["x"][b]
    cx = inp["ctx"][b]
    if rev:
        x = x[::-1]
        cx = cx[::-1]
    m = {}
    m["xT"] = np.ascontiguousarray(x.T)
    m["ctxT"] = np.ascontiguousarray(cx.T)
    m["cvec"] = np.ascontiguousarray(np.stack([_chunk(inp["c"][b]), _chunk(inp["c_ctx"])], axis=-1))
    m["w_ada"] = np.ascontiguousarray(inp["w_ada"][0])
    m["b_ada"] = _chunk(inp["b_ada"][0])
    m["nw"] = np.ascontiguousarray(np.stack([_chunk(inp["norm1_w"][0]), _chunk(inp["norm2_w"][0]), _chunk(inp["norm_f_w"])], axis=1))
    w_in = inp["w_in"][0]
    if rev:
        w_in = w_in.copy()
        for base in (2048, 2056):
            blk = w_in[:, base:base + 8].copy()
            w_in[:, base:base + 4] = blk[:, 4:8]
            w_in[:, base + 4:base + 8] = blk[:, 0:4]
    m["w_in"] = np.ascontiguousarray(w_in)
    cw = inp["dn_conv_w"][0]
    if rev:
        cw = cw[::-1]
    m["convw"] = np.ascontiguousarray(cw.T.reshape(12, 128, 3).transpose(1, 0, 2))
    al = inp["dn_a_log"][0][dm].reshape(8)
    dtb = inp["dn_dt_bias"][0][dm].reshape(8)
    m["gpar"] = np.ascontiguousarray(np.broadcast_to(np.stack([np.tile(al, NT), np.tile(dtb, NT)])[None], (128, 2, NT * 8))).astype(f)
    m["dnw"] = np.ascontiguousarray(np.broadcast_to(inp["dn_norm_w"][0][None, :], (128, 128))).astype(f)
    m["w_a_out"] = np.ascontiguousarray(inp["w_a_out"][0])

    def pairlay(a):
        a = a[dm]
        return a.reshape(2, 16, 2, 64).transpose(2, 3, 0, 1).reshape(128, 2, 16)
    are = pairlay(inp["s5_a_re"][0])
    aim = pairlay(inp["s5_a_im"][0])
    ls = pairlay(np.broadcast_to(inp["s5_log_step"][0][:, :, None], (2, 32, 64)))
    rep = lambda a: np.broadcast_to(a[..., None], (128, 2, 16, 32)).reshape(128, 1024)
    m["s5p"] = np.ascontiguousarray(np.stack([rep(are), rep(aim), rep(ls)], axis=1)).astype(f)

    def blay(a):
        a = a[dm]
        o = np.zeros((2, 64, 2, 16, 2, 16), f)
        a6 = a.reshape(2, 16, 2, 64, 16)
        for gl in range(2):
            o[gl, :, :, :, gl, :] = a6[:, :, gl].transpose(2, 0, 1, 3)
        return o.reshape(128, 1024)
    m["s5b"] = np.ascontiguousarray(np.stack([blay(inp["s5_b_re"][0]), blay(inp["s5_b_im"][0])], axis=1))

    def clay(a):
        a = a[dm]
        o = np.zeros((2, 64, 2, 16, 4, 2, 16), f)
        a6 = a.reshape(2, 16, 2, 16, 64)
        for j in range(16):
            for gl in range(2):
                o[gl, :, :, j, j % 4, gl, :] = a6[:, j, gl].transpose(2, 0, 1)
        return o.reshape(128, 2, 16 * 128)
    cr = clay(inp["s5_c_re"][0])
    ci = clay(inp["s5_c_im"][0])
    m["s5c"] = np.ascontiguousarray(np.stack([cr, ci], axis=2).reshape(128, 2 * 2 * 16 * 128))
    m["s5d"] = _chunk(inp["s5_d"][0])
    m["w_glu"] = np.ascontiguousarray(inp["w_glu"][0])
    m["b_glu"] = _chunk(inp["b_glu"][0])
    m["w_b_out"] = np.ascontiguousarray(inp["w_b_out"][0])
    m["w_o"] = np.ascontiguousarray(inp["w_o"][0])
    m["w_up"] = np.ascontiguousarray(inp["w_up"][0])
    fw = inp["ffn_conv_w"][0]
    if rev:
        fw = fw[::-1, ::-1]
    m["fcw"] = np.ascontiguousarray(fw.reshape(9, 44, 128).transpose(2, 1, 0))
    m["w_down"] = np.ascontiguousarray(inp["w_down"][0])
    m["consts"] = cst
    return {k_: np.ascontiguousarray(v, dtype=np.float32) for k_, v in m.items()}


def kernel(**inputs):
    inp = {k_: np.asarray(v) for k_, v in inputs.items()}
    cst = make_consts()
    nc = build_program()
    in_maps = [prep_core(inp, c, cst) for c in range(8)]
    res = run_bass_kernel_spmd(nc, in_maps, core_ids=list(range(8)))
    out = np.zeros((4, T, D), np.float32)
    for c in range(8):
        b, h = c // 2, c % 2
        y = np.asarray(res.results[c]["yT"]).T
        if h == 0:
            out[b, 0:OUTN] = y
        else:
            out[b, T - OUTN:T] = y[::-1]
    return out
```

```python
import numpy as np
from contextlib import ExitStack
import concourse.bass as bass
import concourse.mybir as mybir
from concourse.bass_utils import run_bass_kernel_spmd

F32 = mybir.dt.float32
BF16 = mybir.dt.bfloat16
AF = mybir.ActivationFunctionType
ALU = mybir.AluOpType

D = 1024
KC = 8
T = 4096
CTX = 256
OWN = 2176
OUTN = 2048
TT = T + CTX
NTL = 32
NT = 34
OWNT = 17
INC = 4624
DFF = 2816
BIG = 30000.0


class Sched:
    def __init__(self, nc, es, same_engine_sync=True, n_dma_sems=32):
        self.nc = nc
        self.eng = {"pe": nc.tensor, "act": nc.scalar, "dve": nc.vector, "pool": nc.gpsimd, "sp": nc.sync}
        self.sem = {k: es.enter_context(nc.semaphore("sem_" + k)) for k in self.eng}
        self.cnt = {k: 0 for k in self.eng}
        self.seen = {k: {} for k in self.eng}
        self.dma_sems = [es.enter_context(nc.semaphore("dsem%d" % i)) for i in range(n_dma_sems)]
        self.dma_cnt = [0] * n_dma_sems
        self.dma_rr = 0
        self.W = {}
        self.R = {}
        self.same = same_engine_sync
        self.ninst = 0

    def _wait(self, e, tok):
        sem, val, owner = tok
        if owner == e and (not self.same or e == "pe"):
            return
        sid = id(sem)
        if self.seen[e].get(sid, 0) >= val:
            return
        self.eng[e].wait_ge(sem, val)
        self.seen[e][sid] = val

    def _deps(self, e, reads, writes):
        toks = []
        for k in reads:
            toks += list(self.W.get(k, {}).values())
        for k in writes:
            toks += [t for t in self.W.get(k, {}).values() if t[2] != e or k in reads]
            toks += [t for t in self.R.get(k, {}).values() if t[2] != e]
        for t in toks:
            self._wait(e, t)

    def _record(self, tok, reads, writes):
        sid = id(tok[0])
        for k in reads:
            self.R.setdefault(k, {})[sid] = tok
        for k in writes:
            self.W.setdefault(k, {})[sid] = tok
            self.R[k] = {}

    def op(self, e, fn, reads=(), writes=()):
        self._deps(e, reads, writes)
        inst = fn()
        self.cnt[e] += 1
        inst.then_inc(self.sem[e], 1)
        tok = (self.sem[e], self.cnt[e], e)
        self._record(tok, reads, writes)
        self.ninst += 1
        return tok

    def dma(self, q, out, in_, reads=(), writes=(), **kw):
        self._deps(q, reads, writes)
        i = self.dma_rr
        self.dma_rr = (self.dma_rr + 1) % len(self.dma_sems)
        sem = self.dma_sems[i]
        if self.dma_cnt[i] > 0:
            self._wait(q, (sem, self.dma_cnt[i], None))
        self.dma_cnt[i] += 16
        self.eng[q].dma_start(out=out, in_=in_, **kw).then_inc(sem, 16)
        tok = (sem, self.dma_cnt[i], None)
        self._record(tok, reads, writes)
        self.ninst += 1
        return tok

    def barrier(self):
        for e in self.eng:
            for o in self.eng:
                if o != e and self.cnt[o] > 0:
                    self._wait(e, (self.sem[o], self.cnt[o], o))
            for i, s in enumerate(self.dma_sems):
                if self.dma_cnt[i] > 0:
                    self._wait(e, (s, self.dma_cnt[i], None))

    def finish(self, keys):
        for k in keys:
            for t in self.W.get(k, {}).values():
                self._wait("sp", t)


class K:
    def __init__(self, nc, S):
        self.nc = nc
        self.S = S
        self.rr = 0

    def mm(self, out, lhsT, rhs, start, stop, r, w):
        nc = self.nc
        return self.S.op("pe", lambda: nc.tensor.matmul(out, lhsT=lhsT, rhs=rhs, start=start, stop=stop), reads=r, writes=w)

    def tr(self, out, in_, ident, r, w):
        nc = self.nc
        return self.S.op("pe", lambda: nc.tensor.transpose(out, in_, ident), reads=r, writes=w)

    def act(self, out, in_, func, r, w, scale=None, bias=None):
        nc = self.nc
        kw = {}
        if scale is not None:
            kw["scale"] = scale
        if bias is not None:
            kw["bias"] = bias
        return self.S.op("act", lambda: nc.scalar.activation(out=out, in_=in_, func=func, **kw), reads=r, writes=w)

    def stt(self, out, in0, scalar, in1, op0, op1, r, w):
        nc = self.nc
        return self.S.op("dve", lambda: nc.vector.scalar_tensor_tensor(out=out, in0=in0, scalar=scalar, in1=in1, op0=op0, op1=op1), reads=r, writes=w)

    def tt(self, e, out, in0, in1, op, r, w):
        eng = self.S.eng[e]
        return self.S.op(e, lambda: eng.tensor_tensor(out=out, in0=in0, in1=in1, op=op), reads=r, writes=w)

    def ts(self, e, out, in0, s1, op0, r, w, s2=None, op1=None):
        eng = self.S.eng[e]
        if op1 is None:
            return self.S.op(e, lambda: eng.tensor_scalar(out=out, in0=in0, scalar1=s1, scalar2=None, op0=op0), reads=r, writes=w)
        return self.S.op(e, lambda: eng.tensor_scalar(out=out, in0=in0, scalar1=s1, scalar2=s2, op0=op0, op1=op1), reads=r, writes=w)

    def cp(self, e, out, in_, r, w):
        if e == "act":
            return self.act(out, in_, AF.Copy, r, w)
        eng = self.S.eng[e]
        return self.S.op(e, lambda: eng.tensor_copy(out=out, in_=in_), reads=r, writes=w)

    def memset(self, e, ap, val, w):
        eng = self.S.eng[e]
        return self.S.op(e, lambda: eng.memset(ap, val), writes=w)

    def recip(self, out, in_, r, w):
        nc = self.nc
        return self.S.op("dve", lambda: nc.vector.reciprocal(out=out, in_=in_), reads=r, writes=w)

    def dma(self, q, out, in_, r, w, **kw):
        return self.S.dma(q, out, in_, reads=r, writes=w, **kw)

    def load_cast(self, dst, src, ncols, r, w):
        c = 0
        while c < ncols:
            n = min(1024, ncols - c)
            self.dma("pool", dst[..., c:c + n], src[..., c:c + n], r, w)
            c += n


def build_program(upto=99, dbg=False, dn_limit=None, skip2=False, skip3=False, stop_after=None):
    nc = bass.Bass("TRN2", target_bir_lowering=False)
    skind = "ExternalOutput" if dbg else "Internal"

    def din(name, shape, dt=F32):
        return nc.dram_tensor(name, list(shape), dt, kind="ExternalInput").ap()

    def dscr(name, shape, dt=F32):
        return nc.dram_tensor(name, list(shape), dt, kind=skind).ap()

    xT = din("xT", [D, T])
    ctxT = din("ctxT", [D, CTX])
    cvec = din("cvec", [128, KC, 2])
    w_ada = din("w_ada", [D, 6 * D])
    b_ada = din("b_ada", [128, 48])
    nw = din("nw", [128, 3, KC])
    w_in = din("w_in", [D, INC])
    convw = din("convw", [128, 12, 3])
    gpar = din("gpar", [128, 2, NT * 8])
    dnw = din("dnw", [128, 128])
    w_a_out = din("w_a_out", [512, D])
    s5p = din("s5p", [128, 3, 2 * 16 * 32])
    s5b = din("s5b", [128, 2, 2 * 16 * 32])
    s5c = din("s5c", [128, 2 * 2 * 16 * 128])
    s5d = din("s5d", [128, 4])
    w_glu = din("w_glu", [512, D])
    b_glu = din("b_glu", [128, 8])
    w_b_out = din("w_b_out", [512, D])
    w_o = din("w_o", [D, D])
    w_up = din("w_up", [D, 2 * DFF])
    fcw = din("fcw", [128, 44, 9])
    w_down = din("w_down", [DFF, D])
    consts = din("consts", [128, 11, 128])
    yT = nc.dram_tensor("yT", [D, OUTN], F32, kind="ExternalOutput").ap()

    qkvT_s = dscr("qkvT_s", [1536, TT])
    z_s = dscr("z_s", [OWN, 512], BF16)
    ba_s = dscr("ba_s", [TT, 16])
    uT_s = dscr("uT_s", [512, TT], BF16)
    gT_s = dscr("gT_s", [2048, OWN], BF16)
    o_s = dscr("o_s", [2, OWN, 512])
    xl1_s = dscr("xl1_s", [D, OWN])
    ygT_s = dscr("ygT_s", [512, OWN], BF16)
    act_s = dscr("act_s", [DFF, OUTN], BF16)
    y_dbg = dscr("y_dbg", [512, OWN]) if dbg else None
    dbg_s = dscr("dbg_s", [128, 4096]) if dbg else None

    with ExitStack() as es:
        S = Sched(nc, es)
        k = K(nc, S)

        def sb(name, shape, dt=F32, stack=es):
            return stack.enter_context(nc.sbuf_tensor(name, list(shape), dt))

        psall = es.enter_context(nc.psum_tensor("psall", [128, 8, 512], F32))
        ps = [psall[:, i, :] for i in range(8)]
        pk = ["ps%d" % i for i in range(8)]

        def dbgdump(name, ap, keys, shape, dt=F32):
            if not dbg:
                return
            t_ = nc.dram_tensor("dd_" + name, list(shape), dt, kind="ExternalOutput").ap()
            k.dma("sp", t_, ap, keys, ["dd_" + name])
            dumpkeys.append("dd_" + name)
        dumpkeys = []

        cst = sb("cst", [128, 11, 128])
        k.dma("sp", cst[:], consts[:, :, :], [], ["cst"])
        ident_f = cst[:, 0, :]
        ones_f = cst[:, 7, :]
        cstb = sb("cstb", [128, 2, 128], BF16)
        k.cp("dve", cstb[:, 0, :], cst[:, 0, :], ["cst"], ["cstb"])
        k.cp("dve", cstb[:, 1, :], cst[:, 7, :], ["cst"], ["cstb"])
        ident_b = cstb[:, 0, :]
        ones_b = cstb[:, 1, :]
        mods = sb("mods", [128, 6, KC, 2])
        nwt = sb("nwt", [128, 3, KC])
        k.dma("sp", nwt[:], nw[:, :, :], [], ["nwt"])
        A1 = sb("A1", [128, KC, 2])
        A2 = sb("A2", [128, KC])

        p01 = es.enter_context(ExitStack())
        win = sb("win", [128, KC, INC], BF16, stack=p01)
        for kc in range(KC):
            k.load_cast(win[:, kc, :], w_in[kc * 128:(kc + 1) * 128, :], INC, [], ["win"])
        with ExitStack() as p0:
            cv = sb("cv", [128, KC, 2], stack=p0)
            scv = sb("scv", [128, KC, 2], stack=p0)
            bad = sb("bad", [128, 48], stack=p0)
            k.dma("sp", cv[:], cvec[:, :, :], [], ["cv"])
            k.dma("sp", bad[:], b_ada[:, :], [], ["bad"])
            k.act(scv[:], cv[:], AF.Silu, ["cv"], ["scv"])
            wad = [sb("wad%d" % i, [128, KC, D], stack=p0) for i in range(2)]
            for j in range(6):
                wb = wad[j % 2]
                wk = "wad%d" % (j % 2)
                for kc in range(KC):
                    k.dma("sp", wb[:, kc, :], w_ada[kc * 128:(kc + 1) * 128, j * D:(j + 1) * D], [], [wk])
                for oc in range(KC):
                    for kc in range(KC):
                        k.mm(ps[0][:, (j * 8 + oc) * 2:(j * 8 + oc) * 2 + 2], wb[:, kc, oc * 128:(oc + 1) * 128], scv[:, kc, :],
                             kc == 0, kc == KC - 1, [wk, "scv"], ["ps0"])
                for w_ in range(2):
                    k.tt("dve", mods[:, j, :, w_], ps[0][:, j * 16 + w_:j * 16 + 16:2], bad[:, j * 8:(j + 1) * 8], ALU.add,
                         ["ps0", "bad"], ["mods"])
            for w_ in range(2):
                k.stt(A1[:, :, w_], mods[:, 1, :, w_], 1.0, nwt[:, 0, :], ALU.add, ALU.mult, ["mods", "nwt"], ["A1"])
            k.stt(A2[:], mods[:, 4, :, 0], 1.0, nwt[:, 1, :], ALU.add, ALU.mult, ["mods", "nwt"], ["A2"])
        S.barrier()
        if dbg:
            k.dma("sp", dbg_s[:, 0:96], mods[:].rearrange("p a b c -> p (a b c)"), ["mods"], ["dbg_s"])

        if upto >= 1:
            with ExitStack() as p1:
                xb = [sb("xb%d" % i, [128, KC, 512], stack=p1) for i in range(2)]
                sqb = sb("sqb", [128, KC, 512], BF16, stack=p1)
                hb = [sb("hb%d" % i, [128, KC, 512], BF16, stack=p1) for i in range(2)]
                tmp = [sb("tmp%d" % i, [128, 512], stack=p1) for i in range(2)]
                rstd = [sb("rstd%d" % i, [128, 512], stack=p1) for i in range(2)]
                stF = [sb("stF%d" % i, [128, 4, 512], stack=p1) for i in range(2)]
                stB = [sb("stB%d" % i, [128, 4, 512], BF16, stack=p1) for i in range(2)]
                stZ = [sb("stZ%d" % i, [128, 512], BF16, stack=p1) for i in range(2)]
                stA = [sb("stA%d" % i, [128, 16], stack=p1) for i in range(2)]
                blocks = [(0, 512, 0), (512, 512, 0), (1024, 512, 0), (1536, 512, 0), (2048, 128, 0),
                          (2176, 512, 1), (2688, 512, 1), (3200, 512, 1), (3712, 384, 1), (4096, 256, 2)]
                cnt = {"F": 0, "B": 0, "Z": 0, "A": 0, "ps": 0}

                def load_x(bi):
                    t0, N, kind = blocks[bi]
                    xk = "xb%d" % (bi % 2)
                    src = ctxT if kind == 2 else xT
                    c0 = t0 - T if kind == 2 else t0
                    k.dma("sp", xb[bi % 2][:, :, 0:N], src.rearrange("(kc p) t -> p kc t", p=128)[:, :, c0:c0 + N], [], [xk])

                def norm_block(bi):
                    t0, N, kind = blocks[bi]
                    X = xb[bi % 2]
                    xk = "xb%d" % (bi % 2)
                    H = hb[bi % 2]
                    hk = "hb%d" % (bi % 2)
                    wsel = 1 if kind == 2 else 0
                    k.act(sqb[:, :, 0:N], X[:, :, 0:N], AF.Square, [xk], ["sqb"])
                    for kc in range(KC):
                        k.mm(ps[7][:, 0:N], ones_b, sqb[:, kc, 0:N], kc == 0, kc == KC - 1, ["cstb", "sqb"], ["ps7"])
                    rs = rstd[bi % 2]
                    rk = "rstd%d" % (bi % 2)
                    k.act(rs[:, 0:N], ps[7][:, 0:N], AF.Sqrt, ["ps7"], [rk], scale=1.0 / D, bias=1e-6)
                    k.recip(rs[:, 0:N], rs[:, 0:N], [rk], [rk])
                    for kc in range(KC):
                        tb = tmp[kc % 2]
                        tk = "tmp%d" % (kc % 2)
                        k.stt(tb[:, 0:N], X[:, kc, 0:N], A1[:, kc, wsel:wsel + 1], rs[:, 0:N], ALU.mult, ALU.mult, [xk, "A1", rk], [tk])
                        k.act(H[:, kc, 0:N], tb[:, 0:N], AF.Identity, [tk, "mods"], [hk], bias=mods[:, 0, kc, wsel:wsel + 1])

                load_x(0)
                load_x(1)
                norm_block(0)
                for bi, (t0, N, kind) in enumerate(blocks):
                    H = hb[bi % 2]
                    hk = "hb%d" % (bi % 2)
                    if bi + 1 < len(blocks):
                        norm_block(bi + 1)
                    if bi + 2 < len(blocks):
                        load_x(bi + 2)

                    def proj(col0, evac):
                        pi = cnt["ps"] % 4
                        cnt["ps"] += 1
                        for kc in range(KC):
                            k.mm(ps[pi][:, 0:N], win[:, kc, col0:col0 + 128], H[:, kc, 0:N], kc == 0, kc == KC - 1, ["win", hk], [pk[pi]])
                        evac(ps[pi][:, 0:N], pk[pi])

                    groups = [0, 1, 2] if kind == 0 else [1, 2]
                    for g in groups:
                        si = cnt["F"] % 2
                        cnt["F"] += 1
                        st = stF[si]
                        sk = "stF%d" % si
                        for c in range(4):
                            eng = "act" if c % 2 == 0 else "dve"
                            proj((g * 4 + c) * 128, lambda p_, pk_, c=c, eng=eng: k.cp(eng, st[:, c, 0:N], p_, [pk_], [sk]))
                        k.dma("sp", qkvT_s[g * 512:(g + 1) * 512, t0:t0 + N].rearrange("(c p) t -> p c t", p=128), st[:, :, 0:N], [sk], ["qkvT_s"])
                    si = cnt["B"] % 2
                    cnt["B"] += 1
                    st = stB[si]
                    sk = "stB%d" % si
                    for c in range(4):
                        eng = "act" if c % 2 == 0 else "dve"
                        proj(2064 + c * 128, lambda p_, pk_, c=c, eng=eng: k.cp(eng, st[:, c, 0:N], p_, [pk_], [sk]))
                    k.dma("sp", uT_s[:, t0:t0 + N].rearrange("(c p) t -> p c t", p=128), st[:, :, 0:N], [sk], ["uT_s"])
                    if kind == 0:
                        for gg in range(4):
                            si = cnt["B"] % 2
                            cnt["B"] += 1
                            st = stB[si]
                            sk = "stB%d" % si
                            for c in range(4):
                                proj(2576 + (gg * 4 + c) * 128, lambda p_, pk_, c=c: k.act(st[:, c, 0:N], p_, AF.Sigmoid, [pk_], [sk]))
                            k.dma("sp", gT_s[gg * 512:(gg + 1) * 512, t0:t0 + N].rearrange("(c p) t -> p c t", p=128), st[:, :, 0:N], [sk], ["gT_s"])
                    for ti in range(N // 128):
                        tsl = slice(ti * 128, (ti + 1) * 128)
                        if kind == 0:
                            for kc in range(KC):
                                k.mm(ps[4][:, :], H[:, kc, tsl], win[:, kc, 1536:2048], kc == 0, kc == KC - 1, [hk, "win"], ["ps4"])
                            si = cnt["Z"] % 2
                            cnt["Z"] += 1
                            k.act(stZ[si][:], ps[4][:, :], AF.Silu, ["ps4"], ["stZ%d" % si])
                            k.dma("sp", z_s[t0 + ti * 128:t0 + (ti + 1) * 128, :], stZ[si][:], ["stZ%d" % si], ["z_s"])
                        for kc in range(KC):
                            k.mm(ps[5][:, 0:16], H[:, kc, tsl], win[:, kc, 2048:2064], kc == 0, kc == KC - 1, [hk, "win"], ["ps5"])
                        si = cnt["A"] % 2
                        cnt["A"] += 1
                        k.cp("dve", stA[si][:], ps[5][:, 0:16], ["ps5"], ["stA%d" % si])
                        k.dma("sp", ba_s[t0 + ti * 128:t0 + (ti + 1) * 128, :], stA[si][:], ["stA%d" % si], ["ba_s"])
            S.barrier()
        p01.close()

        if upto >= 2 and not skip2:
            phase2(nc, S, k, sb, ps, pk, psall, cst, ident_b, ones_b, dict(qkvT_s=qkvT_s, ba_s=ba_s, o_s=o_s, convw=convw, gpar=gpar, dbg_s=dbg_s, dump=dbgdump, dn_limit=dn_limit))
            S.barrier()
        if upto >= 3 and not skip3:
            phase3(nc, S, k, sb, ps, pk, psall, cst, ident_b, dict(uT_s=uT_s, ygT_s=ygT_s, s5p=s5p, s5b=s5b, s5c=s5c, s5d=s5d, dump=dbgdump, y_dbg=y_dbg))
            S.barrier()
        if upto >= 4:
            d4 = dict(xT=xT, o_s=o_s, z_s=z_s, gT_s=gT_s, ygT_s=ygT_s, xl1_s=xl1_s, yT=yT, act_s=act_s, dump=dbgdump, w_a_out=w_a_out, w_glu=w_glu,
                      w_b_out=w_b_out, w_o=w_o, dnw=dnw, b_glu=b_glu, fcw=fcw, w_up=w_up, w_down=w_down, stop_after=stop_after)
            phase4(nc, S, k, sb, ps, pk, psall, cst, ident_b, ones_b, mods, A2, nwt, d4)
            S.barrier()
        S.finish(["qkvT_s", "z_s", "ba_s", "uT_s", "gT_s", "o_s", "xl1_s", "dbg_s", "yT", "ygT_s", "y_dbg", "act_s"] + dumpkeys)
    return nc


def phase2(nc, S, k, sb, ps, pk, psall, cst, ident_b, ones_b, dr):
    qkvT_s, ba_s, o_s, convw, gpar = dr["qkvT_s"], dr["ba_s"], dr["o_s"], dr["convw"], dr["gpar"]
    ident_f = cst[:, 0, :]
    ones_f = cst[:, 7, :]
    with ExitStack() as p2:
        khT = sb("khT", [128, 4, TT], BF16, stack=p2)
        qhT = sb("qhT", [128, 4, OWN], BF16, stack=p2)
        ktok = sb("ktok", [128, NT, 512], BF16, stack=p2)
        vtok = sb("vtok", [128, NT, 512], BF16, stack=p2)
        cw = sb("cw", [128, 12, 3], stack=p2)
        k.dma("sp", cw[:], convw[:, :, :], [], ["cw"])
        with ExitStack() as pa:
            sx = [sb("sx%d" % i, [128, 4, 514], stack=pa) for i in range(2)]
            acc = [sb("acc%d" % i, [128, 512], stack=pa) for i in range(4)]
            sil = [sb("sil%d" % i, [128, 512], stack=pa) for i in range(4)]
            sq = [sb("sq%d" % i, [128, 512], BF16, stack=pa) for i in range(4)]
            nrm = [sb("nrm%d" % i, [128, 512], stack=pa) for i in range(4)]
            vT = sb("vT", [128, 4, 512], BF16, stack=pa)
            blocks = [(i * 512, 512, 0, T) for i in range(8)] + [(T, 256, T, TT)]
            n = 0
            ci = 0
            for (t0, N, slo, shi) in blocks:
                nq = max(0, min(N, OWN - t0)) if t0 < T else 0
                for g in range(3):
                    if g == 0 and nq == 0:
                        continue
                    Ng = nq if g == 0 else N
                    st = sx[n % 2]
                    sk = "sx%d" % (n % 2)
                    n += 1
                    k.memset("pool", st[:, :, 0:1], 0.0, [sk])
                    k.memset("pool", st[:, :, Ng + 1:Ng + 2], 0.0, [sk])
                    lo = max(t0 - 1, slo)
                    hi = min(t0 + Ng + 1, OWN if g == 0 else shi)
                    k.dma("sp", st[:, :, lo - (t0 - 1):hi - (t0 - 1)],
                          qkvT_s[g * 512:(g + 1) * 512, lo:hi].rearrange("(c p) t -> p c t", p=128), ["qkvT_s"], [sk])
                    for stage in range(9):
                        for c in range(4):
                            a, ak = acc[c], "acc%d" % c
                            sl, slk = sil[c], "sil%d" % c
                            sqb, sqk = sq[c], "sq%d" % c
                            nr, nk = nrm[c], "nrm%d" % c
                            pi = 2 + c
                            gc = g * 4 + c
                            if stage == 0:
                                k.act(a[:, 0:Ng], st[:, c, 1:Ng + 1], AF.Copy, [sk, "cw"], [ak], scale=cw[:, gc, 1:2])
                            elif stage == 1:
                                k.stt(a[:, 0:Ng], st[:, c, 0:Ng], cw[:, gc, 0:1], a[:, 0:Ng], ALU.mult, ALU.add, [sk, "cw", ak], [ak])
                            elif stage == 2:
                                k.stt(a[:, 0:Ng], st[:, c, 2:Ng + 2], cw[:, gc, 2:3], a[:, 0:Ng], ALU.mult, ALU.add, [sk, "cw", ak], [ak])
                            elif stage == 3:
                                if g == 2:
                                    k.act(vT[:, c, 0:Ng], a[:, 0:Ng], AF.Silu, [ak], ["vT%d" % c])
                                else:
                                    k.act(sl[:, 0:Ng], a[:, 0:Ng], AF.Silu, [ak], [slk])
                            elif g == 2:
                                continue
                            elif stage == 4:
                                k.act(sqb[:, 0:Ng], sl[:, 0:Ng], AF.Square, [slk], [sqk])
                            elif stage == 5:
                                k.mm(ps[pi][:, 0:Ng], ones_b, sqb[:, 0:Ng], True, True, ["cstb", sqk], [pk[pi]])
                            elif stage == 6:
                                k.act(nr[:, 0:Ng], ps[pi][:, 0:Ng], AF.Sqrt, [pk[pi]], [nk], bias=1e-6)
                            elif stage == 7:
                                k.recip(nr[:, 0:Ng], nr[:, 0:Ng], [nk], [nk])
                            elif stage == 8:
                                if g == 1:
                                    k.tt("dve", khT[:, c, t0:t0 + Ng], sl[:, 0:Ng], nr[:, 0:Ng], ALU.mult, [slk, nk], ["khT%d" % c])
                                else:
                                    k.stt(qhT[:, c, t0:t0 + Ng], sl[:, 0:Ng], 128.0 ** -0.5, nr[:, 0:Ng], ALU.mult, ALU.mult, [slk, nk], ["qhT"])
                    if g >= 1:
                        for ti in range(Ng // 128):
                            tile_i = (t0 + ti * 128) // 128
                            pT = ps[6].bitcast(BF16)
                            for c in range(4):
                                src = khT[:, c, t0 + ti * 128:t0 + (ti + 1) * 128] if g == 1 else vT[:, c, ti * 128:(ti + 1) * 128]
                                k.tr(pT[:, c * 128:(c + 1) * 128], src, ident_b, [("khT%d" if g == 1 else "vT%d") % c, "cstb"], ["ps6"])
                            dst = ktok if g == 1 else vtok
                            k.cp("act", dst[:, tile_i, :], pT[:, 0:512], ["ps6"], ["ktok" if g == 1 else "vtok"])
        S.barrier()
        dump = dr["dump"]
        dump("khT", khT[:], ["khT"], [128, 4, TT], BF16)
        dump("qhT", qhT[:], ["qhT"], [128, 4, OWN], BF16)
        dump("ktok", ktok[:], ["ktok"], [128, NT, 512], BF16)
        dump("vtok", vtok[:], ["vtok"], [128, NT, 512], BF16)
        with ExitStack() as pb:
            ba = sb("ba", [128, NT, 16], stack=pb)
            gp = sb("gp", [128, 2, NT * 8], stack=pb)
            k.dma("sp", ba[:], ba_s.rearrange("(n p) c -> p n c", p=128), ["ba_s"], ["ba"])
            k.dma("sp", gp[:], gpar[:, :, :], [], ["gp"])
            nbeta = sb("nbeta", [128, NT, 8], stack=pb)
            graw = sb("graw", [128, NT, 8], stack=pb)
            gcol = sb("gcol", [128, NT, 8], stack=pb)
            eg = sb("eg", [128, NT, 8], stack=pb)
            t1 = sb("t1", [128, NT, 8], stack=pb)
            t2 = sb("t2", [128, NT, 8], stack=pb)
            k.act(t1[:], ba[:, :, 0:8], AF.Sigmoid, ["ba"], ["t1"])
            k.ts("pool", nbeta[:], t1[:], -1.0, ALU.mult, ["t1"], ["nbeta"])
            gpv = gp[:].rearrange("p a (n c) -> p a n c", c=8)
            k.tt("dve", t2[:], ba[:, :, 8:16], gpv[:, 1, :, :], ALU.add, ["ba", "gp"], ["t2"])
            k.act(t2[:], t2[:], AF.Exp, ["t2"], ["t2"])
            k.act(t2[:], t2[:], AF.Ln, ["t2"], ["t2"], bias=1.0)
            k.act(t1[:], gpv[:, 0, :, :], AF.Exp, ["gp", "nbeta"], ["t1"])
            k.stt(graw[:], t2[:], -1.0, t1[:], ALU.mult, ALU.mult, ["t1", "t2"], ["graw"])
            for ld in range(2):
                k.mm(ps[0][:, 0:NT * 4].rearrange("p (n c) -> p n c", c=4), cst[:, 1 + ld, :], graw[:, :, ld * 4:(ld + 1) * 4], True, True, ["cst", "graw"], ["ps0"])
                k.cp("dve", gcol[:, :, ld * 4:(ld + 1) * 4], ps[0][:, 0:NT * 4].rearrange("p (n c) -> p n c", c=4), ["ps0"], ["gcol"])
            k.act(eg[:], gcol[:], AF.Exp, ["gcol"], ["eg"])
            if dr.get("dbg_s") is not None:
                k.dma("sp", dr["dbg_s"][:, 128:128 + NT * 8], graw[:].rearrange("p n c -> p (n c)"), ["graw"], ["dbg_s"])
                k.dma("sp", dr["dbg_s"][:, 512:512 + NT * 8], gcol[:].rearrange("p n c -> p (n c)"), ["gcol"], ["dbg_s"])
                k.dma("sp", dr["dbg_s"][:, 1024:1024 + NT * 8], nbeta[:].rearrange("p n c -> p (n c)"), ["nbeta"], ["dbg_s"])

            R = 3
            TTb = [sb("TTb%d" % i, [128, 4, 128], BF16, stack=pb) for i in range(R)]
            ATb = [sb("ATb%d" % i, [128, 4, 128], BF16, stack=pb) for i in range(R)]
            kdb = [sb("kdb%d" % i, [128, 4, 128], BF16, stack=pb) for i in range(R)]
            egl = [sb("egl%d" % i, [128, 4], stack=pb) for i in range(R)]
            Dg = sb("Dg", [128, 4, 128], stack=pb)
            dT = sb("dT", [128, 4, 128], stack=pb)
            d2 = sb("d2", [128, 4, 128], stack=pb)
            decT = sb("decT", [128, 4, 128], stack=pb)
            decS = sb("decS", [128, 4, 128], stack=pb)
            XT32 = sb("XT32", [128, 4, 128], stack=pb)
            XAs = [sb("XA%d" % i, [128, 4, 256], BF16, stack=pb) for i in range(2)]
            XBs = [sb("XB%d" % i, [128, 4, 256], BF16, stack=pb) for i in range(2)]
            Mb = [sb("Mb%d" % i, [128, 4, 128], BF16, stack=pb) for i in range(1)]
            NT0b = sb("NT0b", [128, 4, 128], BF16, stack=pb)
            No1 = sb("No1", [128, 4, 128], BF16, stack=pb)
            No1T = sb("No1T", [128, 4, 128], BF16, stack=pb)
            No2 = sb("No2", [128, 4, 128], BF16, stack=pb)
            Pb = sb("Pb", [128, 4, 128], BF16, stack=pb)
            Qb = sb("Qb", [128, 4, 128], BF16, stack=pb)
            msk = sb("msk", [128, 3, 4, 128], BF16, stack=pb)
            for mi in range(3):
                for h in range(4):
                    k.cp("pool", msk[:, mi, h, :], cst[:, 8 + mi, :], ["cst"], ["msk"])
            idr = sb("idr", [128, 4, 128], stack=pb)
            dd = sb("dd", [128, 4], stack=pb)
            edl = sb("edl", [128, 4], stack=pb)
            for h in range(4):
                k.cp("pool", idr[:, h, :], ident_f, ["cst"], ["idr"])
            S32 = [sb("S32_%d" % i, [128, 4, 128], stack=pb) for i in range(2)]
            Sbf = [sb("Sbf_%d" % i, [128, 4, 128], BF16, stack=pb) for i in range(2)]
            for i in range(2):
                k.memset("pool", S32[i][:], 0.0, ["S32_%d_%d" % (i, h) for h in range(4)])
                k.memset("pool", Sbf[i][:], 0.0, ["Sbf_%d" % i])
            rbf = sb("rbf", [128, 4, 128], BF16, stack=pb)
            vnb = sb("vnb", [128, 4, 128], BF16, stack=pb)
            oq = sb("oq", [128, 4, 128], stack=pb)
            ost = [sb("ost%d" % i, [128, 512], stack=pb) for i in range(2)]
            psA = psall[:, 2:4, :].rearrange("p b (h x) -> p (b h) x", x=256)
            psB = psall[:, 4:6, :].rearrange("p b (h x) -> p (b h) x", x=256)

            order = {1: [33, 32] + list(range(31, -1, -1)), 0: [32, 33] + list(range(0, 17))}
            state = {"slot": 0, "ost": 0}

            def tcols(tl):
                t0 = tl * 128
                return slice(t0, t0 + 128)

            def pre_front(ld, tl, slot):
                own = tl < OWNT
                sfx = "%d" % slot
                last = 127 if ld == 0 else 0
                for h in range(4):
                    k.act(Dg[:, h, :], cst[:, 1 + ld, :], AF.Copy, ["cst", "graw"], ["Dg"], scale=graw[:, tl, ld * 4 + h:ld * 4 + h + 1])
                k.mm(ps[0][:, :], ones_f, Dg[:].rearrange("p h x -> p (h x)"), True, True, ["cst", "Dg"], ["ps0"])
                g0 = ps[0].rearrange("p (h x) -> p h x", x=128)
                for h in range(4):
                    gc = gcol[:, tl, ld * 4 + h:ld * 4 + h + 1]
                    k.stt(dT[:, h, :], g0[:, h, :], gc, cst[:, 3 + ld, :], ALU.subtract, ALU.add, ["ps0", "gcol", "cst"], ["dT"])
                    k.stt(d2[:, h, :], g0[:, h, :], gc, cst[:, 5 + ld, :], ALU.subtract, ALU.add, ["ps0", "gcol", "cst"], ["d2"])
                yield
                k.act(decT[:], dT[:], AF.Exp, ["dT"], ["decT"])
                k.act(decS[:], d2[:], AF.Exp, ["d2"], ["decS"], scale=-1.0)
                gl4 = ps[0][:, last::128]
                k.tt("dve", dd[:], gl4, gcol[:, tl, ld * 4:(ld + 1) * 4], ALU.subtract, ["ps0", "gcol"], ["dd"])
                k.act(edl[:], dd[:], AF.Exp, ["dd"], ["edl"])
                k.act(egl[slot][:], gl4, AF.Exp, ["ps0"], ["egl" + sfx])
                for h in range(4):
                    k.act(kdb[slot][:, h, :], ktok[:, tl, h * 128:(h + 1) * 128], AF.Copy, ["ktok", "edl"], ["kdb" + sfx], scale=edl[:, h:h + 1])
                yield
                k1 = ps[1].rearrange("p (h x) -> p h x", x=128)
                for h in range(4):
                    k.mm(k1[:, h, :], khT[:, h, tcols(tl)], khT[:, h, tcols(tl)], True, True, ["khT"], ["ps1"])
                M = Mb[0]
                for h in range(4):
                    k.stt(M[:, h, :], k1[:, h, :], nbeta[:, tl, ld * 4 + h:ld * 4 + h + 1], decS[:, h, :], ALU.mult, ALU.mult,
                          ["ps1", "nbeta", "decS"], ["Mb0"])
                if own:
                    for h in range(4):
                        k.mm(k1[:, h, :], khT[:, h, tcols(tl)], qhT[:, h, tcols(tl)], True, True, ["khT", "qhT"], ["ps1"])
                    k.tt("dve", ATb[slot][:], k1, decT[:], ALU.mult, ["ps1", "decT"], ["ATb" + sfx])
                yield
                pT = ps[0].bitcast(BF16)
                for h in range(4):
                    k.tr(pT[:, h * 128:(h + 1) * 128], M[:, h, :], ident_b, ["Mb0", "cstb"], ["ps0"])
                k.cp("act", NT0b[:], pT[:, 0:512].rearrange("p (h x) -> p h x", x=128), ["ps0"], ["NT0b"])
                yield

            def pre_back(ld, tl, slot, hh):
                sfx = "%d" % slot
                H = slice(2 * hh, 2 * hh + 2)
                hs = (2 * hh, 2 * hh + 1)
                M = Mb[0]
                XA, XB = XAs, XBs
                pA = psall[:, 2 + hh, :].rearrange("p (h x) -> p h x", x=256)
                pB = psall[:, 4 + hh, :].rearrange("p (h x) -> p h x", x=256)
                pAk, pBk = ["ps%d" % (2 + hh)], ["ps%d" % (4 + hh)]
                ka = lambda c: "XA%d_%d" % (c, hh)
                kb = lambda c: "XB%d_%d" % (c, hh)
                n1, n1t, n2, pbk, qbk = "No1_%d" % hh, "No1T_%d" % hh, "No2_%d" % hh, "Pb_%d" % hh, "Qb_%d" % hh
                k.tt("dve", XA[0][:, H, 128:256], NT0b[:, H], msk[:, 0, H], ALU.mult, ["NT0b", "msk"], [ka(0)])
                k.tt("dve", XB[0][:, H, 128:256], M[:, H], msk[:, 0, H], ALU.mult, ["Mb0", "msk"], [kb(0)])
                k.tt("dve", No1[:, H], M[:, H], msk[:, 1, H], ALU.mult, ["Mb0", "msk"], [n1])
                k.tt("dve", No1T[:, H], NT0b[:, H], msk[:, 1, H], ALU.mult, ["NT0b", "msk"], [n1t])
                k.tt("dve", No2[:, H], M[:, H], msk[:, 2, H], ALU.mult, ["Mb0", "msk"], [n2])
                yield
                k.tt("dve", XA[0][:, H, 0:128], XA[0][:, H, 128:256], idr[:, H], ALU.add, [ka(0), "idr"], [ka(0)])
                k.tt("dve", XA[1][:, H, 0:128], XA[0][:, H, 128:256], idr[:, H], ALU.add, [ka(0), "idr"], [ka(1)])
                k.tt("dve", XB[0][:, H, 0:128], XB[0][:, H, 128:256], idr[:, H], ALU.add, [kb(0), "idr"], [kb(0)])
                k.tt("dve", XB[1][:, H, 0:128], XB[0][:, H, 128:256], idr[:, H], ALU.add, [kb(0), "idr"], [kb(1)])
                yield
                cur = 0
                for lev in range(5):
                    a, b = XA[cur], XB[cur]
                    ak, bk = ka(cur), kb(cur)
                    nxt = 1 - cur
                    an, bn = XA[nxt], XB[nxt]
                    ank, bnk = ka(nxt), kb(nxt)
                    for hi_, h in enumerate(hs):
                        if lev == 0:
                            cs = slice(128, 256)
                        elif lev == 4:
                            cs = slice(0, 128)
                        else:
                            cs = slice(0, 256)
                        k.mm(pA[:, hi_, cs], b[:, h, 128:256], a[:, h, cs], True, True, [ak, bk], pAk)
                        k.mm(pB[:, hi_, cs], a[:, h, 128:256], b[:, h, cs], True, True, [ak, bk], pBk)
                    yield
                    if lev > 0:
                        k.tt("dve", an[:, H, 0:128], pA[:, :, 0:128], a[:, H, 0:128], ALU.add, pAk + [ak], [ank])
                        k.tt("dve", bn[:, H, 0:128], pB[:, :, 0:128], b[:, H, 0:128], ALU.add, pBk + [bk], [bnk])
                    if lev < 4:
                        k.cp("act", an[:, H, 128:256], pA[:, :, 128:256], pAk, [ank])
                        k.cp("act", bn[:, H, 128:256], pB[:, :, 128:256], pBk, [bnk])
                    cur = nxt
                    yield
                a, b = XA[cur], XB[cur]
                ak, bk = ka(cur), kb(cur)
                nxt = 1 - cur
                an, bn = XA[nxt], XB[nxt]
                ank, bnk = ka(nxt), kb(nxt)
                for hi_, h in enumerate(hs):
                    k.mm(pA[:, hi_, 0:128], No1[:, h, :], a[:, h, 0:128], True, True, [n1, ak], pAk)
                    k.mm(pB[:, hi_, 0:128], No1T[:, h, :], b[:, h, 0:128], True, True, [n1t, bk], pBk)
                yield
                k.cp("act", Pb[:, H], pA[:, :, 0:128], pAk, [pbk])
                k.cp("dve", Qb[:, H], pB[:, :, 0:128], pBk, [qbk])
                yield
                for hi_, h in enumerate(hs):
                    k.mm(pA[:, hi_, 128:256], b[:, h, 0:128], Pb[:, h, :], True, True, [bk, pbk], pAk)
                    k.mm(pB[:, hi_, 128:256], a[:, h, 0:128], Qb[:, h, :], True, True, [ak, qbk], pBk)
                yield
                k.tt("dve", an[:, H, 0:128], pA[:, :, 128:256], a[:, H, 0:128], ALU.add, pAk + [ak], [ank])
                k.tt("dve", bn[:, H, 0:128], pB[:, :, 128:256], b[:, H, 0:128], ALU.add, pBk + [bk], [bnk])
                cur = nxt
                yield
                a, b = XA[cur], XB[cur]
                ak, bk = ka(cur), kb(cur)
                for hi_, h in enumerate(hs):
                    k.mm(pA[:, hi_, 0:128], No2[:, h, :], a[:, h, 0:128], True, True, [n2, ak], pAk)
                yield
                k.cp("act", Pb[:, H], pA[:, :, 0:128], pAk, [pbk])
                yield
                for hi_, h in enumerate(hs):
                    k.mm(pA[:, hi_, 128:256], b[:, h, 0:128], Pb[:, h, :], True, True, [bk, pbk], pAk)
                yield
                xk = "XT32_%d" % hh
                k.tt("dve", XT32[:, H], pA[:, :, 128:256], a[:, H, 0:128], ALU.add, pAk + [ak], [xk])
                for h in hs:
                    k.act(TTb[slot][:, h, :], XT32[:, h, :], AF.Copy, [xk, "nbeta"], ["TTb" + sfx], scale=nbeta[:, tl, ld * 4 + h:ld * 4 + h + 1])
                yield

            def serial(ld, tl, slot):
                own = tl < OWNT
                sfx = "%d" % slot
                sk32, skb = "S32_%d" % ld, "Sbf_%d" % ld
                p5 = ps[6].rearrange("p (h x) -> p h x", x=128)
                p7 = ps[7].rearrange("p (h x) -> p h x", x=128)
                for h in range(4):
                    k.mm(p5[:, h, :], khT[:, h, tcols(tl)], Sbf[ld][:, h, :], True, True, ["khT", skb], ["ps6"])
                yield
                for h in range(4):
                    k.stt(rbf[:, h, :], p5[:, h, :], eg[:, tl, ld * 4 + h:ld * 4 + h + 1], vtok[:, tl, h * 128:(h + 1) * 128], ALU.mult, ALU.subtract,
                          ["ps6", "eg", "vtok"], ["rbf"])
                yield
                for h in range(4):
                    k.mm(p7[:, h, :], TTb[slot][:, h, :], rbf[:, h, :], True, True, ["TTb" + sfx, "rbf"], ["ps7"])
                if own:
                    for h in range(4):
                        k.mm(p5[:, h, :], qhT[:, h, tcols(tl)], Sbf[ld][:, h, :], True, True, ["qhT", skb], ["ps6"])
                yield
                k.cp("act", vnb[:], p7, ["ps7"], ["vnb"])
                if own:
                    for h in range(4):
                        k.act(oq[:, h, :], p5[:, h, :], AF.Copy, ["ps6", "eg"], ["oq"], scale=eg[:, tl, ld * 4 + h:ld * 4 + h + 1])
                yield
                for h in range(4):
                    k.mm(p7[:, h, :], kdb[slot][:, h, :], vnb[:, h, :], True, True, ["kdb" + sfx, "vnb"], ["ps7"])
                if own:
                    for h in range(4):
                        k.mm(p5[:, h, :], ATb[slot][:, h, :], vnb[:, h, :], True, True, ["ATb" + sfx, "vnb"], ["ps6"])
                yield
                for h in range(4):
                    k.stt(S32[ld][:, h, :], S32[ld][:, h, :], egl[slot][:, h:h + 1], p7[:, h, :], ALU.mult, ALU.add,
                          ["%s_%d" % (sk32, h), "egl" + sfx, "ps7"], ["%s_%d" % (sk32, h)])
                yield
                k.cp("act", Sbf[ld][:], S32[ld][:], ["%s_%d" % (sk32, h) for h in range(4)], [skb])
                if own:
                    oi = state["ost"] % 2
                    state["ost"] += 1
                    k.tt("dve", ost[oi][:].rearrange("p (h x) -> p h x", x=128), p5, oq[:], ALU.add, ["ps6", "oq"], ["ost%d" % oi])
                    k.dma("sp", o_s[ld, tl * 128:(tl + 1) * 128, :], ost[oi][:], ["ost%d" % oi], ["o_s"])
                yield

            steps = []
            for i in range(34):
                steps.append((1, order[1][i]))
                if i < len(order[0]):
                    steps.append((0, order[0][i]))
            lim = dr.get("dn_limit")
            if lim:
                steps = steps[:lim]
            def runall(gens):
                gens = [g_ for g_ in gens if g_ is not None]
                while gens:
                    nxt_ = []
                    for g_ in gens:
                        if next(g_, "done") != "done":
                            nxt_.append(g_)
                    gens = nxt_

            nst = len(steps)
            runall([pre_front(steps[0][0], steps[0][1], 0)])
            for i in range(nst):
                ld, tl = steps[i]
                b0, b1 = pre_back(ld, tl, i % R, 0), pre_back(ld, tl, i % R, 1)
                fr = pre_front(steps[i + 1][0], steps[i + 1][1], (i + 1) % R) if i + 1 < nst else None
                se = serial(steps[i - 1][0], steps[i - 1][1], (i - 1) % R) if i >= 1 else None
                rnd = 0
                while b0 is not None or b1 is not None or fr is not None or se is not None:
                    if se is not None:
                        if next(se, "done") == "done":
                            se = None
                    if b0 is not None and next(b0, "done") == "done":
                        b0 = None
                    if b1 is not None and next(b1, "done") == "done":
                        b1 = None
                    backs_done = b0 is None and b1 is None
                    if fr is not None and (rnd % 2 == 1 or backs_done):
                        if next(fr, "done") == "done":
                            fr = None
                    rnd += 1
            runall([serial(steps[nst - 1][0], steps[nst - 1][1], (nst - 1) % R)])


def phase3(nc, S, k, sb, ps, pk, psall, cst, ident_b, dr):
    uT_s, ygT_s, s5p, s5b, s5c, s5d = dr["uT_s"], dr["ygT_s"], dr["s5p"], dr["s5b"], dr["s5c"], dr["s5d"]
    dump = dr["dump"]
    W = 1024

    def subkeys(base):
        return [base + ":%d" % i for i in range(16)] + [base + ":f%d" % i for i in range(8)]
    with ExitStack() as p3:
        uTcs = [sb("uTc%d" % i, [128, TT], BF16, stack=p3) for i in range(2)]
        CCb = sb("CCb", [128, 64, 128], BF16, stack=p3)
        for i in range(8):
            k.load_cast(CCb[:, i * 8:(i + 1) * 8, :].rearrange("p a b -> p (a b)"), s5c[:, i * 1024:(i + 1) * 1024], 1024, [], ["CCb"])
        for ld in range(2):
            v = CCb[:, ld * 32 + 16:ld * 32 + 32, :]
            k.ts("pool", v, v, -1.0, ALU.mult, ["CCb"], ["CCb"])
        dsk = sb("dsk", [128, 4], stack=p3)
        k.dma("sp", dsk[:], s5d[:, :], [], ["dsk"])
        LZ = sb("LZ", [128, 16, 128], BF16, stack=p3)
        LZb = sb("LZb", [128, 16, 128], BF16, stack=p3)
        P1 = sb("P1", [128, 3, 8, 32], stack=p3)
        PW8 = sb("PW8", [128, 3, 8, 32], stack=p3)
        PW64 = sb("PW64", [128, 3, 4, 32], stack=p3)
        PWK = sb("PWK", [128, 3, 7, 32], stack=p3)
        WKS = [[sb("WKS%d%d" % (d_, i_), [128, 2, 196], stack=p3) for i_ in range(2)] for d_ in range(2)]
        for d_ in range(2):
            for i_ in range(2):
                k.memset("pool", WKS[d_][i_][:], 0.0, ["WKS%d%d" % (d_, i_)])
        with ExitStack() as pp:
            prm = sb("prm", [128, 3, W], stack=pp)
            bb = sb("bb", [128, 2, W], stack=pp)
            k.dma("sp", prm[:], s5p[:, :, :], [], ["prm"])
            k.dma("sp", bb[:], s5b[:, :, :], [], ["bb"])
            names = ["dt", "mag", "th", "c", "s", "t1", "t2", "t3", "lr", "li", "fr", "fi"]
            A = {n_: sb("w_" + n_, [128, W], stack=pp) for n_ in names}
            key = lambda n_: "w_" + n_
            are, aim = prm[:, 0, :], prm[:, 1, :]
            k.act(A["dt"][:], prm[:, 2, :], AF.Exp, ["prm"], [key("dt")])
            k.tt("dve", A["mag"][:], are, A["dt"][:], ALU.mult, ["prm", key("dt")], [key("mag")])
            k.act(A["mag"][:], A["mag"][:], AF.Exp, [key("mag")], [key("mag")])
            k.tt("dve", A["th"][:], aim, A["dt"][:], ALU.mult, ["prm", key("dt")], [key("th")])
            k.act(A["s"][:], A["th"][:], AF.Sin, [key("th")], [key("s")], scale=0.125)
            hp = sb("hp", [128, 1], stack=pp)
            k.memset("pool", hp[:], float(np.pi / 2), ["hp"])
            k.act(A["c"][:], A["th"][:], AF.Sin, [key("th"), "hp"], [key("c")], scale=-0.125, bias=hp[:, 0:1])
            for it in range(3):
                k.tt("dve", A["t1"][:], A["c"][:], A["c"][:], ALU.mult, [key("c")], [key("t1")])
                k.tt("dve", A["t2"][:], A["s"][:], A["s"][:], ALU.mult, [key("s")], [key("t2")])
                k.tt("dve", A["t3"][:], A["c"][:], A["s"][:], ALU.mult, [key("c"), key("s")], [key("t3")])
                k.tt("dve", A["c"][:], A["t1"][:], A["t2"][:], ALU.subtract, [key("t1"), key("t2")], [key("c")])
                k.ts("dve", A["s"][:], A["t3"][:], 2.0, ALU.mult, [key("t3")], [key("s")])
            k.tt("dve", A["lr"][:], A["mag"][:], A["c"][:], ALU.mult, [key("mag"), key("c")], [key("lr")])
            k.tt("dve", A["li"][:], A["mag"][:], A["s"][:], ALU.mult, [key("mag"), key("s")], [key("li")])
            k.tt("dve", A["t1"][:], are, are, ALU.mult, ["prm"], [key("t1")])
            k.tt("dve", A["t2"][:], aim, aim, ALU.mult, ["prm"], [key("t2")])
            k.tt("dve", A["t1"][:], A["t1"][:], A["t2"][:], ALU.add, [key("t1"), key("t2")], [key("t1")])
            k.recip(A["t1"][:], A["t1"][:], [key("t1")], [key("t1")])
            k.ts("dve", A["t2"][:], A["lr"][:], -1.0, ALU.add, [key("lr")], [key("t2")])
            k.tt("dve", A["fr"][:], A["t2"][:], are, ALU.mult, [key("t2"), "prm"], [key("fr")])
            k.tt("dve", A["t3"][:], A["li"][:], aim, ALU.mult, [key("li"), "prm"], [key("t3")])
            k.tt("dve", A["fr"][:], A["fr"][:], A["t3"][:], ALU.add, [key("fr"), key("t3")], [key("fr")])
            k.tt("dve", A["fr"][:], A["fr"][:], A["t1"][:], ALU.mult, [key("fr"), key("t1")], [key("fr")])
            k.tt("dve", A["fi"][:], A["li"][:], are, ALU.mult, [key("li"), "prm"], [key("fi")])
            k.tt("dve", A["t3"][:], A["t2"][:], aim, ALU.mult, [key("t2"), "prm"], [key("t3")])
            k.tt("dve", A["fi"][:], A["fi"][:], A["t3"][:], ALU.subtract, [key("fi"), key("t3")], [key("fi")])
            k.tt("dve", A["fi"][:], A["fi"][:], A["t1"][:], ALU.mult, [key("fi"), key("t1")], [key("fi")])
            BB = sb("BB", [128, 2, W], BF16, stack=pp)
            k.tt("dve", A["t1"][:], A["fr"][:], bb[:, 0, :], ALU.mult, [key("fr"), "bb"], [key("t1")])
            k.tt("dve", A["t2"][:], A["fi"][:], bb[:, 1, :], ALU.mult, [key("fi"), "bb"], [key("t2")])
            k.tt("dve", BB[:, 0, :], A["t1"][:], A["t2"][:], ALU.subtract, [key("t1"), key("t2")], ["BB"])
            k.tt("dve", A["t1"][:], A["fr"][:], bb[:, 1, :], ALU.mult, [key("fr"), "bb"], [key("t1")])
            k.tt("dve", A["t2"][:], A["fi"][:], bb[:, 0, :], ALU.mult, [key("fi"), "bb"], [key("t2")])
            k.tt("dve", BB[:, 1, :], A["t1"][:], A["t2"][:], ALU.add, [key("t1"), key("t2")], ["BB"])
            pT = ps[7].bitcast(BF16)
            for ld in range(2):
                for ri in range(2):
                    for cu in range(4):
                        idx = (ld * 2 + ri) * 4 + cu
                        c0 = ld * 512 + cu * 128
                        k.tr(pT[:, (idx % 4) * 128:(idx % 4 + 1) * 128], BB[:, ri, c0:c0 + 128], ident_b, ["BB", "cstb"], ["ps7"])
                        if idx % 4 == 3:
                            k.cp("act", LZ[:, idx - 3:idx + 1, :], pT[:, 0:512].rearrange("p (a x) -> p a x", x=128), ["ps7"], ["LZ"])
            lrv = A["lr"][:].rearrange("p (a r) -> p a r", r=32)[:, :, 0]
            liv = A["li"][:].rearrange("p (a r) -> p a r", r=32)[:, :, 0]
            k.cp("dve", P1[:, 0, 0, :], lrv, [key("lr")], ["P1"])
            k.cp("dve", P1[:, 1, 0, :], liv, [key("li")], ["P1"])
            q1 = sb("q1", [128, 4, 32], stack=pp)

            def cmul(dst, m, a_r, a_i, b_r, b_i, dk):
                k.tt("dve", q1[:, 0, :], a_r, b_r, ALU.mult, [dk], ["q1"])
                k.tt("dve", q1[:, 1, :], a_i, b_i, ALU.mult, [dk], ["q1"])
                k.tt("dve", q1[:, 2, :], a_r, b_i, ALU.mult, [dk], ["q1"])
                k.tt("dve", q1[:, 3, :], a_i, b_r, ALU.mult, [dk], ["q1"])
                k.tt("dve", dst[:, 0, m, :], q1[:, 0, :], q1[:, 1, :], ALU.subtract, ["q1"], [dk])
                k.tt("dve", dst[:, 1, m, :], q1[:, 2, :], q1[:, 3, :], ALU.add, ["q1"], [dk])
            for m in range(1, 8):
                cmul(P1, m, P1[:, 0, m - 1, :], P1[:, 1, m - 1, :], P1[:, 0, 0, :], P1[:, 1, 0, :], "P1")
            k.cp("dve", PW8[:, 0, 0, :], P1[:, 0, 7, :], ["P1"], ["P16"])
            k.cp("dve", PW8[:, 1, 0, :], P1[:, 1, 7, :], ["P1"], ["P16"])
            for m in range(1, 8):
                cmul(PW8, m, PW8[:, 0, m - 1, :], PW8[:, 1, m - 1, :], PW8[:, 0, 0, :], PW8[:, 1, 0, :], "P16")
            k.cp("dve", PW64[:, 0, 0, :], PW8[:, 0, 7, :], ["P16"], ["P16"])
            k.cp("dve", PW64[:, 1, 0, :], PW8[:, 1, 7, :], ["P16"], ["P16"])
            for m in range(1, 4):
                cmul(PW64, m, PW64[:, 0, m - 1, :], PW64[:, 1, m - 1, :], PW64[:, 0, 0, :], PW64[:, 1, 0, :], "P16")
            for kk_, src_m in ((0, 0), (1, 1), (2, 3)):
                k.cp("dve", PWK[:, 0, kk_, :], PW64[:, 0, src_m, :], ["P16"], ["P16"])
                k.cp("dve", PWK[:, 1, kk_, :], PW64[:, 1, src_m, :], ["P16"], ["P16"])
            for kk_ in range(3, 7):
                cmul(PWK, kk_, PWK[:, 0, kk_ - 1, :], PWK[:, 1, kk_ - 1, :], PWK[:, 0, kk_ - 1, :], PWK[:, 1, kk_ - 1, :], "P16")
            k.ts("dve", PWK[:, 2, :, :], PWK[:, 1, :, :], -1.0, ALU.mult, ["P16"], ["P16"])
            k.ts("dve", PW8[:, 2, :, :], PW8[:, 1, :, :], -1.0, ALU.mult, ["P16"], ["P16"])
            k.ts("dve", PW64[:, 2, :, :], PW64[:, 1, :, :], -1.0, ALU.mult, ["P16"], ["P16"])
            k.ts("dve", P1[:, 2, :, :], P1[:, 1, :, :], -1.0, ALU.mult, ["P1"], ["P1"])
            k.cp("pool", LZb[:], LZ[:], ["LZ"], ["LZb"])
            k.memset("pool", LZb[64:96, :, :], 0.0, ["LZb"])
            dump("LZ", LZ[:], ["LZ"], [128, 16, 128], BF16)
        S.barrier()
        L1, L0 = TT, 2560
        Z1cs = [sb("Z1c%d" % a, [128, 2, L1], stack=p3) for a in range(2)]
        Z0cs = [sb("Z0c%d" % a, [128, 2, L0], stack=p3) for a in range(2)]
        Xb = [[sb("Xb%d%d" % (ld, ri), [128, OWN], BF16, stack=p3) for ri in range(2)] for ld in range(2)]
        yv = sb("yv", [128, 512], stack=p3)
        Gs = [sb("G%d" % i, [128, 4, OWN // 8], BF16, stack=p3) for i in range(2)]
        gcnt = {"n": 0}
        gw = [sb("gw%d" % i, [128, 512], stack=p3) for i in range(2)]
        ygs = [sb("ygs%d" % i, [128, 512], BF16, stack=p3) for i in range(2)]
        for a in range(2):
            k.memset("pool", Z0cs[a][:, :, 0:128], 0.0, subkeys("Z0%d" % a))
        oblocks = [(0, 512), (512, 512), (1024, 512), (1536, 512), (2048, 128)]
        zcnt = {"n": 0}

        def cmadd(dr_, di_, sr_, si_, pr, pi, npi, kr, ki, cd, cs):
            rd = [kr + ":%d" % cs, ki + ":%d" % cs, "P1", "P16"]
            wr_, wi_ = [kr + ":%d" % cd], [ki + ":%d" % cd]
            k.stt(dr_, sr_, pr, dr_, ALU.mult, ALU.add, rd + wr_, wr_)
            k.stt(di_, si_, pr, di_, ALU.mult, ALU.add, rd + wi_, wi_)
            k.stt(dr_, si_, npi, dr_, ALU.mult, ALU.add, rd + wr_, wr_)
            k.stt(di_, sr_, pi, di_, ALU.mult, ALU.add, rd + wi_, wi_)

        def scan(Zc, kz, L, rev, col):
            ns = L // 256
            Z5 = Zc[:].rearrange("p r (s b a i) -> p r s b a i", b=4, a=8, i=8)
            Z3 = Zc[:].rearrange("p r (q i) -> p r q i", i=8)
            Z2 = Zc[:].rearrange("p r (q a i) -> p r q a i", a=8, i=8)
            e8 = 0 if rev else 7
            e4 = 0 if rev else 3

            def sc(P, m):
                return P[:, 0, m, col:col + 1], P[:, 1, m, col:col + 1], P[:, 2, m, col:col + 1]

            def cm(dv, sv, pw, cd, cs_, wtag=None):
                pr, pi, npi = pw
                rd = [kz + ":%d" % c for c in (cs_, cs_ + 8)] + ["P1", "P16"]
                wr_ = [kz + ":%d" % c for c in (cd, cd + 8)] if wtag is None else [kz + ":f%d" % wtag]
                k.stt(dv, sv, pr, dv, ALU.mult, ALU.add, rd + wr_, wr_)
                yield
                k.stt(dv[:, 0], sv[:, 1], npi, dv[:, 0], ALU.mult, ALU.add, rd + wr_, wr_)
                k.stt(dv[:, 1], sv[:, 0], pi, dv[:, 1], ALU.mult, ALU.add, rd + wr_, wr_)
            for step in range(1, 8):
                i = 7 - step if rev else step
                pv = i + 1 if rev else i - 1
                yield from cm(Z3[:, :, :, i], Z3[:, :, :, pv], sc(P1, 0), i, pv)
                yield
            for step in range(1, 8):
                a = 7 - step if rev else step
                pv = a + 1 if rev else a - 1
                yield from cm(Z2[:, :, :, a, e8], Z2[:, :, :, pv, e8], sc(PW8, 0), e8, e8)
                yield
            nq = L // 64
            ldx = 1 if rev else 0
            V = Z2[:, :, :, e8, e8]
            ecls = [kz + ":%d" % c for c in (e8, e8 + 8)]
            cur = 0
            wk = lambda i_: "WKS%d%d" % (ldx, i_)
            k.cp("dve", WKS[ldx][0][:, :, 64:64 + nq], V, ecls, [wk(0)])
            yield
            kk_ = 0
            sft = 1
            while sft < nq:
                src, dst = WKS[ldx][cur], WKS[ldx][1 - cur]
                off = 64 + sft if rev else 64 - sft
                sview = src[:, :, off:off + nq]
                cview = src[:, :, 64:64 + nq]
                dview = dst[:, :, 64:64 + nq]
                pr, pi, npi = sc(PWK, kk_)
                k.stt(dview, sview, pr, cview, ALU.mult, ALU.add, [wk(cur), "P16"], [wk(1 - cur)])
                yield
                k.stt(dview[:, 0], sview[:, 1], npi, dview[:, 0], ALU.mult, ALU.add, [wk(cur), wk(1 - cur), "P16"], [wk(1 - cur)])
                k.stt(dview[:, 1], sview[:, 0], pi, dview[:, 1], ALU.mult, ALU.add, [wk(cur), wk(1 - cur), "P16"], [wk(1 - cur)])
                cur = 1 - cur
                sft *= 2
                kk_ += 1
                yield
            k.cp("dve", V, WKS[ldx][cur][:, :, 64:64 + nq], [wk(cur)], ecls)
            yield
            nq = L // 64
            for a in range(8):
                if a == e8:
                    continue
                m = (8 - a) if rev else (a + 1)
                if rev:
                    dsl, ssl = slice(0, OWN // 64 + 1), slice(1, OWN // 64 + 2)
                else:
                    dsl, ssl = slice(5, nq), slice(4, nq - 1)
                yield from cm(Z2[:, :, dsl, a, e8], Z2[:, :, ssl, e8, e8], sc(PW8, m - 1), e8, e8, wtag=a)
                yield

        def zfill(j, ld, a):
            cu, off = j // 4, (j % 4) * 32
            uTc, uk = uTcs[cu % 2], "uTc%d" % (cu % 2)
            if ld == 1:
                segs = [(0, c0, min(512, L1 - c0)) for c0 in range(0, L1, 512)]
                buf = Z1cs[a]
            else:
                segs = [(128 - T, T, 256)] + [(384, c0, n_) for (c0, n_) in oblocks]
                buf = Z0cs[a]
            for ri in range(2):
                for (dd_, c0, n_) in segs:
                    dcol = c0 + dd_ if ld == 0 else c0
                    pi_ = 5 + zcnt["n"] % 2
                    zcnt["n"] += 1
                    if off == 96:
                        k.mm(ps[pi_][:, 0:n_], LZb[64:128, (ld * 2 + ri) * 4 + cu, :], uTc[64:128, c0:c0 + n_], True, True, ["LZb", uk], [pk[pi_]])
                    else:
                        k.mm(ps[pi_][:, 0:n_], LZ[off:off + 32, (ld * 2 + ri) * 4 + cu, :], uTc[off:off + 32, c0:c0 + n_], True, True, ["LZ", uk], [pk[pi_]])
                    k.cp("act", buf[:, ri, dcol:dcol + n_], ps[pi_][:, 0:n_], [pk[pi_]], subkeys("Z%d%d" % (ld, a)))

        def fill_pair(j):
            cu = j // 4
            if j % 4 == 0:
                k.dma("sp", uTcs[cu % 2][:], uT_s[cu * 128:(cu + 1) * 128, :], ["uT_s"], ["uTc%d" % (cu % 2)])
            for ld in (1, 0):
                zfill(j, ld, j % 2)

        fill_pair(0)
        for j in range(16):
            cu = j // 4
            a = j % 2
            uTc, uk = uTcs[cu % 2], "uTc%d" % (cu % 2)
            if j + 1 < 16:
                fill_pair(j + 1)
            gens = [scan(Z1cs[a], "Z1%d" % a, L1, True, 16 + j), scan(Z0cs[a], "Z0%d" % a, L0, False, j)]
            while gens:
                gens = [g_ for g_ in gens if next(g_, "done") != "done"]
            for ri in range(2):
                k.cp("act", Xb[1][ri][:], Z1cs[a][:, ri, 0:OWN], subkeys("Z1%d" % a), ["Xb1%d" % ri])
            for ri in range(2):
                k.cp("act", Xb[0][ri][:], Z0cs[a][:, ri, 384:L0], subkeys("Z0%d" % a), ["Xb0%d" % ri])
            NG = OWN // 8
            for ld in range(2):
                Zc_ = Z1cs[a] if ld == 1 else Z0cs[a]
                zk = subkeys("Z%d%d" % (ld, a))
                col = (16 + j) if ld == 1 else j
                Zg = Zc_[:].rearrange("p r (q i) -> p r q i", i=8)
                for pos in range(8):
                    if ld == 1:
                        if pos == 0:
                            continue
                        m = 8 - pos
                        Er, Ei = Zg[:, 0, 1:NG + 1, 0], Zg[:, 1, 1:NG + 1, 0]
                    else:
                        if pos == 7:
                            continue
                        m = pos + 1
                        Er, Ei = Zg[:, 0, 47:47 + NG, 7], Zg[:, 1, 47:47 + NG, 7]
                    pr, pi, npi = P1[:, 0, m - 1, col:col + 1], P1[:, 1, m - 1, col:col + 1], P1[:, 2, m - 1, col:col + 1]
                    gi = gcnt["n"] % 2
                    gcnt["n"] += 1
                    G = Gs[gi]
                    gk = "G%d" % gi
                    k.act(G[:, 0, :], Er, AF.Copy, zk + ["P1"], [gk], scale=pr)
                    k.act(G[:, 1, :], Ei, AF.Copy, zk + ["P1"], [gk], scale=npi)
                    k.act(G[:, 2, :], Er, AF.Copy, zk + ["P1"], [gk], scale=pi)
                    k.act(G[:, 3, :], Ei, AF.Copy, zk + ["P1"], [gk], scale=pr)
                    first = (j % 4 == 0) and ld == 0 and pos == 0
                    for bi, (c0, n_) in enumerate(oblocks):
                        g0, gn = c0 // 8, n_ // 8
                        if first:
                            k.mm(ps[bi][:, 0:n_], CCb[:, (0 * 2 + 0) * 16 + j, :], Xb[0][0][:, c0:c0 + n_], True, False, ["CCb", "Xb00"], [pk[bi]])
                        O = ps[bi][:, 0:n_].rearrange("p (g i) -> p g i", i=8)[:, :, pos]
                        for q_, (cm_, gsel) in enumerate(((0, 0), (0, 1), (1, 2), (1, 3))):
                            k.mm(O, CCb[:, (ld * 2 + cm_) * 16 + j, :], G[:, gsel, g0:g0 + gn], False, False, ["CCb", gk], [pk[bi]])
            for bi, (c0, n_) in enumerate(oblocks):
                q = 0
                for ld in range(2):
                    for ri in range(2):
                        if not (ld == 0 and ri == 0 and j % 4 == 0):
                            k.mm(ps[bi][:, 0:n_], CCb[:, (ld * 2 + ri) * 16 + j, :], Xb[ld][ri][:, c0:c0 + n_],
                                 False, (j % 4 == 3) and q == 3, ["CCb", "Xb%d%d" % (ld, ri)], [pk[bi]])
                        elif False:
                            pass
                        q += 1
                if j % 4 != 0:
                    pass
            if j % 4 == 3:
                for bi, (c0, n_) in enumerate(oblocks):
                    g1, g2 = gw[0], gw[1]
                    yg = ygs[bi % 2]
                    ygk = "ygs%d" % (bi % 2)
                    k.stt(yv[:, 0:n_], uTc[:, c0:c0 + n_], dsk[:, cu:cu + 1], ps[bi][:, 0:n_], ALU.mult, ALU.add, [uk, "dsk", pk[bi]], ["yv"])
                    if dr.get("y_dbg") is not None:
                        k.dma("sp", dr["y_dbg"][cu * 128:(cu + 1) * 128, c0:c0 + n_], yv[:, 0:n_], ["yv"], ["y_dbg"])
                    ge = "dve" if cu == 3 else "pool"
                    k.tt(ge, g1[:, 0:n_], yv[:, 0:n_], yv[:, 0:n_], ALU.mult, ["yv"], ["gw0"])
                    k.ts(ge, g1[:, 0:n_], g1[:, 0:n_], 0.044715, ALU.mult, ["gw0"], ["gw0"], s2=1.0, op1=ALU.add)
                    k.tt(ge, g1[:, 0:n_], g1[:, 0:n_], yv[:, 0:n_], ALU.mult, ["gw0", "yv"], ["gw0"])
                    k.act(g2[:, 0:n_], g1[:, 0:n_], AF.Sigmoid, ["gw0"], ["gw1"], scale=1.5957691216057308)
                    k.tt(ge, yg[:, 0:n_], g2[:, 0:n_], yv[:, 0:n_], ALU.mult, ["gw1", "yv"], [ygk])
                    k.dma("sp", ygT_s[cu * 128:(cu + 1) * 128, c0:c0 + n_], yg[:, 0:n_], [ygk], ["ygT_s"])


def phase4(nc, S, k, sb, ps, pk, psall, cst, ident_b, ones_b, mods, A2, nwt, dr):
    xT, o_s, z_s, gT_s, ygT_s, xl1_s, yT = dr["xT"], dr["o_s"], dr["z_s"], dr["gT_s"], dr["ygT_s"], dr["xl1_s"], dr["yT"]
    dump = dr["dump"]
    blocks = [(0, 512), (512, 512), (1024, 512), (1536, 512), (2048, 128)]
    with ExitStack() as p4:
        h2 = sb("h2", [128, KC, OWN], BF16, stack=p4)
        with ExitStack() as pa:
            wa = sb("wa", [128, 4, D], BF16, stack=pa)
            wg = sb("wg", [128, 4, D], BF16, stack=pa)
            wbo = sb("wbo", [128, 4, D], BF16, stack=pa)
            wo = sb("wo", [128, KC, D], BF16, stack=pa)
            for kc in range(4):
                k.load_cast(wa[:, kc, :], dr["w_a_out"][kc * 128:(kc + 1) * 128, :], D, [], ["wa"])
                k.load_cast(wg[:, kc, :], dr["w_glu"][kc * 128:(kc + 1) * 128, :], D, [], ["wg"])
                k.load_cast(wbo[:, kc, :], dr["w_b_out"][kc * 128:(kc + 1) * 128, :], D, [], ["wbo"])
            for kc in range(KC):
                k.load_cast(wo[:, kc, :], dr["w_o"][kc * 128:(kc + 1) * 128, :], D, [], ["wo"])
            dnw = sb("dnw_sb", [128, 128], stack=pa)
            bgl = sb("bgl_sb", [128, 8], stack=pa)
            k.dma("sp", dnw[:], dr["dnw"][:, :], [], ["dnw"])
            k.dma("sp", bgl[:], dr["b_glu"][:, :], [], ["bgl"])
            xb = sb("xb4", [128, KC, 512], stack=pa)
            xl1 = sb("xl1", [128, KC, 512], stack=pa)
            gts = sb("gts", [128, 16, 512], BF16, stack=pa)
            ygbs = [sb("ygb%d" % i, [128, 4, 512], BF16, stack=pa) for i in range(2)]
            glus = [sb("glu%d" % i, [128, 4, 512], BF16, stack=pa) for i in range(2)]
            yTbs = [sb("yTb%d" % i, [128, 4, 512], BF16, stack=pa) for i in range(2)]
            mixin = sb("mixin", [128, KC, 512], BF16, stack=pa)
            o0 = [sb("o0_%d" % i, [128, 512], stack=pa) for i in range(2)]
            o1 = [sb("o1_%d" % i, [128, 512], stack=pa) for i in range(2)]
            zs = [sb("zs%d" % i, [128, 512], BF16, stack=pa) for i in range(2)]
            ons = [sb("on%d" % i, [128, 512], stack=pa) for i in range(2)]
            junk = sb("junk", [128, 128], stack=pa)
            ss4s = [sb("ss4_%d" % i, [128, 4], stack=pa) for i in range(2)]
            ytoks = [sb("ytok%d" % i, [128, 512], BF16, stack=pa) for i in range(2)]
            sgb = sb("sgb", [128, 512], stack=pa)
            tA = sb("tA", [128, 512], stack=pa)
            tB = sb("tB", [128, 512], stack=pa)
            sqb = sb("sqb4", [128, KC, 512], BF16, stack=pa)
            rstd = sb("rstd4", [128, 512], stack=pa)
            tmp = [sb("tmp4_%d" % i, [128, 512], stack=pa) for i in range(2)]
            tc = {"n": 0}

            def stageX(bx):
                t0, N = blocks[bx]
                ygb, ygk = ygbs[bx % 2], "ygb%d" % (bx % 2)
                yTb, yTk = yTbs[bx % 2], "yTb%d" % (bx % 2)
                glu, gluk = glus[bx % 2], "glu%d" % (bx % 2)
                k.dma("sp", ygb[:, :, 0:N], ygT_s[:, t0:t0 + N].rearrange("(c p) t -> p c t", p=128), ["ygT_s"], [ygk])
                for ti in range(N // 128):
                    r0 = t0 + ti * 128
                    bi = tc["n"] % 2
                    tc["n"] += 1
                    k.dma("sp", o0[bi][:], o_s[0, r0:r0 + 128, :], ["o_s"], ["o0_%d" % bi])
                    k.dma("sp", o1[bi][:], o_s[1, r0:r0 + 128, :], ["o_s"], ["o1_%d" % bi])
                    k.dma("sp", zs[bi][:], z_s[r0:r0 + 128, :], ["z_s"], ["zs%d" % bi])
                    ok = "o0_%d" % bi
                    k.tt("dve", o0[bi][:], o0[bi][:], o1[bi][:], ALU.add, [ok, "o1_%d" % bi], [ok])
                    on, onk = ons[bi], "on%d" % bi
                    ss4, ssk = ss4s[bi], "ss4_%d" % bi
                    ytok, ytk = ytoks[bi], "ytok%d" % bi
                    for h in range(4):
                        k.S.op("act", lambda h=h, ss4=ss4, bi=bi: nc.scalar.activation(out=junk[:], in_=o0[bi][:, h * 128:(h + 1) * 128], func=AF.Square, accum_out=ss4[:, h:h + 1]),
                               reads=[ok], writes=["junk", ssk])
                    k.act(ss4[:], ss4[:], AF.Sqrt, [ssk], [ssk], scale=1.0 / 128, bias=1e-6)
                    k.recip(ss4[:], ss4[:], [ssk], [ssk])
                    yield
                    for h in range(4):
                        k.stt(on[:, h * 128:(h + 1) * 128], o0[bi][:, h * 128:(h + 1) * 128], ss4[:, h:h + 1], dnw[:], ALU.mult, ALU.mult, [ok, ssk, "dnw"], [onk])
                    k.tt("dve", ytok[:], on[:], zs[bi][:], ALU.mult, [onk, "zs%d" % bi], [ytk])
                    pT = ps[7].bitcast(BF16)
                    for h in range(4):
                        k.tr(pT[:, h * 128:(h + 1) * 128], ytok[:, h * 128:(h + 1) * 128], ident_b, [ytk, "cstb"], ["ps7"])
                    k.cp("act", yTb[:, :, ti * 128:(ti + 1) * 128], pT[:, 0:512].rearrange("p (h x) -> p h x", x=128), ["ps7"], [yTk])
                    yield
                for a in range(4):
                    for kc in range(4):
                        k.mm(ps[0][:, 0:N], wg[:, kc, a * 128:(a + 1) * 128], ygb[:, kc, 0:N], kc == 0, kc == 3, ["wg", ygk], ["ps0"])
                    for kc in range(4):
                        k.mm(ps[1][:, 0:N], wg[:, kc, (a + 4) * 128:(a + 5) * 128], ygb[:, kc, 0:N], kc == 0, kc == 3, ["wg", ygk], ["ps1"])
                    k.act(sgb[:, 0:N], ps[1][:, 0:N], AF.Sigmoid, ["ps1", "bgl"], ["sgb"], bias=bgl[:, a + 4:a + 5])
                    k.stt(glu[:, a, 0:N], ps[0][:, 0:N], bgl[:, a:a + 1], sgb[:, 0:N], ALU.add, ALU.mult, ["ps0", "bgl", "sgb"], [gluk])
                    yield

            def stageY(bx):
                t0, N = blocks[bx]
                yTb, yTk = yTbs[bx % 2], "yTb%d" % (bx % 2)
                glu, gluk = glus[bx % 2], "glu%d" % (bx % 2)
                k.dma("sp", xb[:, :, 0:N], xT.rearrange("(kc p) t -> p kc t", p=128)[:, :, t0:t0 + N], [], ["xb4"])
                k.dma("sp", gts[:, :, 0:N], gT_s[:, t0:t0 + N].rearrange("(c p) t -> p c t", p=128), ["gT_s"], ["gts"])
                for oc in range(KC):
                    pa_, pb_ = 2 + (oc % 2) * 2, 3 + (oc % 2) * 2
                    for kc in range(4):
                        k.mm(ps[pa_][:, 0:N], wa[:, kc, oc * 128:(oc + 1) * 128], yTb[:, kc, 0:N], kc == 0, kc == 3, ["wa", yTk], [pk[pa_]])
                    for kc in range(4):
                        k.mm(ps[pb_][:, 0:N], wbo[:, kc, oc * 128:(oc + 1) * 128], glu[:, kc, 0:N], kc == 0, kc == 3, ["wbo", gluk], [pk[pb_]])
                    k.tt("dve", tA[:, 0:N], ps[pa_][:, 0:N], gts[:, oc, 0:N], ALU.mult, [pk[pa_], "gts"], ["tA"])
                    k.tt("dve", tB[:, 0:N], ps[pb_][:, 0:N], gts[:, 8 + oc, 0:N], ALU.mult, [pk[pb_], "gts"], ["tB"])
                    k.tt("dve", mixin[:, oc, 0:N], tA[:, 0:N], tB[:, 0:N], ALU.add, ["tA", "tB"], ["mixin"])
                    yield
                for oc in range(KC):
                    pi = 6 if oc % 2 == 0 else 2
                    for kc in range(KC):
                        k.mm(ps[pi][:, 0:N], wo[:, kc, oc * 128:(oc + 1) * 128], mixin[:, kc, 0:N], kc == 0, kc == KC - 1, ["wo", "mixin"], [pk[pi]])
                    k.stt(xl1[:, oc, 0:N], ps[pi][:, 0:N], mods[:, 2, oc, 0:1], xb[:, oc, 0:N], ALU.mult, ALU.add, [pk[pi], "mods", "xb4"], ["xl1"])
                    yield
                k.dma("sp", xl1_s[:, t0:t0 + N].rearrange("(c p) t -> p c t", p=128), xl1[:, :, 0:N], ["xl1"], ["xl1_s"])
                k.act(sqb[:, :, 0:N], xl1[:, :, 0:N], AF.Square, ["xl1"], ["sqb4"])
                for kc in range(KC):
                    k.mm(ps[3][:, 0:N], ones_b, sqb[:, kc, 0:N], kc == 0, kc == KC - 1, ["cstb", "sqb4"], ["ps3"])
                k.act(rstd[:, 0:N], ps[3][:, 0:N], AF.Sqrt, ["ps3"], ["rstd4"], scale=1.0 / D, bias=1e-6)
                k.recip(rstd[:, 0:N], rstd[:, 0:N], ["rstd4"], ["rstd4"])
                yield
                for kc in range(KC):
                    tb = tmp[kc % 2]
                    tk = "tmp4_%d" % (kc % 2)
                    k.stt(tb[:, 0:N], xl1[:, kc, 0:N], A2[:, kc:kc + 1], rstd[:, 0:N], ALU.mult, ALU.mult, ["xl1", "A2", "rstd4"], [tk])
                    k.act(h2[:, kc, t0:t0 + N], tb[:, 0:N], AF.Identity, [tk, "mods"], ["h2"], bias=mods[:, 3, kc, 0:1])
                    if kc % 2 == 1:
                        yield

            def run_rr(gens):
                gens = [g_ for g_ in gens if g_ is not None]
                while gens:
                    gens = [g_ for g_ in gens if next(g_, "done") != "done"]

            run_rr([stageX(0)])
            for bx in range(len(blocks)):
                run_rr([stageY(bx), stageX(bx + 1) if bx + 1 < len(blocks) else None])
        S.barrier()
        if dr.get("stop_after") == "4A":
            return
        act_s = dr["act_s"]
        wd = sb("wd", [128, 22, D], BF16, stack=p4)
        with ExitStack() as pbk:
            actst = [sb("actst%d" % i, [128, 512], BF16, stack=pbk) for i in range(2)]
            fcw = sb("fcw_sb", [128, 44, 9], stack=pbk)
            k.dma("sp", fcw[:], dr["fcw"][:, :, :], [], ["fcw"])
            wu = [sb("wu%d" % i, [128, KC, 128], BF16, stack=pbk) for i in range(4)]
            dg = [sb("dg%d" % i, [128, 9, 128], BF16, stack=pbk) for i in range(4)]
            upb = [sb("upb%d" % i, [128, 35 * 64], BF16, stack=pbk) for i in range(4)]
            sg = [sb("sg%d" % i, [128, 512], stack=pbk) for i in range(2)]
            cacc = [[sb("cacc%d%d" % (g_, i_), [128, 512], stack=pbk) for i_ in range(2)] for g_ in range(2)]
            for i in range(4):
                k.memset("pool", upb[i][:, 0:64], 0.0, ["upb%d" % i])
            ublocks = [(0, 512), (512, 512), (1024, 512), (1536, 512), (2048, 64)]
            pcnt = 0
            ccnt = 0
            w_up = dr["w_up"]
            for i in range(22):
                k.load_cast(wd[:, i, :], dr["w_down"][i * 128:(i + 1) * 128, :], D, [], ["wd"])
                sel = []
                for gv in range(2):
                    bi = (i % 2) * 2 + gv
                    ch = i + 22 * gv
                    col0 = ch * 128
                    k.dma("pool", wu[bi][:], w_up[:, col0:col0 + 128].rearrange("(kc p) c -> p kc c", p=128), [], ["wu%d" % bi])
                    for tap in range(9):
                        k.act(dg[bi][:, tap, :], cst[:, 0, :], AF.Copy, ["cst", "fcw"], ["dg%d" % bi], scale=fcw[:, ch, tap:tap + 1])
                    for (t0, N) in ublocks:
                        pi = pcnt % 4
                        pcnt += 1
                        for kc in range(KC):
                            k.mm(ps[pi][:, 0:N], wu[bi][:, kc, :], h2[:, kc, t0:t0 + N], kc == 0, kc == KC - 1, ["wu%d" % bi, "h2"], [pk[pi]])
                        k.cp("act", upb[bi][:, 64 + t0:64 + t0 + N], ps[pi][:, 0:N], [pk[pi]], ["upb%d" % bi])
                    sel.append(bi)
                for b in range(4):
                    pg, pv = 4 + (ccnt % 2) * 2, 5 + (ccnt % 2) * 2
                    ccnt += 1
                    for gv, pp in ((0, pg), (1, pv)):
                        bi = sel[gv]
                        U = upb[bi][:].rearrange("p (r c) -> p r c", c=64)
                        O = ps[pp].rearrange("p (r c) -> p r c", c=64)
                        taps = [(1, 1)] + [(a_, b_) for a_ in range(3) for b_ in range(3) if (a_, b_) != (1, 1)]
                        for ti, (a_, b_) in enumerate(taps):
                            da, db = a_ - 1, b_ - 1
                            c_lo, c_hi = max(0, -db), 64 - max(0, db)
                            r_in = 8 * b + 1 + da
                            k.mm(O[:, :, c_lo:c_hi], dg[bi][:, a_ * 3 + b_, :], U[:, r_in:r_in + 8, c_lo + db:c_hi + db], ti == 0, ti == 8,
                                 ["dg%d" % bi, "upb%d" % bi], [pk[pp]])
                    si = ccnt % 2
                    k.act(sg[si][:], ps[pg][:, :], AF.Silu, [pk[pg]], ["sg%d" % si])
                    k.tt("dve", actst[si][:], ps[pv][:, :], sg[si][:], ALU.mult, [pk[pv], "sg%d" % si], ["actst%d" % si])
                    k.dma("sp", act_s[i * 128:(i + 1) * 128, b * 512:(b + 1) * 512], actst[si][:], ["actst%d" % si], ["act_s"])
        S.barrier()
        phase4c(nc, S, k, sb, ps, pk, ones_b, mods, nwt, dr, wd)


def phase4c(nc, S, k, sb, ps, pk, ones_b, mods, nwt, dr, wd):
    xl1_s, yT, act_s = dr["xl1_s"], dr["yT"], dr["act_s"]
    with ExitStack() as pc:
        acb = [sb("acb%d" % i, [128, 22, 512], BF16, stack=pc) for i in range(2)]
        xl = [sb("xl_%d" % i, [128, KC, 512], stack=pc) for i in range(2)]
        sq = sb("sqc", [128, KC, 512], BF16, stack=pc)
        rstd = sb("rstdc", [128, 512], stack=pc)
        ob = [sb("ob%d" % i, [128, KC, 512], stack=pc) for i in range(1)]
        for b in range(4):
            X = xl[b % 2]
            xk = "xl_%d" % (b % 2)
            O = ob[0]
            okk = "ob0"
            xks = [xk + ":%d" % oc_ for oc_ in range(KC)]
            k.dma("sp", X[:], xl1_s[:, b * 512:(b + 1) * 512].rearrange("(c p) t -> p c t", p=128), ["xl1_s"], xks)
            act = acb[b % 2]
            ack = "acb%d" % (b % 2)
            k.dma("sp", act[:], act_s[:, b * 512:(b + 1) * 512].rearrange("(c p) t -> p c t", p=128), ["act_s"], [ack])
            for oc in range(KC):
                pi = oc % 4
                for kc in range(22):
                    k.mm(ps[pi][:, :], wd[:, kc, oc * 128:(oc + 1) * 128], act[:, kc, :], kc == 0, kc == 21, ["wd", ack], [pk[pi]])
                k.stt(X[:, oc, :], ps[pi][:, :], mods[:, 5, oc, 0:1], X[:, oc, :], ALU.mult, ALU.add, [pk[pi], "mods", xks[oc]], [xks[oc]])
            k.act(sq[:], X[:], AF.Square, xks, ["sqc"])
            for kc in range(KC):
                k.mm(ps[4][:, :], ones_b, sq[:, kc, :], kc == 0, kc == KC - 1, ["cstb", "sqc"], ["ps4"])
            k.act(rstd[:], ps[4][:, :], AF.Sqrt, ["ps4"], ["rstdc"], scale=1.0 / D, bias=1e-6)
            k.recip(rstd[:], rstd[:], ["rstdc"], ["rstdc"])
            for oc in range(KC):
                k.stt(O[:, oc, :], X[:, oc, :], nwt[:, 2, oc:oc + 1], rstd[:], ALU.mult, ALU.mult, [xks[oc], "nwt", "rstdc"], [okk])
            k.dma("sp", yT[:, b * 512:(b + 1) * 512].rearrange("(c p) t -> p c t", p=128), O[:], [okk], ["yT"])


def _chunk(v):
    return np.ascontiguousarray(v.reshape(-1, 128).T)


def make_consts():
    c = np.zeros((128, 11, 128), np.float32)
    i = np.arange(128)
    c[:, 0, :] = np.eye(128)
    c[:, 1, :] = (i[:, None] <= i[None, :])
    c[:, 2, :] = (i[:, None] >= i[None, :])
    c[:, 3, :] = np.where(i[None, :] >= i[:, None], 0.0, -BIG)
    c[:, 4, :] = np.where(i[None, :] <= i[:, None], 0.0, -BIG)
    c[:, 5, :] = np.where(i[None, :] < i[:, None], 0.0, BIG)
    c[:, 6, :] = np.where(i[None, :] > i[:, None], 0.0, BIG)
    c[:, 7, :] = 1.0
    c[:, 8, :] = (i[:, None] // 32 == i[None, :] // 32)
    c[:, 9, :] = (i[:, None] // 64 == i[None, :] // 64) & (i[:, None] // 32 != i[None, :] // 32)
    c[:, 10, :] = (i[:, None] // 64 != i[None, :] // 64)
    return c


def prep_core(inp, core, cst):
    b, h = core // 2, core % 2
    rev = h == 1
    dm = [1, 0] if rev else [0, 1]
    f = np.float32
    x = inp["x"][b]
    cx = inp["ctx"][b]
    if rev:
        x = x[::-1]
        cx = cx[::-1]
    m = {}
    m["xT"] = np.ascontiguousarray(x.T)
    m["ctxT"] = np.ascontiguousarray(cx.T)
    m["cvec"] = np.ascontiguousarray(np.stack([_chunk(inp["c"][b]), _chunk(inp["c_ctx"])], axis=-1))
    m["w_ada"] = np.ascontiguousarray(inp["w_ada"][0])
    m["b_ada"] = _chunk(inp["b_ada"][0])
    m["nw"] = np.ascontiguousarray(np.stack([_chunk(inp["norm1_w"][0]), _chunk(inp["norm2_w"][0]), _chunk(inp["norm_f_w"])], axis=1))
    w_in = inp["w_in"][0]
    if rev:
        w_in = w_in.copy()
        for base in (2048, 2056):
            blk = w_in[:, base:base + 8].copy()
            w_in[:, base:base + 4] = blk[:, 4:8]
            w_in[:, base + 4:base + 8] = blk[:, 0:4]
    m["w_in"] = np.ascontiguousarray(w_in)
    cw = inp["dn_conv_w"][0]
    if rev:
        cw = cw[::-1]
    m["convw"] = np.ascontiguousarray(cw.T.reshape(12, 128, 3).transpose(1, 0, 2))
    al = inp["dn_a_log"][0][dm].reshape(8)
    dtb = inp["dn_dt_bias"][0][dm].reshape(8)
    m["gpar"] = np.ascontiguousarray(np.broadcast_to(np.stack([np.tile(al, NT), np.tile(dtb, NT)])[None], (128, 2, NT * 8))).astype(f)
    m["dnw"] = np.ascontiguousarray(np.broadcast_to(inp["dn_norm_w"][0][None, :], (128, 128))).astype(f)
    m["w_a_out"] = np.ascontiguousarray(inp["w_a_out"][0])

    def pairlay(a):
        a = a[dm]
        return a.reshape(2, 16, 2, 64).transpose(2, 3, 0, 1).reshape(128, 2, 16)
    are = pairlay(inp["s5_a_re"][0])
    aim = pairlay(inp["s5_a_im"][0])
    ls = pairlay(np.broadcast_to(inp["s5_log_step"][0][:, :, None], (2, 32, 64)))
    rep = lambda a: np.broadcast_to(a[..., None], (128, 2, 16, 32)).reshape(128, 1024)
    m["s5p"] = np.ascontiguousarray(np.stack([rep(are), rep(aim), rep(ls)], axis=1)).astype(f)

    def blay(a):
        a = a[dm]
        o = np.zeros((2, 64, 2, 16, 2, 16), f)
        a6 = a.reshape(2, 16, 2, 64, 16)
        for gl in range(2):
            o[gl, :, :, :, gl, :] = a6[:, :, gl].transpose(2, 0, 1, 3)
        return o.reshape(128, 1024)
    m["s5b"] = np.ascontiguousarray(np.stack([blay(inp["s5_b_re"][0]), blay(inp["s5_b_im"][0])], axis=1))

    def clay(a):
        a = a[dm]
        o = np.zeros((2, 64, 2, 16, 4, 2, 16), f)
        a6 = a.reshape(2, 16, 2, 16, 64)
        for j in range(16):
            for gl in range(2):
                o[gl, :, :, j, j % 4, gl, :] = a6[:, j, gl].transpose(2, 0, 1)
        return o.reshape(128, 2, 16 * 128)
    cr = clay(inp["s5_c_re"][0])
    ci = clay(inp["s5_c_im"][0])
    m["s5c"] = np.ascontiguousarray(np.stack([cr, ci], axis=2).reshape(128, 2 * 2 * 16 * 128))
    m["s5d"] = _chunk(inp["s5_d"][0])
    m["w_glu"] = np.ascontiguousarray(inp["w_glu"][0])
    m["b_glu"] = _chunk(inp["b_glu"][0])
    m["w_b_out"] = np.ascontiguousarray(inp["w_b_out"][0])
    m["w_o"] = np.ascontiguousarray(inp["w_o"][0])
    m["w_up"] = np.ascontiguousarray(inp["w_up"][0])
    fw = inp["ffn_conv_w"][0]
    if rev:
        fw = fw[::-1, ::-1]
    m["fcw"] = np.ascontiguousarray(fw.reshape(9, 44, 128).transpose(2, 1, 0))
    m["w_down"] = np.ascontiguousarray(inp["w_down"][0])
    m["consts"] = cst
    return {k_: np.ascontiguousarray(v, dtype=np.float32) for k_, v in m.items()}


def kernel(**inputs):
    inp = {k_: np.asarray(v) for k_, v in inputs.items()}
    cst = make_consts()
    nc = build_program()
    in_maps = [prep_core(inp, c, cst) for c in range(8)]
    res = run_bass_kernel_spmd(nc, in_maps, core_ids=list(range(8)))
    out = np.zeros((4, T, D), np.float32)
    for c in range(8):
        b, h = c // 2, c % 2
        y = np.asarray(res.results[c]["yT"]).T
        if h == 0:
            out[b, 0:OUTN] = y
        else:
            out[b, T - OUTN:T] = y[::-1]
    return out
```
